# Optimizing a Trainium2 kernel written in Bass

```python
import math
import jax, jax.numpy as jnp
from jax import lax
import numpy as np


D_MODEL = 1024
BATCH = 4
SEQ = 4096
DEPTH = 4
DEC_BATCH = 8
DEC_SEQ = 8192
PAST_LEN = 128

N_MIXERS = 2
N_A_LAYERS = (DEPTH + 1) // 2
N_B_LAYERS = DEPTH // 2
CHUNK = 128
GMLP_WIDTH = D_MODEL
GMLP_GROUPS = 8
GMLP_GROUP_DIM = GMLP_WIDTH // GMLP_GROUPS
DIFF_HEADS = 8
DIFF_HEAD_DIM = D_MODEL // (2 * DIFF_HEADS)
DIFF_V_DIM = 2 * DIFF_HEAD_DIM
D_FF = 2816
CONV_WIDTH = 3
ROPE_THETA = 10000.0
NORM_EPS = 1e-6
SUBLN_EPS = 1e-5
BLOCK_Q = 128

kernel_name = "hybrid_gmlp_diffattn_convffn_encoder"


def rmsnorm(x, g, eps=NORM_EPS):
    xf = x.astype(jnp.float32)
    y = xf * lax.rsqrt(jnp.mean(xf * xf, axis=-1, keepdims=True) + eps)
    return (y * g.astype(jnp.float32)).astype(x.dtype)


def rope(x):
    S, dh = x.shape[1], x.shape[-1]
    pos = jnp.arange(S, dtype=jnp.float32)
    inv_freq = ROPE_THETA ** (-jnp.arange(0, dh, 2, dtype=jnp.float32) / dh)
    ang = pos[:, None] * inv_freq[None, :]
    ang = jnp.concatenate([ang, ang], axis=-1)
    cos = jnp.cos(ang)[None, :, None, :].astype(x.dtype)
    sin = jnp.sin(ang)[None, :, None, :].astype(x.dtype)
    x1, x2 = jnp.split(x, 2, axis=-1)
    rot = jnp.concatenate([-x2, x1], axis=-1)
    return x * cos + rot * sin


def gmlp_mixer(h, w_in, v_gain, w_s, b_s, w_out):
    B, S, _ = h.shape
    z = jax.nn.gelu(h @ w_in)
    u, v = jnp.split(z, 2, axis=-1)
    v = rmsnorm(v, v_gain)
    v = v.reshape(B, S // CHUNK, CHUNK, GMLP_GROUPS, GMLP_GROUP_DIM)
    v = jnp.einsum('gpq,bcqgd->bcpgd', w_s, v) + b_s.T[None, None, :, :, None]
    y = u * v.reshape(B, S, GMLP_WIDTH)
    return y @ w_out


def diff_attention(h, w_qkv, lam_q1, lam_k1, lam_q2, lam_k2, subln_g, w_out, layer_idx):
    B, S, _ = h.shape
    H, dh = DIFF_HEADS, DIFF_HEAD_DIM
    q, k, v = jnp.split(h @ w_qkv, 3, axis=-1)
    q = rope(q.reshape(B, S, 2 * H, dh)).reshape(B, S, H, 2, dh)
    k = rope(k.reshape(B, S, 2 * H, dh)).reshape(B, S, H, 2, dh)
    v = v.reshape(B, S, H, DIFF_V_DIM)
    scale = dh ** -0.5
    lam_init = 0.8 - 0.6 * math.exp(-0.3 * layer_idx)
    f32 = jnp.float32
    lam = (jnp.exp(jnp.sum(lam_q1.astype(f32) * lam_k1.astype(f32)))
           - jnp.exp(jnp.sum(lam_q2.astype(f32) * lam_k2.astype(f32))) + lam_init)
    nb = S // BLOCK_Q
    qb = jnp.moveaxis(q.reshape(B, nb, BLOCK_Q, H, 2, dh), 1, 0)

    def block(qblk):
        s = jnp.einsum('bqhcd,bkhcd->bhcqk', qblk, k).astype(f32) * scale
        p = jax.nn.softmax(s, axis=-1)
        a = p[:, :, 0] - lam * p[:, :, 1]
        return jnp.einsum('bhqk,bkhe->bqhe', a.astype(v.dtype), v)

    o = lax.map(block, qb)
    o = jnp.moveaxis(o, 0, 1).reshape(B, S, H, DIFF_V_DIM)
    o = rmsnorm(o, subln_g, SUBLN_EPS) * (1.0 - lam_init)
    return o.reshape(B, S, H * DIFF_V_DIM) @ w_out


def conv_ffn(h, w_in, conv_w, conv_b, w_out):
    S = h.shape[1]
    a = h @ w_in
    half = CONV_WIDTH // 2
    ap = jnp.pad(a, ((0, 0), (half, half), (0, 0)))
    c = conv_b
    for t in range(CONV_WIDTH):
        c = c + ap[:, t:t + S] * conv_w[t]
    g, u = jnp.split(c, 2, axis=-1)
    return (jax.nn.silu(g) * u) @ w_out


def trunk(x, norm_mix, norm_ffn, norm_final,
          gmlp_w_in, gmlp_v_gain, gmlp_w_s, gmlp_b_s, gmlp_w_out,
          diff_w_qkv, diff_lam_q1, diff_lam_k1, diff_lam_q2, diff_lam_k2, diff_subln_g, diff_w_out,
          ffn_w_in, ffn_conv_w, ffn_conv_b, ffn_w_out):
    for i in range(DEPTH):
        h = rmsnorm(x, norm_mix[i])
        j = i // N_MIXERS
        if i % N_MIXERS == 0:
            x = x + gmlp_mixer(h, gmlp_w_in[j], gmlp_v_gain[j], gmlp_w_s[j], gmlp_b_s[j], gmlp_w_out[j])
        else:
            x = x + diff_attention(h, diff_w_qkv[j], diff_lam_q1[j], diff_lam_k1[j],
                                   diff_lam_q2[j], diff_lam_k2[j], diff_subln_g[j], diff_w_out[j], i)
        h = rmsnorm(x, norm_ffn[i])
        x = x + conv_ffn(h, ffn_w_in[i], ffn_conv_w[i], ffn_conv_b[i], ffn_w_out[i])
    return rmsnorm(x, norm_final)


def setup_inputs(seed: int = 0) -> dict:
    key = jax.random.key(seed)
    ks = jax.random.split(key, 24)
    f32 = jnp.float32
    nrm = lambda k, shape, s: (jax.random.normal(k, shape, f32) * s)
    res_scale = (2.0 * DEPTH) ** -0.5
    return {
        'x_prompt': nrm(ks[0], (BATCH, SEQ, D_MODEL), 1.0),
        'x_sample': nrm(ks[1], (DEC_BATCH, DEC_SEQ, D_MODEL), 1.0),
        'norm_mix': 1.0 + nrm(ks[2], (DEPTH, D_MODEL), 0.02),
        'norm_ffn': 1.0 + nrm(ks[3], (DEPTH, D_MODEL), 0.02),
        'norm_final': 1.0 + nrm(ks[4], (D_MODEL,), 0.02),
        'gmlp_w_in': nrm(ks[5], (N_A_LAYERS, D_MODEL, 2 * GMLP_WIDTH), D_MODEL ** -0.5),
        'gmlp_v_gain': 1.0 + nrm(ks[6], (N_A_LAYERS, GMLP_WIDTH), 0.02),
        'gmlp_w_s': nrm(ks[7], (N_A_LAYERS, GMLP_GROUPS, CHUNK, CHUNK), CHUNK ** -0.5),
        'gmlp_b_s': 1.0 + nrm(ks[8], (N_A_LAYERS, GMLP_GROUPS, CHUNK), 0.02),
        'gmlp_w_out': nrm(ks[9], (N_A_LAYERS, GMLP_WIDTH, D_MODEL), GMLP_WIDTH ** -0.5 * res_scale),
        'diff_w_qkv': nrm(ks[10], (N_B_LAYERS, D_MODEL, 3 * D_MODEL), D_MODEL ** -0.5),
        'diff_lam_q1': nrm(ks[11], (N_B_LAYERS, DIFF_HEAD_DIM), 0.1),
        'diff_lam_k1': nrm(ks[12], (N_B_LAYERS, DIFF_HEAD_DIM), 0.1),
        'diff_lam_q2': nrm(ks[13], (N_B_LAYERS, DIFF_HEAD_DIM), 0.1),
        'diff_lam_k2': nrm(ks[14], (N_B_LAYERS, DIFF_HEAD_DIM), 0.1),
        'diff_subln_g': 1.0 + nrm(ks[15], (N_B_LAYERS, DIFF_V_DIM), 0.02),
        'diff_w_out': nrm(ks[16], (N_B_LAYERS, D_MODEL, D_MODEL), D_MODEL ** -0.5 * res_scale),
        'ffn_w_in': nrm(ks[17], (DEPTH, D_MODEL, 2 * D_FF), D_MODEL ** -0.5),
        'ffn_conv_w': nrm(ks[18], (DEPTH, CONV_WIDTH, 2 * D_FF), CONV_WIDTH ** -0.5),
        'ffn_conv_b': nrm(ks[19], (DEPTH, 2 * D_FF), 0.01),
        'ffn_w_out': nrm(ks[20], (DEPTH, D_FF, D_MODEL), D_FF ** -0.5 * res_scale),
    }


def reference(x_prompt, x_sample, norm_mix, norm_ffn, norm_final,
              gmlp_w_in, gmlp_v_gain, gmlp_w_s, gmlp_b_s, gmlp_w_out,
              diff_w_qkv, diff_lam_q1, diff_lam_k1, diff_lam_q2, diff_lam_k2, diff_subln_g, diff_w_out,
              ffn_w_in, ffn_conv_w, ffn_conv_b, ffn_w_out):
    params = (norm_mix, norm_ffn, norm_final,
              gmlp_w_in, gmlp_v_gain, gmlp_w_s, gmlp_b_s, gmlp_w_out,
              diff_w_qkv, diff_lam_q1, diff_lam_k1, diff_lam_q2, diff_lam_k2, diff_subln_g, diff_w_out,
              ffn_w_in, ffn_conv_w, ffn_conv_b, ffn_w_out)
    y_prompt = trunk(x_prompt, *params)
    y_sample = trunk(x_sample, *params)
    return (y_prompt, y_sample)
```

```python
import contextlib
import math
import numpy as np
import concourse.bass as bass
import concourse.mybir as mybir
from concourse.bass_utils import run_bass_kernel_spmd

F32 = mybir.dt.float32
BF16 = mybir.dt.bfloat16
AF = mybir.ActivationFunctionType
ALU = mybir.AluOpType

D = 1024
KC = 8
DEPTH = 4
DFF = 2816
NPAIR = 22
H = 8
NORM_EPS = 1e-6
SUBLN_EPS = 1e-5
ROPE_THETA = 10000.0
TT = 512
GELU_C = 0.044715


class Buf:
    __slots__ = ("name", "w", "r", "excl", "merge", "dsem")

    def __init__(self, name, excl=False, merge=False):
        self.name = name
        self.w = {}
        self.r = {}
        self.excl = excl
        self.merge = merge
        self.dsem = {}


class Trk:
    EPOCH = 30000

    def __init__(self, nc, es):
        self.nc = nc
        self.es = es
        self.eng = {"pe": nc.tensor, "act": nc.scalar, "dve": nc.vector, "pool": nc.gpsimd, "sp": nc.sync}
        self.semh = {}
        self.cnt = {e: 0 for e in self.eng}
        self.epoch = {e: 0 for e in self.eng}
        self.cur = {}
        self.waited = {e: {} for e in self.eng}
        self.dval = {}
        self.active = []
        self.dfree = {e: [] for e in self.eng}
        self.nd = 0
        self.nwait = 0
        self.nins = {e: 0 for e in self.eng}
        for e in self.eng:
            self._newsem(e)

    def _mk(self, name):
        self.semh[name] = self.es.enter_context(self.nc.semaphore(name))
        return name

    def _newsem(self, e):
        self.cur[e] = self._mk("q_%s_%d" % (e, self.epoch[e]))
        self.cnt[e] = 0

    def _deps(self, reads, writes):
        deps = {}
        reads = [b for b in reads if not b.merge]
        writes = [b for b in writes if not b.merge]
        for b in reads:
            for s, v in b.w.items():
                if deps.get(s, 0) < v:
                    deps[s] = v
            if b.excl:
                for s, v in b.r.items():
                    if deps.get(s, 0) < v:
                        deps[s] = v
        for b in writes:
            for s, v in b.w.items():
                if deps.get(s, 0) < v:
                    deps[s] = v
            for s, v in b.r.items():
                if deps.get(s, 0) < v:
                    deps[s] = v
        return deps

    def _wait(self, e, deps):
        engine = self.eng[e]
        wd = self.waited[e]
        own = self.cur[e]
        for s, v in deps.items():
            if e == "pe" and s == own:
                continue
            if wd.get(s, 0) >= v:
                continue
            engine.wait_ge(self.semh[s], v)
            wd[s] = v
            self.nwait += 1

    def _record(self, ev, reads, writes):
        s, v = ev
        for b in writes:
            if b.merge:
                continue
            b.w = {s: v}
            b.r = {}
        for b in reads:
            if b.merge:
                continue
            if b.excl:
                b.w = {s: v}
                b.r = {}
            elif b.r.get(s, 0) < v:
                b.r[s] = v

    def op(self, e, fn, reads=(), writes=()):
        if self.cnt[e] >= self.EPOCH:
            self.epoch[e] += 1
            self._newsem(e)
        self._wait(e, self._deps(reads, writes))
        ins = fn(self.eng[e])
        self.cnt[e] += 1
        ev = (self.cur[e], self.cnt[e])
        ins.then_inc(self.semh[ev[0]], 1)
        self._record(ev, reads, writes)
        self.nins[e] += 1
        return ins

    def dma(self, q, dst, dst_ap, src, src_ap, side, slow=False):
        if q not in side.dsem:
            if self.dfree[q]:
                side.dsem[q] = self.dfree[q].pop()
            else:
                side.dsem[q] = self._mk("d%s%d" % (q, self.nd))
                self.nd += 1
                self.dval[side.dsem[q]] = 0
            self.active.append((side, q))
        name = side.dsem[q]
        self._wait(q, self._deps((src,), (dst,)))
        if slow:
            ins = self.eng[q].dma_start(out=dst_ap, in_=src_ap, allow_slow_non_contiguous=True)
        else:
            ins = self.eng[q].dma_start(out=dst_ap, in_=src_ap)
        self.dval[name] += 16
        ev = (name, self.dval[name])
        ins.then_inc(self.semh[name], 16)
        self._record(ev, (src,), (dst,))
        self.nins[q] += 1
        return ins

    def barrier(self, bufs=(), release=True):
        deps = {}
        for e in self.eng:
            if self.cnt[e] > 0:
                deps[self.cur[e]] = self.cnt[e]
        for b, q in self.active:
            deps[b.dsem[q]] = self.dval[b.dsem[q]]
        for e in self.eng:
            self._wait(e, dict(deps))
        if release:
            for b, q in self.active:
                self.dfree[q].append(b.dsem.pop(q))
            self.active = []


class Lay:
    def __init__(self):
        self.off = {}
        self.n = 0

    def add(self, name, ncols):
        self.off[name] = (self.n, ncols)
        self.n += ncols


def make_layouts():
    wl = Lay()
    for j in range(2):
        wl.add("g%d_wu" % j, 8 * 8 * 128)
        wl.add("g%d_wv" % j, 8 * 1024)
        wl.add("g%d_ws" % j, 8 * 128)
        wl.add("g%d_wo" % j, 8 * 8 * 128)
    for j in range(2):
        wl.add("a%d_wqk" % j, 16 * 2 * 8 * 128)
        wl.add("a%d_wv" % j, 8 * 1024)
        wl.add("a%d_wo" % j, 8 * 8 * 128)
    for i in range(DEPTH):
        wl.add("f%d_win" % i, 44 * 8 * 128)
        wl.add("f%d_wout" % i, 8 * NPAIR * 128)
    pl = Lay()
    pl.add("nmix", DEPTH * 8)
    pl.add("nffn", DEPTH * 8)
    pl.add("nfin", 8)
    pl.add("vgain", 2 * 8)
    pl.add("convw", DEPTH * 44 * 4)
    pl.add("bsrep", 2 * 8 * 128)
    pl.add("subg", 2 * 128)
    pl.add("lam", 2 * 4 * 64)
    pl.add("ident", 128)
    return wl, pl


def pack_lhsT(w):
    K, M = w.shape
    return w.reshape(K // 128, 128, M // 128, 128).transpose(1, 2, 0, 3).reshape(128, -1)


def pack_rhs(w):
    K, N = w.shape
    return w.reshape(K // 128, 128, N).transpose(1, 0, 2).reshape(128, -1)


def pack_wout(w):
    K, N = w.shape
    return w.reshape(K // 128, 128, N // 128, 128).transpose(1, 2, 0, 3).reshape(128, -1)


def host_pack(inp, smax):
    wl, pl = make_layouts()
    wall = np.empty((128, wl.n), np.float32)
    pall = np.empty((128, pl.n), np.float32)

    def putw(name, arr):
        o, n = wl.off[name]
        assert arr.shape == (128, n), (name, arr.shape, n)
        wall[:, o:o + n] = arr

    def putp(name, arr):
        o, n = pl.off[name]
        assert arr.shape == (128, n), (name, arr.shape, n)
        pall[:, o:o + n] = arr

    for j in range(2):
        w = np.asarray(inp["gmlp_w_in"][j])
        putw("g%d_wu" % j, pack_lhsT(w[:, :1024]))
        putw("g%d_wv" % j, pack_rhs(w[:, 1024:]))
        putw("g%d_ws" % j, np.asarray(inp["gmlp_w_s"][j]).transpose(2, 0, 1).reshape(128, -1))
        putw("g%d_wo" % j, pack_wout(np.asarray(inp["gmlp_w_out"][j])))
    swap = np.arange(128)
    swap = (swap // 64) * 64 + ((swap % 64) + 32) % 64
    for j in range(2):
        w = np.asarray(inp["diff_w_qkv"][j])
        qk = w[:, :2048].reshape(1024, 16, 128)
        both = np.stack([qk, qk[:, :, swap]], axis=2)
        arr = both.reshape(8, 128, 16, 2, 128).transpose(1, 2, 3, 0, 4)
        putw("a%d_wqk" % j, arr.reshape(128, -1))
        putw("a%d_wv" % j, pack_rhs(w[:, 2048:]))
        putw("a%d_wo" % j, pack_wout(np.asarray(inp["diff_w_out"][j])))
    for i in range(DEPTH):
        w = np.asarray(inp["ffn_w_in"][i])
        wp = np.stack([w[:, :DFF].reshape(1024, NPAIR, 128), w[:, DFF:].reshape(1024, NPAIR, 128)], axis=2)
        putw("f%d_win" % i, pack_lhsT(wp.reshape(1024, 2 * DFF)))
        putw("f%d_wout" % i, pack_wout(np.asarray(inp["ffn_w_out"][i])))

    def pervec(v):
        v = np.asarray(v).reshape(-1, 8, 128)
        return v.transpose(2, 0, 1).reshape(128, -1)

    putp("nmix", pervec(inp["norm_mix"]))
    putp("nffn", pervec(inp["norm_ffn"]))
    putp("nfin", pervec(inp["norm_final"]))
    putp("vgain", pervec(inp["gmlp_v_gain"]))
    cw = np.asarray(inp["ffn_conv_w"])
    cb = np.asarray(inp["ffn_conv_b"])
    c4 = np.concatenate([cw, cb[:, None, :]], axis=1)
    c4 = c4.reshape(DEPTH, 4, 2, NPAIR, 128)
    putp("convw", c4.transpose(4, 0, 3, 2, 1).reshape(128, -1))
    bs = np.asarray(inp["gmlp_b_s"])
    bsr = np.broadcast_to(bs[None, :, :, :], (128, 2, 8, 128))
    putp("bsrep", np.ascontiguousarray(bsr).reshape(128, -1))
    sg = np.asarray(inp["diff_subln_g"])
    putp("subg", np.ascontiguousarray(np.broadcast_to(sg[None], (128, 2, 128))).reshape(128, -1))
    lam = np.stack([np.asarray(inp["diff_lam_q1"]), np.asarray(inp["diff_lam_k1"]),
                    np.asarray(inp["diff_lam_q2"]), np.asarray(inp["diff_lam_k2"])], axis=1)
    putp("lam", np.ascontiguousarray(np.broadcast_to(lam[None], (128, 2, 4, 64))).reshape(128, -1))
    putp("ident", np.eye(128, dtype=np.float32))
    pos = np.arange(smax, dtype=np.float32)
    inv_freq = (ROPE_THETA ** (-np.arange(0, 64, 2, dtype=np.float32) / np.float32(64))).astype(np.float32)
    ang = (pos[:, None] * inv_freq[None, :]).astype(np.float32)
    dd = np.arange(128) % 64
    cosT = np.cos(ang).astype(np.float32).T[dd % 32, :]
    sinT = np.sin(ang).astype(np.float32).T[dd % 32, :]
    sgn = np.where(dd < 32, -1.0, 1.0).astype(np.float32)[:, None]
    return wall, pall, np.ascontiguousarray(cosT), np.ascontiguousarray(sinT * sgn)


def build_program(seqs):
    wl, pl = make_layouts()
    smax = max(s for _, s in seqs)
    nc = bass.Bass("TRN2", target_bir_lowering=False)
    wall = nc.dram_tensor("wall", [128, wl.n], F32, kind="ExternalInput").ap()
    pall_d = nc.dram_tensor("pall", [128, pl.n], F32, kind="ExternalInput").ap()
    cos_d = nc.dram_tensor("cosT", [128, smax], F32, kind="ExternalInput").ap()
    sin_d = nc.dram_tensor("sinT", [128, smax], F32, kind="ExternalInput").ap()
    wbf = nc.dram_tensor("wbf", [128, wl.n], BF16, kind="Internal").ap()
    Bwall = Buf("wall", merge=True)
    Bwbf = Buf("wbf", merge=True)
    Bconst = Buf("constd", merge=True)
    SQ = {}
    for tag, S in seqs:
        d = {}
        d["S"] = S
        d["xin"] = nc.dram_tensor("x_" + tag, [D, S], F32, kind="ExternalInput").ap()
        d["yout"] = nc.dram_tensor("y_" + tag, [D, S], F32, kind="ExternalOutput").ap()
        d["xa"] = nc.dram_tensor("xa_" + tag, [D, S], F32, kind="Internal").ap()
        d["xb"] = nc.dram_tensor("xb_" + tag, [D, S], F32, kind="Internal").ap()
        d["qk"] = nc.dram_tensor("qk_" + tag, [16, 128, S], BF16, kind="Internal").ap()
        d["vs"] = nc.dram_tensor("vs_" + tag, [H, S, 128], BF16, kind="Internal").ap()
        d["ot"] = nc.dram_tensor("ot_" + tag, [H, 128, S], BF16, kind="Internal").ap()
        for k in ("xin", "yout", "xa", "xb", "qk", "vs", "ot"):
            d["B" + k] = Buf(k + "_" + tag, merge=True)
        SQ[tag] = d

    uid = [0]

    def sbt(name, shape, dt):
        uid[0] += 1
        return nc.sbuf_tensor("%s_u%d" % (name, uid[0]), shape, dt)

    def pst_(name, shape, dt):
        uid[0] += 1
        return nc.psum_tensor("%s_u%d" % (name, uid[0]), shape, dt)

    es = contextlib.ExitStack()
    with es:
        T = Trk(nc, es)

        def wslice(name, a=0, n=None):
            o, tot = wl.off[name]
            if n is None:
                n = tot - a
            return wbf[:, o + a:o + a + n]

        pall = es.enter_context(sbt("pall_sb", [128, pl.n], F32))
        Bpall = Buf("pall")
        T.dma("sp", Bpall, pall[:], Bconst, pall_d[:, :], Bpall)

        def pcol(name, a, n=1):
            o, _ = pl.off[name]
            return pall[:, o + a:o + a + n]

        ident = es.enter_context(sbt("ident", [128, 128], BF16))
        ones32 = es.enter_context(sbt("ones32", [128, 128], F32))
        lamt = es.enter_context(sbt("lamt", [128, 16], F32))
        subgs = es.enter_context(sbt("subgs", [128, 2, 128], F32))
        Bmisc = Buf("misc")
        T.op("dve", lambda e: e.tensor_copy(out=ident[:], in_=pcol("ident", 0, 128)), reads=[Bpall], writes=[Bmisc])
        T.op("pool", lambda e: e.memset(ones32[:], 1.0), writes=[Bmisc])
        lam_inits = []
        for j in range(2):
            li = 0.8 - 0.6 * math.exp(-0.3 * (2 * j + 1))
            lam_inits.append(li)
            lo = pl.off["lam"][0] + j * 256
            for w_ in range(2):
                T.op("dve", lambda e: e.tensor_tensor(out=subgs[:, 0, 0:64], in0=pall[:, lo + w_ * 128:lo + w_ * 128 + 64],
                                                      in1=pall[:, lo + w_ * 128 + 64:lo + w_ * 128 + 128], op=ALU.mult),
                     reads=[Bpall, Bmisc], writes=[Bmisc])
                T.op("dve", lambda e: e.reduce_sum(out=lamt[:, 8 + w_:9 + w_], in_=subgs[:, 0, 0:64], axis=mybir.AxisListType.X),
                     reads=[Bmisc], writes=[Bmisc])
            T.op("act", lambda e: e.activation(out=lamt[:, 10:12], in_=lamt[:, 8:10], func=AF.Exp), reads=[Bmisc], writes=[Bmisc])
            T.op("dve", lambda e: e.scalar_tensor_tensor(out=lamt[:, j:j + 1], in0=lamt[:, 11:12], scalar=-li, in1=lamt[:, 10:11],
                                                         op0=ALU.add, op1=ALU.subtract), reads=[Bmisc], writes=[Bmisc])
        for j in range(2):
            T.op("dve", lambda e: e.tensor_scalar(out=subgs[:, j, :], in0=pcol("subg", j * 128, 128), scalar1=1.0 - lam_inits[j],
                                                  scalar2=None, op0=ALU.mult), reads=[Bpall, Bmisc], writes=[Bmisc])
        T.barrier([Bpall])

        def phase_cast():
            CW = 4096
            with contextlib.ExitStack() as ph:
                NS = 3
                f = [ph.enter_context(sbt("cf%d" % i, [128, CW], F32)) for i in range(NS)]
                b = [ph.enter_context(sbt("cb%d" % i, [128, CW], BF16)) for i in range(NS)]
                Bf = [Buf("cf%d" % i) for i in range(NS)]
                Bb = [Buf("cb%d" % i) for i in range(NS)]
                nch = (wl.n + CW - 1) // CW
                for c in range(nch):
                    s = c % NS
                    a = c * CW
                    n = min(CW, wl.n - a)
                    T.dma("sp", Bf[s], f[s][:, 0:n], Bwall, wall[:, a:a + n], Bf[s])
                    if c % 2 == 0:
                        T.op("dve", lambda e: e.tensor_copy(out=b[s][:, 0:n], in_=f[s][:, 0:n]), reads=[Bf[s]], writes=[Bb[s]])
                    else:
                        T.op("act", lambda e: e.activation(out=b[s][:, 0:n], in_=f[s][:, 0:n], func=AF.Copy), reads=[Bf[s]], writes=[Bb[s]])
                    T.dma("pool", Bwbf, wbf[:, a:a + n], Bb[s], b[s][:, 0:n], Bb[s])
                T.barrier(Bf + Bb)

        phase_cast()

        def xT(ap, t0, n):
            return ap.rearrange("(kc p) t -> p kc t", p=128)[:, :, t0:t0 + n]

        class NormCtx:
            def __init__(self, ph, nslot=2):
                self.ns = nslot
                self.xin = [ph.enter_context(sbt("n_xin%d" % i, [128, KC, TT], F32)) for i in range(nslot)]
                self.hT = [ph.enter_context(sbt("n_hT%d" % i, [128, KC, TT], BF16)) for i in range(nslot)]
                self.sq = ph.enter_context(sbt("n_sq", [128, 2, TT], F32))
                self.rs = ph.enter_context(sbt("n_rs", [128, TT], F32))
                self.ps = ph.enter_context(pst_("n_ps", [128, TT], F32))
                self.Bxin = [Buf("n_xin%d" % i) for i in range(nslot)]
                self.BhT = [Buf("n_hT%d" % i) for i in range(nslot)]
                self.Bsq = [Buf("n_sq0"), Buf("n_sq1")]
                self.Brs = Buf("n_rs")
                self.Bps = Buf("n_ps", excl=True)

            def bufs(self):
                return self.Bxin + self.BhT + self.Bsq + [self.Brs]

            def emit(self, slot, srcB, src_ap, t0, n, gname, gidx, out_f32=None):
                xin, hT = self.xin[slot], self.hT[slot]
                Bxin, BhT = self.Bxin[slot], self.BhT[slot]
                T.dma("sp", Bxin, xin[:, :, 0:n], srcB, xT(src_ap, t0, n), Bxin)
                for kc in range(KC):
                    T.op("act", lambda e: e.activation(out=self.sq[:, kc % 2, 0:n], in_=xin[:, kc, 0:n], func=AF.Square),
                         reads=[Bxin], writes=[self.Bsq[kc % 2]])
                    T.op("pe", lambda e: e.matmul(self.ps[:, 0:n], ones32[:], self.sq[:, kc % 2, 0:n], start=(kc == 0), stop=(kc == KC - 1)),
                         reads=[self.Bsq[kc % 2], Bmisc], writes=[self.Bps])
                T.op("act", lambda e: e.activation(out=self.rs[:, 0:n], in_=self.ps[:, 0:n], func=AF.Ln, scale=1.0 / D, bias=NORM_EPS),
                     reads=[self.Bps], writes=[self.Brs])
                T.op("act", lambda e: e.activation(out=self.rs[:, 0:n], in_=self.rs[:, 0:n], func=AF.Exp, scale=-0.5),
                     reads=[self.Brs], writes=[self.Brs])
                for kc in range(KC):
                    dst = hT[:, kc, 0:n] if out_f32 is None else out_f32[:, kc, 0:n]
                    T.op("dve", lambda e: e.scalar_tensor_tensor(out=dst, in0=xin[:, kc, 0:n], scalar=pcol(gname, gidx * 8 + kc),
                                                                 in1=self.rs[:, 0:n], op0=ALU.mult, op1=ALU.mult),
                         reads=[Bxin, self.Brs, Bpall], writes=[BhT])

        def phase_ffn(sq, li, srcB, src, dstB, dst):
            S = sq["S"]
            J = S // TT
            with contextlib.ExitStack() as ph:
                N = NormCtx(ph)
                NW = 3
                win = [ph.enter_context(sbt("f_win%d" % i, [128, 2, KC, 128], BF16)) for i in range(NW)]
                Bwin = [Buf("f_win%d" % i) for i in range(NW)]
                wout = ph.enter_context(sbt("f_wout", [128, KC, NPAIR, 128], BF16))
                Bwout = [Buf("f_wout%d" % i) for i in range(KC)]
                NA = 3
                aext = [ph.enter_context(sbt("f_aext%d" % i, [128, 2, TT + 2], F32)) for i in range(NA)]
                Baext = [Buf("f_aext%d" % i) for i in range(NA)]
                ct = [ph.enter_context(sbt("f_ct%d" % i, [128, 2, TT], F32)) for i in range(NA)]
                Bct = [Buf("f_ct%d" % i) for i in range(NA)]
                sg = [ph.enter_context(sbt("f_sg%d" % i, [128, TT], F32)) for i in range(NA)]
                Bsg = [Buf("f_sg%d" % i) for i in range(NA)]
                carry = ph.enter_context(sbt("f_carry", [128, 2 * NPAIR, 2], F32))
                Bcarry = [Buf("f_carry%d" % i) for i in range(NPAIR)]
                yT = ph.enter_context(sbt("f_yT", [128, NPAIR, TT], BF16))
                ByT = [Buf("f_yT%d" % i) for i in range(NPAIR)]
                xres = ph.enter_context(sbt("f_xres", [128, KC, TT], F32))
                Bxres = Buf("f_xres")
                psa = [ph.enter_context(pst_("f_psa%d" % i, [128, 2, TT], F32)) for i in range(2)]
                Bpsa = [[Buf("f_psa%d_%d" % (i, h), excl=True) for h in range(2)] for i in range(2)]
                pso = [ph.enter_context(pst_("f_pso%d" % i, [128, TT], F32)) for i in range(2)]
                Bpso = [Buf("f_pso%d" % i, excl=True) for i in range(2)]
                cwo = pl.off["convw"][0] + li * 44 * 4

                T.op("pool", lambda e: e.memset(carry[:], 0.0), writes=Bcarry)
                ctr = [0, 0, 0]
                pend = []
                wout_loaded = [False]

                def win_stage(j, slot):
                    virt = (j == J)
                    hT, BhT = N.hT[slot], N.BhT[slot]
                    n = 1 if virt else TT
                    prev = None
                    for pr in range(NPAIR + 1):
                        if pr == 3:
                            while pend:
                                pend.pop()()
                        if pr == NPAIR // 2 and j + 1 < J:
                            N.emit((j + 1) % 2, srcB, src, (j + 1) * TT, TT, "nffn", li)
                        if pr < NPAIR:
                            ws = ctr[0] % NW
                            ctr[0] += 1
                            a = ctr[1] % NA
                            pb = ctr[1] % 2
                            ctr[1] += 1
                            if not virt:
                                T.dma("sp", Bwin[ws], win[ws][:].rearrange("p a b c -> p (a b c)"), Bwbf,
                                      wslice("f%d_win" % li, pr * 2048, 2048), Bwin[ws])
                                for hf in range(2):
                                    for kc in range(KC):
                                        T.op("pe", lambda e: e.matmul(psa[pb][:, hf, :], win[ws][:, hf, kc, :], hT[:, kc, :],
                                                                      start=(kc == 0), stop=(kc == KC - 1)),
                                             reads=[Bwin[ws], BhT], writes=[Bpsa[pb][hf]])
                            T.op("pool", lambda e: e.tensor_copy(out=aext[a][:, :, 0:2], in_=carry[:, 2 * pr:2 * pr + 2, :]),
                                 reads=[Bcarry[pr]], writes=[Baext[a]])
                            if virt:
                                T.op("pool", lambda e: e.memset(aext[a][:, :, 2:TT + 2], 0.0), writes=[Baext[a]])
                            else:
                                T.op("act", lambda e: e.activation(out=aext[a][:, :, 2:TT + 2], in_=psa[pb][:, :, :], func=AF.Copy),
                                     reads=Bpsa[pb], writes=[Baext[a]])
                                T.op("pool", lambda e: e.tensor_copy(out=carry[:, 2 * pr:2 * pr + 2, :], in_=aext[a][:, :, TT:TT + 2]),
                                     reads=[Baext[a]], writes=[Bcarry[pr]])
                            for hf in range(2):
                                co = cwo + (pr * 2 + hf) * 4
                                T.op("act", lambda e: e.activation(out=ct[a][:, hf, 0:n], in_=aext[a][:, hf, 0:n], func=AF.Identity,
                                                                   scale=pall[:, co:co + 1], bias=pall[:, co + 3:co + 4]),
                                     reads=[Baext[a], Bpall], writes=[Bct[a]])
                        if prev is not None:
                            pa, ppr = prev
                            T.op("act", lambda e: e.activation(out=sg[pa][:, 0:n], in_=ct[pa][:, 0, 0:n], func=AF.Silu),
                                 reads=[Bct[pa]], writes=[Bsg[pa]])
                        if pr < NPAIR:
                            for hf in range(2):
                                co = cwo + (pr * 2 + hf) * 4
                                for t in (1, 2):
                                    T.op("dve", lambda e: e.scalar_tensor_tensor(out=ct[a][:, hf, 0:n], in0=aext[a][:, hf, t:t + n],
                                                                                 scalar=pall[:, co + t:co + t + 1], in1=ct[a][:, hf, 0:n],
                                                                                 op0=ALU.mult, op1=ALU.add),
                                         reads=[Baext[a], Bct[a], Bpall], writes=[Bct[a]])
                        if prev is not None:
                            pa, ppr = prev
                            T.op("dve", lambda e: e.tensor_tensor(out=yT[:, ppr, 0:n], in0=sg[pa][:, 0:n], in1=ct[pa][:, 1, 0:n], op=ALU.mult),
                                 reads=[Bsg[pa], Bct[pa]], writes=[ByT[ppr]])
                        prev = (a, pr) if pr < NPAIR else None

                def wout_stage(j):
                    virt = (j == J)
                    lo = 1 if j == 0 else 0
                    hi = 1 if virt else TT
                    n = hi - lo
                    tok0 = j * TT - 1 + lo
                    T.dma("sp", Bxres, xres[:, :, 0:n], srcB, xT(src, tok0, n), Bxres, slow=(n == 1))
                    for dc in range(KC):
                        o = dc % 2
                        if not wout_loaded[0]:
                            T.dma("sp", Bwout[dc], wout[:, dc, :, :].rearrange("p a b -> p (a b)"), Bwbf,
                                  wslice("f%d_wout" % li, dc * NPAIR * 128, NPAIR * 128), Bwout[dc])
                        for pr in range(NPAIR):
                            T.op("pe", lambda e: e.matmul(pso[o][:, 0:n], wout[:, dc, pr, :], yT[:, pr, lo:hi],
                                                          start=(pr == 0), stop=(pr == NPAIR - 1)),
                                 reads=[Bwout[dc], ByT[pr]], writes=[Bpso[o]])
                        T.op("dve", lambda e: e.tensor_tensor(out=xres[:, dc, 0:n], in0=pso[o][:, 0:n], in1=xres[:, dc, 0:n], op=ALU.add),
                             reads=[Bpso[o], Bxres], writes=[Bxres])
                    wout_loaded[0] = True
                    pend.append(lambda: T.dma("sp", dstB, xT(dst, tok0, n), Bxres, xres[:, :, 0:n], Bxres, slow=(n == 1)))

                N.emit(0, srcB, src, 0, TT, "nffn", li)
                for j in range(J + 1):
                    win_stage(j, j % 2)
                    wout_stage(j)
                while pend:
                    pend.pop()()
                T.barrier()

        def phase_gmlp(sq, li, gj, srcB, src, dstB, dst):
            S = sq["S"]
            J = S // TT
            with contextlib.ExitStack() as ph:
                N = NormCtx(ph)
                wu = ph.enter_context(sbt("g_wu", [128, 8, KC, 128], BF16))
                wv = ph.enter_context(sbt("g_wv", [128, KC, 1024], BF16))
                ws = ph.enter_context(sbt("g_ws", [128, 8, 128], BF16))
                wo = ph.enter_context(sbt("g_wo", [128, 8, KC, 128], BF16))
                Bw = Buf("g_w")
                Bw2 = [Buf("g_w2_%d" % i) for i in range(3)]
                uT = ph.enter_context(sbt("g_uT", [128, 8, TT], F32))
                BuT = [Buf("g_uT%d" % i) for i in range(8)]
                vg = [ph.enter_context(sbt("g_vg%d" % i, [128, 1024], F32)) for i in range(2)]
                Bvg = [Buf("g_vg%d" % i) for i in range(2)]
                vsq = ph.enter_context(sbt("g_vsq", [128, 1024], BF16))
                Bvsq = Buf("g_vsq")
                vn = ph.enter_context(sbt("g_vn", [128, 4, 1024], BF16))
                Bvn = [Buf("g_vn%d" % i) for i in range(4)]
                st = [ph.enter_context(sbt("g_st%d" % i, [128, 8], F32)) for i in range(2)]
                Bst = [Buf("g_st%d" % i) for i in range(2)]
                v2 = [ph.enter_context(sbt("g_v2_%d" % i, [128, TT], F32)) for i in range(2)]
                Bv2 = [Buf("g_v2_%d" % i) for i in range(2)]
                yT = ph.enter_context(sbt("g_yT", [128, 8, TT], BF16))
                ByT = [Buf("g_yT%d" % i) for i in range(8)]
                psu = [ph.enter_context(pst_("g_psu%d" % i, [128, TT], F32)) for i in range(2)]
                Bpsu = [Buf("g_psu%d" % i, excl=True) for i in range(2)]
                psv = [ph.enter_context(pst_("g_psv%d" % i, [128, TT], F32)) for i in range(2)]
                Bpsv = [Buf("g_psv%d" % i, excl=True) for i in range(2)]
                pss = [ph.enter_context(pst_("g_pss%d" % i, [128, TT], F32)) for i in range(2)]
                Bpss = [Buf("g_pss%d" % i, excl=True) for i in range(2)]
                T.dma("sp", Bw, wu[:].rearrange("p a b c -> p (a b c)"), Bwbf, wslice("g%d_wu" % gj), Bw)
                T.dma("sp", Bw2[0], wv[:].rearrange("p a b -> p (a b)"), Bwbf, wslice("g%d_wv" % gj), Bw2[0])
                T.dma("sp", Bw2[1], ws[:].rearrange("p a b -> p (a b)"), Bwbf, wslice("g%d_ws" % gj), Bw2[1])
                T.dma("sp", Bw2[2], wo[:].rearrange("p a b c -> p (a b c)"), Bwbf, wslice("g%d_wo" % gj), Bw2[2])
                bso = pl.off["bsrep"][0] + gj * 8 * 128
                for j in range(J):
                    slot = j % 2
                    N.emit(slot, srcB, src, j * TT, TT, "nmix", li)
                    hT, BhT, xin, Bxin = N.hT[slot], N.BhT[slot], N.xin[slot], N.Bxin[slot]
                    for m in range(8):
                        o = m % 2
                        for kc in range(KC):
                            T.op("pe", lambda e: e.matmul(psu[o][:], wu[:, m, kc, :], hT[:, kc, :], start=(kc == 0), stop=(kc == KC - 1)),
                                 reads=[Bw, BhT], writes=[Bpsu[o]])
                        T.op("act", lambda e: e.activation(out=uT[:, m, :], in_=psu[o][:], func=AF.Gelu_apprx_tanh),
                             reads=[Bpsu[o]], writes=[BuT[m]])
                    for sub in range(4):
                        a = sub % 2
                        for hf in range(2):
                            for kc in range(KC):
                                T.op("pe", lambda e: e.matmul(psv[hf][:], hT[:, kc, sub * 128:(sub + 1) * 128], wv[:, kc, hf * 512:(hf + 1) * 512],
                                                              start=(kc == 0), stop=(kc == KC - 1)),
                                     reads=[Bw2[0], BhT], writes=[Bpsv[hf]])
                            T.op("act", lambda e: e.activation(out=vg[a][:, hf * 512:(hf + 1) * 512], in_=psv[hf][:], func=AF.Gelu_apprx_tanh),
                                 reads=[Bpsv[hf]], writes=[Bvg[a]])
                        T.op("act", lambda e: e.activation(out=vsq[:], in_=vg[a][:], func=AF.Square, accum_out=st[a][:, 0:1]),
                             reads=[Bvg[a]], writes=[Bvsq, Bst[a]])
                        T.op("act", lambda e: e.activation(out=st[a][:, 1:2], in_=st[a][:, 0:1], func=AF.Ln, scale=1.0 / 1024, bias=NORM_EPS),
                             reads=[Bst[a]], writes=[Bst[a]])
                        T.op("act", lambda e: e.activation(out=st[a][:, 2:3], in_=st[a][:, 1:2], func=AF.Exp, scale=-0.5),
                             reads=[Bst[a]], writes=[Bst[a]])
                        T.op("dve", lambda e: e.tensor_scalar(out=vn[:, sub, :], in0=vg[a][:], scalar1=st[a][:, 2:3], scalar2=None, op0=ALU.mult),
                             reads=[Bvg[a], Bst[a]], writes=[Bvn[sub]])
                    for g in range(8):
                        o = g % 2
                        for sub in range(4):
                            T.op("pe", lambda e: e.matmul(pss[o][:, sub * 128:(sub + 1) * 128], vn[:, sub, g * 128:(g + 1) * 128], ws[:, g, :],
                                                          start=True, stop=True, skip_group_check=True),
                                 reads=[Bw2[1], Bvn[sub]], writes=[Bpss[o]])
                        for sub in range(4):
                            T.op("dve", lambda e: e.scalar_tensor_tensor(out=v2[o][:, sub * 128:(sub + 1) * 128], in0=pss[o][:, sub * 128:(sub + 1) * 128],
                                                                         scalar=pcol("vgain", gj * 8 + g),
                                                                         in1=pall[:, bso + g * 128:bso + (g + 1) * 128], op0=ALU.mult, op1=ALU.add),
                                 reads=[Bpss[o], Bpall], writes=[Bv2[o]])
                        T.op("pool", lambda e: e.tensor_tensor(out=yT[:, g, :], in0=uT[:, g, :], in1=v2[o][:], op=ALU.mult),
                             reads=[BuT[g], Bv2[o]], writes=[ByT[g]])
                    for dc in range(KC):
                        o = dc % 2
                        for g in range(8):
                            T.op("pe", lambda e: e.matmul(psu[o][:], wo[:, dc, g, :], yT[:, g, :], start=(g == 0), stop=(g == 7)),
                                 reads=[Bw2[2], ByT[g]], writes=[Bpsu[o]])
                        T.op("dve", lambda e: e.tensor_tensor(out=xin[:, dc, :], in0=psu[o][:], in1=xin[:, dc, :], op=ALU.add),
                             reads=[Bpsu[o], Bxin], writes=[Bxin])
                    T.dma("pool", dstB, xT(dst, j * TT, TT), Bxin, xin[:], Bxin)
                T.barrier()

        def phase_qkv(sq, li, aj, srcB, src):
            S = sq["S"]
            J = S // TT
            with contextlib.ExitStack() as ph:
                N = NormCtx(ph)
                NWS = 3
                wqk = [ph.enter_context(sbt("a_wqk%d" % i, [128, 2, KC, 128], BF16)) for i in range(NWS)]
                Bwqk = [Buf("a_wqk%d" % i) for i in range(NWS)]
                wv = ph.enter_context(sbt("a_wv", [128, KC, 1024], BF16))
                Bwv = Buf("a_wv")
                cs = [ph.enter_context(sbt("a_cs%d" % i, [128, 2, TT], F32)) for i in range(2)]
                Bcs = [Buf("a_cs%d" % i) for i in range(2)]
                t1 = [ph.enter_context(sbt("a_t1_%d" % i, [128, 2, TT], F32)) for i in range(2)]
                Bt1 = [Buf("a_t1_%d" % i) for i in range(2)]
                qo = [ph.enter_context(sbt("a_qo%d" % i, [128, TT], BF16)) for i in range(3)]
                Bqo = [Buf("a_qo%d" % i) for i in range(3)]
                vt = [ph.enter_context(sbt("a_vt%d" % i, [128, 1024], BF16)) for i in range(2)]
                Bvt = [Buf("a_vt%d" % i) for i in range(2)]
                psz = [ph.enter_context(pst_("a_psz%d" % i, [128, 2, TT], F32)) for i in range(2)]
                Bpsz = [[Buf("a_psz%d_%d" % (i, v), excl=True) for v in range(2)] for i in range(2)]
                psv = [ph.enter_context(pst_("a_psv%d" % i, [128, TT], F32)) for i in range(2)]
                Bpsv = [Buf("a_psv%d" % i, excl=True) for i in range(2)]
                T.dma("sp", Bwv, wv[:].rearrange("p a b -> p (a b)"), Bwbf, wslice("a%d_wv" % aj), Bwv)
                c0 = 0
                c1 = 0
                for j in range(J):
                    slot = j % 2
                    t0 = j * TT
                    N.emit(slot, srcB, src, t0, TT, "nmix", li)
                    hT, BhT = N.hT[slot], N.BhT[slot]
                    T.dma("sp", Bcs[slot], cs[slot][:, 0, :], Bconst, cos_d[:, t0:t0 + TT], Bcs[slot])
                    T.dma("sp", Bcs[slot], cs[slot][:, 1, :], Bconst, sin_d[:, t0:t0 + TT], Bcs[slot])
                    for m in range(16):
                        ws_ = c0 % NWS
                        a = c0 % 2
                        q = c0 % 3
                        c0 += 1
                        T.dma("sp", Bwqk[ws_], wqk[ws_][:].rearrange("p a b c -> p (a b c)"), Bwbf,
                              wslice("a%d_wqk" % aj, m * 2048, 2048), Bwqk[ws_])
                        for var in range(2):
                            for kc in range(KC):
                                T.op("pe", lambda e: e.matmul(psz[a][:, var, :], wqk[ws_][:, var, kc, :], hT[:, kc, :],
                                                              start=(kc == 0), stop=(kc == KC - 1)),
                                     reads=[Bwqk[ws_], BhT], writes=[Bpsz[a][var]])
                        for var in range(2):
                            T.op("dve", lambda e: e.tensor_tensor(out=t1[a][:, var, :], in0=psz[a][:, var, :], in1=cs[slot][:, var, :], op=ALU.mult),
                                 reads=[Bpsz[a][var], Bcs[slot]], writes=[Bt1[a]])
                        T.op("pool", lambda e: e.tensor_tensor(out=qo[q][:], in0=t1[a][:, 0, :], in1=t1[a][:, 1, :], op=ALU.add),
                             reads=[Bt1[a]], writes=[Bqo[q]])
                        T.dma("pool", sq["Bqk"], sq["qk"][m, :, t0:t0 + TT], Bqo[q], qo[q][:], Bqo[q])
                    for sub in range(4):
                        v_ = c1 % 2
                        c1 += 1
                        for hf in range(2):
                            for kc in range(KC):
                                T.op("pe", lambda e: e.matmul(psv[hf][:], hT[:, kc, sub * 128:(sub + 1) * 128], wv[:, kc, hf * 512:(hf + 1) * 512],
                                                              start=(kc == 0), stop=(kc == KC - 1)),
                                     reads=[Bwv, BhT], writes=[Bpsv[hf]])
                            T.op("act", lambda e: e.activation(out=vt[v_][:, hf * 512:(hf + 1) * 512], in_=psv[hf][:], func=AF.Copy),
                                 reads=[Bpsv[hf]], writes=[Bvt[v_]])
                        tk = t0 + sub * 128
                        T.dma("pool", sq["Bvs"], sq["vs"][:, tk:tk + 128, :].rearrange("h t e -> t h e"),
                              Bvt[v_], vt[v_][:].rearrange("p (h e) -> p h e", h=H), Bvt[v_])
                T.barrier(N.bufs() + Bwqk + [Bwv] + Bcs + Bqo + Bvt)

        def phase_attn(sq, li, aj):
            S = sq["S"]
            NQ = S // TT
            NK = S // 128
            scale = 64 ** -0.5
            with contextlib.ExitStack() as ph:
                kT = [ph.enter_context(sbt("b_kT%d" % i, [128, S], BF16)) for i in range(2)]
                qT = [ph.enter_context(sbt("b_qT%d" % i, [128, S], BF16)) for i in range(2)]
                vh = [ph.enter_context(sbt("b_vh%d" % i, [128, NK, 130], BF16)) for i in range(2)]
                BkT = [Buf("b_kT%d" % i) for i in range(2)]
                BqT = [Buf("b_qT%d" % i) for i in range(2)]
                Bvh = [Buf("b_vh%d" % i) for i in range(2)]
                NP_ = 3
                pT = [ph.enter_context(sbt("b_pT%d" % i, [128, 2, TT], BF16)) for i in range(NP_)]
                BpT = [Buf("b_pT%d" % i) for i in range(NP_)]
                accs = ph.enter_context(sbt("b_accs", [128, 8, 129], F32))
                Baccs = Buf("b_accs")
                sm = ph.enter_context(sbt("b_sm", [128, 32], F32))
                Bsm = Buf("b_sm")
                o1 = [ph.enter_context(sbt("b_o1_%d" % i, [128, 128], F32)) for i in range(2)]
                o2 = [ph.enter_context(sbt("b_o2_%d" % i, [128, 128], F32)) for i in range(2)]
                ob = [ph.enter_context(sbt("b_ob_%d" % i, [128, 128], BF16)) for i in range(4)]
                Bob = [Buf("b_ob%d" % i) for i in range(4)]
                Bo = [Buf("b_o%d" % i) for i in range(2)]
                oT = [ph.enter_context(sbt("b_oT%d" % i, [128, TT], BF16)) for i in range(2)]
                BoT = [Buf("b_oT%d" % i) for i in range(2)]
                pst = [ph.enter_context(pst_("b_pst%d" % i, [128, 2, TT], F32)) for i in range(2)]
                Bpst = [Buf("b_pst%d" % i, excl=True) for i in range(2)]
                pacc = ph.enter_context(pst_("b_pacc", [128, 3, TT], F32))
                Bpacc = [Buf("b_pacc%d" % i, excl=True) for i in range(3)]
                ptr = ph.enter_context(pst_("b_ptr", [128, TT], BF16))
                Bptr = Buf("b_ptr", excl=True)
                for i in range(2):
                    T.op("pool", lambda e: e.memset(vh[i][:, :, 128:130], 1.0), writes=[Bvh[i]])
                cc = 0
                eo = 0
                pending = []
                for h in range(H):
                    s = h % 2
                    T.dma("sp", BkT[s], kT[s][:], sq["Bqk"], sq["qk"][8 + h, :, :], BkT[s])
                    T.dma("sp", BqT[s], qT[s][:], sq["Bqk"], sq["qk"][h, :, :], BqT[s])
                    T.dma("sp", Bvh[s], vh[s][:, :, 0:128], sq["Bvs"], sq["vs"][h, :, :].rearrange("(kt p) e -> p kt e", p=128), Bvh[s])
                    units = [(qt_, kt_) for qt_ in range(NQ) for kt_ in range(NK)]

                    def emit_S(i):
                        qt_, kt_ = units[i]
                        a_ = (cc + i) % 2
                        for c in range(2):
                            T.op("pe", lambda e: e.matmul(pst[a_][:, c, :], kT[s][c * 64:(c + 1) * 64, kt_ * 128:(kt_ + 1) * 128],
                                                          qT[s][c * 64:(c + 1) * 64, qt_ * TT:(qt_ + 1) * TT], start=True, stop=True),
                                 reads=[BkT[s], BqT[s]], writes=[Bpst[a_]])

                    emit_S(0)
                    for ui in range(len(units)):
                        qt, kt = units[ui]
                        a = (cc + ui) % 2
                        p = (cc + ui) % NP_
                        if ui + 1 < len(units):
                            emit_S(ui + 1)
                        T.op("act", lambda e: e.activation(out=pT[p][:].rearrange("p a b -> p (a b)"),
                                                           in_=pst[a][:].rearrange("p a b -> p (a b)"), func=AF.Exp, scale=scale),
                             reads=[Bpst[a]], writes=[BpT[p]])
                        for c in range(2):
                            for qb in range(4):
                                ai = c * 4 + qb
                                bk, col = ai // 3, (ai % 3) * 129
                                T.op("pe", lambda e: e.matmul(pacc[:, bk, col:col + 129], pT[p][:, c, qb * 128:(qb + 1) * 128], vh[s][:, kt, 0:129],
                                                              start=(kt == 0 and ai % 3 == 0), stop=(kt == NK - 1), skip_group_check=True),
                                     reads=[BpT[p], Bvh[s]], writes=[Bpacc[bk]])
                        if kt == min(2, NK - 2) and pending:
                            for f_ in pending:
                                f_()
                            pending = []
                        if kt != NK - 1:
                            continue
                        for bk in range(3):
                            na = 3 if bk < 2 else 2
                            T.op("dve", lambda e: e.tensor_copy(out=accs[:, bk * 3:bk * 3 + na, :].rearrange("p a b -> p (a b)"), in_=pacc[:, bk, 0:na * 129]),
                                 reads=[Bpacc[bk]], writes=[Baccs])
                        T.op("dve", lambda e: e.reciprocal(out=sm[:, 0:8], in_=accs[:, :, 128]), reads=[Baccs], writes=[Bsm])
                        T.op("dve", lambda e: e.tensor_scalar(out=sm[:, 8:12], in0=sm[:, 4:8], scalar1=lamt[:, aj:aj + 1], scalar2=None, op0=ALU.mult),
                             reads=[Bsm, Bmisc], writes=[Bsm])
                        ot_s = eo % 2
                        eo += 1
                        for qb in range(4):
                            w_ = qb % 2
                            T.op("dve", lambda e: e.tensor_scalar(out=o2[w_][:], in0=accs[:, 4 + qb, 0:128], scalar1=sm[:, 8 + qb:9 + qb], scalar2=None, op0=ALU.mult),
                                 reads=[Baccs, Bsm], writes=[Bo[w_]])
                            T.op("dve", lambda e: e.scalar_tensor_tensor(out=o1[w_][:], in0=accs[:, qb, 0:128], scalar=sm[:, qb:qb + 1], in1=o2[w_][:],
                                                                         op0=ALU.mult, op1=ALU.add),
                                 reads=[Baccs, Bsm, Bo[w_]], writes=[Bo[w_]])
                            T.op("act", lambda e: e.activation(out=o2[w_][:], in_=o1[w_][:], func=AF.Square, accum_out=sm[:, 12 + qb:13 + qb]),
                                 reads=[Bo[w_], Bsm], writes=[Bo[w_], Bsm])
                            T.op("act", lambda e: e.activation(out=sm[:, 16 + qb:17 + qb], in_=sm[:, 12 + qb:13 + qb], func=AF.Ln, scale=1.0 / 128, bias=SUBLN_EPS),
                                 reads=[Bsm], writes=[Bsm])
                            T.op("act", lambda e: e.activation(out=sm[:, 20 + qb:21 + qb], in_=sm[:, 16 + qb:17 + qb], func=AF.Exp, scale=-0.5),
                                 reads=[Bsm], writes=[Bsm])
                            T.op("dve", lambda e: e.scalar_tensor_tensor(out=ob[qb][:], in0=o1[w_][:], scalar=sm[:, 20 + qb:21 + qb], in1=subgs[:, aj, :],
                                                                         op0=ALU.mult, op1=ALU.mult),
                                 reads=[Bo[w_], Bsm, Bmisc], writes=[Bob[qb]])

                        def part2(qt=qt, ot_s=ot_s, h=h):
                            for qb in range(4):
                                T.op("pe", lambda e: e.transpose(ptr[:, qb * 128:(qb + 1) * 128], ob[qb][:], ident[:]),
                                     reads=[Bob[qb], Bmisc], writes=[Bptr])
                            T.op("act", lambda e: e.activation(out=oT[ot_s][:], in_=ptr[:], func=AF.Copy), reads=[Bptr], writes=[BoT[ot_s]])
                            T.dma("pool", sq["Bot"], sq["ot"][h, :, qt * TT:(qt + 1) * TT], BoT[ot_s], oT[ot_s][:], BoT[ot_s])

                        pending.append(part2)
                    for f_ in pending:
                        f_()
                    pending = []
                    cc += len(units)
                T.barrier(BkT + BqT + Bvh + BoT)

        def phase_aout(sq, li, aj, srcB, src, dstB, dst):
            S = sq["S"]
            J = S // TT
            with contextlib.ExitStack() as ph:
                wo = ph.enter_context(sbt("c_wo", [128, 8, KC, 128], BF16))
                Bwo = Buf("c_wo")
                oT = [ph.enter_context(sbt("c_oT%d" % i, [128, H, TT], BF16)) for i in range(2)]
                BoT = [Buf("c_oT%d" % i) for i in range(2)]
                xr = [ph.enter_context(sbt("c_xr%d" % i, [128, KC, TT], F32)) for i in range(2)]
                Bxr = [Buf("c_xr%d" % i) for i in range(2)]
                xn = [ph.enter_context(sbt("c_xn%d" % i, [128, KC, TT], F32)) for i in range(2)]
                Bxn = [Buf("c_xn%d" % i) for i in range(2)]
                ps = [ph.enter_context(pst_("c_ps%d" % i, [128, TT], F32)) for i in range(2)]
                Bps = [Buf("c_ps%d" % i, excl=True) for i in range(2)]
                T.dma("sp", Bwo, wo[:].rearrange("p a b c -> p (a b c)"), Bwbf, wslice("a%d_wo" % aj), Bwo)
                for j in range(J):
                    s = j % 2
                    t0 = j * TT
                    T.dma("sp", BoT[s], oT[s][:], sq["Bot"], sq["ot"][:, :, t0:t0 + TT].rearrange("h p t -> p h t"), BoT[s])
                    T.dma("sp", Bxr[s], xr[s][:], srcB, xT(src, t0, TT), Bxr[s])
                    for dc in range(KC):
                        o = dc % 2
                        for h in range(H):
                            T.op("pe", lambda e: e.matmul(ps[o][:], wo[:, dc, h, :], oT[s][:, h, :], start=(h == 0), stop=(h == H - 1)),
                                 reads=[Bwo, BoT[s]], writes=[Bps[o]])
                        T.op("dve", lambda e: e.tensor_tensor(out=xn[s][:, dc, :], in0=ps[o][:], in1=xr[s][:, dc, :], op=ALU.add),
                             reads=[Bps[o], Bxr[s]], writes=[Bxn[s]])
                    T.dma("pool", dstB, xT(dst, t0, TT), Bxn[s], xn[s][:], Bxn[s])
                T.barrier([Bwo] + BoT + Bxr + Bxn)

        def phase_final(sq, srcB, src):
            S = sq["S"]
            J = S // TT
            with contextlib.ExitStack() as ph:
                N = NormCtx(ph)
                yo = [ph.enter_context(sbt("z_yo%d" % i, [128, KC, TT], F32)) for i in range(2)]
                Byo = [Buf("z_yo%d" % i) for i in range(2)]
                for j in range(J):
                    s = j % 2
                    N.BhT[s] = Byo[s]
                    N.emit(s, srcB, src, j * TT, TT, "nfin", 0, out_f32=yo[s])
                    T.dma("pool", sq["Byout"], xT(sq["yout"], j * TT, TT), Byo[s], yo[s][:], Byo[s])
                T.barrier(N.bufs() + Byo)

        cur = {tag: ("Bxin", "xin") for tag, _ in seqs}

        def nxt(tag):
            return ("Bxb", "xb") if cur[tag][1] in ("xin", "xa") else ("Bxa", "xa")

        for li in range(DEPTH):
            for tag, _ in seqs:
                sq = SQ[tag]
                sB, s = cur[tag]
                dB, d = nxt(tag)
                if li % 2 == 0:
                    phase_gmlp(sq, li, li // 2, sq[sB], sq[s], sq[dB], sq[d])
                else:
                    phase_qkv(sq, li, li // 2, sq[sB], sq[s])
                    phase_attn(sq, li, li // 2)
                    phase_aout(sq, li, li // 2, sq[sB], sq[s], sq[dB], sq[d])
                cur[tag] = (dB, d)
            for tag, _ in seqs:
                sq = SQ[tag]
                sB, s = cur[tag]
                dB, d = nxt(tag)
                phase_ffn(sq, li, sq[sB], sq[s], sq[dB], sq[d])
                cur[tag] = (dB, d)
        for tag, _ in seqs:
            sq = SQ[tag]
            sB, s = cur[tag]
            phase_final(sq, sq[sB], sq[s])
        T.barrier([], release=False)
        build_program.stats = (dict(T.nins), T.nwait, len(T.semh))
    return nc


N_CORES = 8


def kernel(**inputs):
    xs = np.asarray(inputs["x_sample"], dtype=np.float32)
    xp = np.asarray(inputs["x_prompt"], dtype=np.float32)
    Ss, Sp = xs.shape[1], xp.shape[1]
    wall, pall, cosT, sinT = host_pack(inputs, max(Ss, Sp))
    nc = build_program([("s", Ss), ("p", Sp)])
    in_maps = []
    for c in range(N_CORES):
        in_maps.append({
            "wall": wall, "pall": pall, "cosT": cosT, "sinT": sinT,
            "x_s": np.ascontiguousarray(xs[c].T),
            "x_p": np.ascontiguousarray(xp[c // 2].T),
        })
    res = run_bass_kernel_spmd(nc, in_maps, core_ids=list(range(N_CORES)))
    y_s = np.stack([np.ascontiguousarray(res.results[c]["y_s"].T) for c in range(N_CORES)], axis=0)
    y_p = np.stack([np.ascontiguousarray(res.results[2 * i]["y_p"].T) for i in range(xp.shape[0])], axis=0)
    return (y_p.astype(np.float32), y_s.astype(np.float32))
```

```python
import contextlib
import math
import numpy as np
import concourse.bass as bass
import concourse.mybir as mybir
from concourse.bass_utils import run_bass_kernel_spmd

F32 = mybir.dt.float32
BF16 = mybir.dt.bfloat16
AF = mybir.ActivationFunctionType
ALU = mybir.AluOpType

D = 1024
KC = 8
DEPTH = 4
DFF = 2816
NPAIR = 22
H = 8
NORM_EPS = 1e-6
SUBLN_EPS = 1e-5
ROPE_THETA = 10000.0
TT = 512
GELU_C = 0.044715


class Buf:
    __slots__ = ("name", "w", "r", "excl", "merge", "dsem")

    def __init__(self, name, excl=False, merge=False):
        self.name = name
        self.w = {}
        self.r = {}
        self.excl = excl
        self.merge = merge
        self.dsem = {}


class Trk:
    EPOCH = 30000

    def __init__(self, nc, es):
        self.nc = nc
        self.es = es
        self.eng = {"pe": nc.tensor, "act": nc.scalar, "dve": nc.vector, "pool": nc.gpsimd, "sp": nc.sync}
        self.semh = {}
        self.cnt = {e: 0 for e in self.eng}
        self.epoch = {e: 0 for e in self.eng}
        self.cur = {}
        self.waited = {e: {} for e in self.eng}
        self.dval = {}
        self.active = []
        self.dfree = {e: [] for e in self.eng}
        self.nd = 0
        self.nwait = 0
        self.nins = {e: 0 for e in self.eng}
        for e in self.eng:
            self._newsem(e)

    def _mk(self, name):
        self.semh[name] = self.es.enter_context(self.nc.semaphore(name))
        return name

    def _newsem(self, e):
        self.cur[e] = self._mk("q_%s_%d" % (e, self.epoch[e]))
        self.cnt[e] = 0

    def _deps(self, reads, writes):
        deps = {}
        reads = [b for b in reads if not b.merge]
        writes = [b for b in writes if not b.merge]
        for b in reads:
            for s, v in b.w.items():
                if deps.get(s, 0) < v:
                    deps[s] = v
            if b.excl:
                for s, v in b.r.items():
                    if deps.get(s, 0) < v:
                        deps[s] = v
        for b in writes:
            for s, v in b.w.items():
                if deps.get(s, 0) < v:
                    deps[s] = v
            for s, v in b.r.items():
                if deps.get(s, 0) < v:
                    deps[s] = v
        return deps

    def _wait(self, e, deps):
        engine = self.eng[e]
        wd = self.waited[e]
        own = self.cur[e]
        for s, v in deps.items():
            if e == "pe" and s == own:
                continue
            if wd.get(s, 0) >= v:
                continue
            engine.wait_ge(self.semh[s], v)
            wd[s] = v
            self.nwait += 1

    def _record(self, ev, reads, writes):
        s, v = ev
        for b in writes:
            if b.merge:
                continue
            b.w = {s: v}
            b.r = {}
        for b in reads:
            if b.merge:
                continue
            if b.excl:
                b.w = {s: v}
                b.r = {}
            elif b.r.get(s, 0) < v:
                b.r[s] = v

    def op(self, e, fn, reads=(), writes=()):
        if self.cnt[e] >= self.EPOCH:
            self.epoch[e] += 1
            self._newsem(e)
        self._wait(e, self._deps(reads, writes))
        ins = fn(self.eng[e])
        self.cnt[e] += 1
        ev = (self.cur[e], self.cnt[e])
        ins.then_inc(self.semh[ev[0]], 1)
        self._record(ev, reads, writes)
        self.nins[e] += 1
        return ins

    def dma(self, q, dst, dst_ap, src, src_ap, side, slow=False):
        if q not in side.dsem:
            if self.dfree[q]:
                side.dsem[q] = self.dfree[q].pop()
            else:
                side.dsem[q] = self._mk("d%s%d" % (q, self.nd))
                self.nd += 1
                self.dval[side.dsem[q]] = 0
            self.active.append((side, q))
        name = side.dsem[q]
        self._wait(q, self._deps((src,), (dst,)))
        if slow:
            ins = self.eng[q].dma_start(out=dst_ap, in_=src_ap, allow_slow_non_contiguous=True)
        else:
            ins = self.eng[q].dma_start(out=dst_ap, in_=src_ap)
        self.dval[name] += 16
        ev = (name, self.dval[name])
        ins.then_inc(self.semh[name], 16)
        self._record(ev, (src,), (dst,))
        self.nins[q] += 1
        return ins

    def barrier(self, bufs=(), release=True):
        deps = {}
        for e in self.eng:
            if self.cnt[e] > 0:
                deps[self.cur[e]] = self.cnt[e]
        for b, q in self.active:
            deps[b.dsem[q]] = self.dval[b.dsem[q]]
        for e in self.eng:
            self._wait(e, dict(deps))
        if release:
            for b, q in self.active:
                self.dfree[q].append(b.dsem.pop(q))
            self.active = []


class Lay:
    def __init__(self):
        self.off = {}
        self.n = 0

    def add(self, name, ncols):
        self.off[name] = (self.n, ncols)
        self.n += ncols


def make_layouts():
    wl = Lay()
    for j in range(2):
        wl.add("g%d_wu" % j, 8 * 8 * 128)
        wl.add("g%d_wv" % j, 8 * 1024)
        wl.add("g%d_ws" % j, 8 * 128)
        wl.add("g%d_wo" % j, 8 * 8 * 128)
    for j in range(2):
        wl.add("a%d_wqk" % j, 16 * 2 * 8 * 128)
        wl.add("a%d_wv" % j, 8 * 1024)
        wl.add("a%d_wo" % j, 8 * 8 * 128)
    for i in range(DEPTH):
        wl.add("f%d_win" % i, 44 * 8 * 128)
        wl.add("f%d_wout" % i, 8 * NPAIR * 128)
    pl = Lay()
    pl.add("nmix", DEPTH * 8)
    pl.add("nffn", DEPTH * 8)
    pl.add("nfin", 8)
    pl.add("vgain", 2 * 8)
    pl.add("convw", DEPTH * 44 * 4)
    pl.add("bsrep", 2 * 8 * 128)
    pl.add("subg", 2 * 128)
    pl.add("lam", 2 * 4 * 64)
    pl.add("ident", 128)
    return wl, pl


def pack_lhsT(w):
    K, M = w.shape
    return w.reshape(K // 128, 128, M // 128, 128).transpose(1, 2, 0, 3).reshape(128, -1)


def pack_rhs(w):
    K, N = w.shape
    return w.reshape(K // 128, 128, N).transpose(1, 0, 2).reshape(128, -1)


def pack_wout(w):
    K, N = w.shape
    return w.reshape(K // 128, 128, N // 128, 128).transpose(1, 2, 0, 3).reshape(128, -1)


def host_pack(inp, smax):
    wl, pl = make_layouts()
    wall = np.empty((128, wl.n), np.float32)
    pall = np.empty((128, pl.n), np.float32)

    def putw(name, arr):
        o, n = wl.off[name]
        assert arr.shape == (128, n), (name, arr.shape, n)
        wall[:, o:o + n] = arr

    def putp(name, arr):
        o, n = pl.off[name]
        assert arr.shape == (128, n), (name, arr.shape, n)
        pall[:, o:o + n] = arr

    for j in range(2):
        w = np.asarray(inp["gmlp_w_in"][j])
        putw("g%d_wu" % j, pack_lhsT(w[:, :1024]))
        putw("g%d_wv" % j, pack_rhs(w[:, 1024:]))
        putw("g%d_ws" % j, np.asarray(inp["gmlp_w_s"][j]).transpose(2, 0, 1).reshape(128, -1))
        putw("g%d_wo" % j, pack_wout(np.asarray(inp["gmlp_w_out"][j])))
    swap = np.arange(128)
    swap = (swap // 64) * 64 + ((swap % 64) + 32) % 64
    for j in range(2):
        w = np.asarray(inp["diff_w_qkv"][j])
        qk = w[:, :2048].reshape(1024, 16, 128)
        both = np.stack([qk, qk[:, :, swap]], axis=2)
        arr = both.reshape(8, 128, 16, 2, 128).transpose(1, 2, 3, 0, 4)
        putw("a%d_wqk" % j, arr.reshape(128, -1))
        putw("a%d_wv" % j, pack_rhs(w[:, 2048:]))
        putw("a%d_wo" % j, pack_wout(np.asarray(inp["diff_w_out"][j])))
    for i in range(DEPTH):
        w = np.asarray(inp["ffn_w_in"][i])
        wp = np.stack([w[:, :DFF].reshape(1024, NPAIR, 128), w[:, DFF:].reshape(1024, NPAIR, 128)], axis=2)
        putw("f%d_win" % i, pack_lhsT(wp.reshape(1024, 2 * DFF)))
        putw("f%d_wout" % i, pack_wout(np.asarray(inp["ffn_w_out"][i])))

    def pervec(v):
        v = np.asarray(v).reshape(-1, 8, 128)
        return v.transpose(2, 0, 1).reshape(128, -1)

    putp("nmix", pervec(inp["norm_mix"]))
    putp("nffn", pervec(inp["norm_ffn"]))
    putp("nfin", pervec(inp["norm_final"]))
    putp("vgain", pervec(inp["gmlp_v_gain"]))
    cw = np.asarray(inp["ffn_conv_w"])
    cb = np.asarray(inp["ffn_conv_b"])
    c4 = np.concatenate([cw, cb[:, None, :]], axis=1)
    c4 = c4.reshape(DEPTH, 4, 2, NPAIR, 128)
    putp("convw", c4.transpose(4, 0, 3, 2, 1).reshape(128, -1))
    bs = np.asarray(inp["gmlp_b_s"])
    bsr = np.broadcast_to(bs[None, :, :, :], (128, 2, 8, 128))
    putp("bsrep", np.ascontiguousarray(bsr).reshape(128, -1))
    sg = np.asarray(inp["diff_subln_g"])
    putp("subg", np.ascontiguousarray(np.broadcast_to(sg[None], (128, 2, 128))).reshape(128, -1))
    lam = np.stack([np.asarray(inp["diff_lam_q1"]), np.asarray(inp["diff_lam_k1"]),
                    np.asarray(inp["diff_lam_q2"]), np.asarray(inp["diff_lam_k2"])], axis=1)
    putp("lam", np.ascontiguousarray(np.broadcast_to(lam[None], (128, 2, 4, 64))).reshape(128, -1))
    putp("ident", np.eye(128, dtype=np.float32))
    pos = np.arange(smax, dtype=np.float32)
    inv_freq = (ROPE_THETA ** (-np.arange(0, 64, 2, dtype=np.float32) / np.float32(64))).astype(np.float32)
    ang = (pos[:, None] * inv_freq[None, :]).astype(np.float32)
    dd = np.arange(128) % 64
    cosT = np.cos(ang).astype(np.float32).T[dd % 32, :]
    sinT = np.sin(ang).astype(np.float32).T[dd % 32, :]
    sgn = np.where(dd < 32, -1.0, 1.0).astype(np.float32)[:, None]
    return wall, pall, np.ascontiguousarray(cosT), np.ascontiguousarray(sinT * sgn)


def build_program(seqs):
    wl, pl = make_layouts()
    smax = max(s for _, s in seqs)
    nc = bass.Bass("TRN2", target_bir_lowering=False)
    wall = nc.dram_tensor("wall", [128, wl.n], F32, kind="ExternalInput").ap()
    pall_d = nc.dram_tensor("pall", [128, pl.n], F32, kind="ExternalInput").ap()
    cos_d = nc.dram_tensor("cosT", [128, smax], F32, kind="ExternalInput").ap()
    sin_d = nc.dram_tensor("sinT", [128, smax], F32, kind="ExternalInput").ap()
    wbf = nc.dram_tensor("wbf", [128, wl.n], BF16, kind="Internal").ap()
    Bwall = Buf("wall", merge=True)
    Bwbf = Buf("wbf", merge=True)
    Bconst = Buf("constd", merge=True)
    SQ = {}
    for tag, S in seqs:
        d = {}
        d["S"] = S
        d["xin"] = nc.dram_tensor("x_" + tag, [D, S], F32, kind="ExternalInput").ap()
        d["yout"] = nc.dram_tensor("y_" + tag, [D, S], F32, kind="ExternalOutput").ap()
        d["xa"] = nc.dram_tensor("xa_" + tag, [D, S], F32, kind="Internal").ap()
        d["xb"] = nc.dram_tensor("xb_" + tag, [D, S], F32, kind="Internal").ap()
        d["qk"] = nc.dram_tensor("qk_" + tag, [16, 128, S], BF16, kind="Internal").ap()
        d["vs"] = nc.dram_tensor("vs_" + tag, [H, S, 128], BF16, kind="Internal").ap()
        d["ot"] = nc.dram_tensor("ot_" + tag, [H, 128, S], BF16, kind="Internal").ap()
        for k in ("xin", "yout", "xa", "xb", "qk", "vs", "ot"):
            d["B" + k] = Buf(k + "_" + tag, merge=True)
        SQ[tag] = d

    uid = [0]

    def sbt(name, shape, dt):
        uid[0] += 1
        return nc.sbuf_tensor("%s_u%d" % (name, uid[0]), shape, dt)

    def pst_(name, shape, dt):
        uid[0] += 1
        return nc.psum_tensor("%s_u%d" % (name, uid[0]), shape, dt)

    es = contextlib.ExitStack()
    with es:
        T = Trk(nc, es)

        def wslice(name, a=0, n=None):
            o, tot = wl.off[name]
            if n is None:
                n = tot - a
            return wbf[:, o + a:o + a + n]

        pall = es.enter_context(sbt("pall_sb", [128, pl.n], F32))
        Bpall = Buf("pall")
        T.dma("sp", Bpall, pall[:], Bconst, pall_d[:, :], Bpall)

        def pcol(name, a, n=1):
            o, _ = pl.off[name]
            return pall[:, o + a:o + a + n]

        ident = es.enter_context(sbt("ident", [128, 128], BF16))
        ones32 = es.enter_context(sbt("ones32", [128, 128], F32))
        lamt = es.enter_context(sbt("lamt", [128, 16], F32))
        subgs = es.enter_context(sbt("subgs", [128, 2, 128], F32))
        Bmisc = Buf("misc")
        T.op("dve", lambda e: e.tensor_copy(out=ident[:], in_=pcol("ident", 0, 128)), reads=[Bpall], writes=[Bmisc])
        T.op("pool", lambda e: e.memset(ones32[:], 1.0), writes=[Bmisc])
        lam_inits = []
        for j in range(2):
            li = 0.8 - 0.6 * math.exp(-0.3 * (2 * j + 1))
            lam_inits.append(li)
            lo = pl.off["lam"][0] + j * 256
            for w_ in range(2):
                T.op("dve", lambda e: e.tensor_tensor(out=subgs[:, 0, 0:64], in0=pall[:, lo + w_ * 128:lo + w_ * 128 + 64],
                                                      in1=pall[:, lo + w_ * 128 + 64:lo + w_ * 128 + 128], op=ALU.mult),
                     reads=[Bpall, Bmisc], writes=[Bmisc])
                T.op("dve", lambda e: e.reduce_sum(out=lamt[:, 8 + w_:9 + w_], in_=subgs[:, 0, 0:64], axis=mybir.AxisListType.X),
                     reads=[Bmisc], writes=[Bmisc])
            T.op("act", lambda e: e.activation(out=lamt[:, 10:12], in_=lamt[:, 8:10], func=AF.Exp), reads=[Bmisc], writes=[Bmisc])
            T.op("dve", lambda e: e.scalar_tensor_tensor(out=lamt[:, j:j + 1], in0=lamt[:, 11:12], scalar=-li, in1=lamt[:, 10:11],
                                                         op0=ALU.add, op1=ALU.subtract), reads=[Bmisc], writes=[Bmisc])
        for j in range(2):
            T.op("dve", lambda e: e.tensor_scalar(out=subgs[:, j, :], in0=pcol("subg", j * 128, 128), scalar1=1.0 - lam_inits[j],
                                                  scalar2=None, op0=ALU.mult), reads=[Bpall, Bmisc], writes=[Bmisc])
        T.barrier([Bpall])

        def phase_cast():
            CW = 4096
            with contextlib.ExitStack() as ph:
                NS = 3
                f = [ph.enter_context(sbt("cf%d" % i, [128, CW], F32)) for i in range(NS)]
                b = [ph.enter_context(sbt("cb%d" % i, [128, CW], BF16)) for i in range(NS)]
                Bf = [Buf("cf%d" % i) for i in range(NS)]
                Bb = [Buf("cb%d" % i) for i in range(NS)]
                nch = (wl.n + CW - 1) // CW
                for c in range(nch):
                    s = c % NS
                    a = c * CW
                    n = min(CW, wl.n - a)
                    T.dma("sp", Bf[s], f[s][:, 0:n], Bwall, wall[:, a:a + n], Bf[s])
                    if c % 2 == 0:
                        T.op("dve", lambda e: e.tensor_copy(out=b[s][:, 0:n], in_=f[s][:, 0:n]), reads=[Bf[s]], writes=[Bb[s]])
                    else:
                        T.op("act", lambda e: e.activation(out=b[s][:, 0:n], in_=f[s][:, 0:n], func=AF.Copy), reads=[Bf[s]], writes=[Bb[s]])
                    T.dma("pool", Bwbf, wbf[:, a:a + n], Bb[s], b[s][:, 0:n], Bb[s])
                T.barrier(Bf + Bb)

        phase_cast()

        def xT(ap, t0, n):
            return ap.rearrange("(kc p) t -> p kc t", p=128)[:, :, t0:t0 + n]

        class NormCtx:
            def __init__(self, ph, nslot=2):
                self.ns = nslot
                self.xin = [ph.enter_context(sbt("n_xin%d" % i, [128, KC, TT], F32)) for i in range(nslot)]
                self.hT = [ph.enter_context(sbt("n_hT%d" % i, [128, KC, TT], BF16)) for i in range(nslot)]
                self.sq = ph.enter_context(sbt("n_sq", [128, 2, TT], F32))
                self.rs = ph.enter_context(sbt("n_rs", [128, TT], F32))
                self.ps = ph.enter_context(pst_("n_ps", [128, TT], F32))
                self.Bxin = [Buf("n_xin%d" % i) for i in range(nslot)]
                self.BhT = [Buf("n_hT%d" % i) for i in range(nslot)]
                self.Bsq = [Buf("n_sq0"), Buf("n_sq1")]
                self.Brs = Buf("n_rs")
                self.Bps = Buf("n_ps", excl=True)

            def bufs(self):
                return self.Bxin + self.BhT + self.Bsq + [self.Brs]

            def emit(self, slot, srcB, src_ap, t0, n, gname, gidx, out_f32=None):
                xin, hT = self.xin[slot], self.hT[slot]
                Bxin, BhT = self.Bxin[slot], self.BhT[slot]
                T.dma("sp", Bxin, xin[:, :, 0:n], srcB, xT(src_ap, t0, n), Bxin)
                for kc in range(KC):
                    T.op("act", lambda e: e.activation(out=self.sq[:, kc % 2, 0:n], in_=xin[:, kc, 0:n], func=AF.Square),
                         reads=[Bxin], writes=[self.Bsq[kc % 2]])
                    T.op("pe", lambda e: e.matmul(self.ps[:, 0:n], ones32[:], self.sq[:, kc % 2, 0:n], start=(kc == 0), stop=(kc == KC - 1)),
                         reads=[self.Bsq[kc % 2], Bmisc], writes=[self.Bps])
                T.op("act", lambda e: e.activation(out=self.rs[:, 0:n], in_=self.ps[:, 0:n], func=AF.Ln, scale=1.0 / D, bias=NORM_EPS),
                     reads=[self.Bps], writes=[self.Brs])
                T.op("act", lambda e: e.activation(out=self.rs[:, 0:n], in_=self.rs[:, 0:n], func=AF.Exp, scale=-0.5),
                     reads=[self.Brs], writes=[self.Brs])
                for kc in range(KC):
                    dst = hT[:, kc, 0:n] if out_f32 is None else out_f32[:, kc, 0:n]
                    T.op("dve", lambda e: e.scalar_tensor_tensor(out=dst, in0=xin[:, kc, 0:n], scalar=pcol(gname, gidx * 8 + kc),
                                                                 in1=self.rs[:, 0:n], op0=ALU.mult, op1=ALU.mult),
                         reads=[Bxin, self.Brs, Bpall], writes=[BhT])

        def phase_ffn(sq, li, srcB, src, dstB, dst):
            S = sq["S"]
            J = S // TT
            with contextlib.ExitStack() as ph:
                N = NormCtx(ph)
                NW = 3
                win = [ph.enter_context(sbt("f_win%d" % i, [128, 2, KC, 128], BF16)) for i in range(NW)]
                Bwin = [Buf("f_win%d" % i) for i in range(NW)]
                wout = ph.enter_context(sbt("f_wout", [128, KC, NPAIR, 128], BF16))
                Bwout = [Buf("f_wout%d" % i) for i in range(KC)]
                NA = 3
                aext = [ph.enter_context(sbt("f_aext%d" % i, [128, 2, TT + 2], F32)) for i in range(NA)]
                Baext = [Buf("f_aext%d" % i) for i in range(NA)]
                ct = [ph.enter_context(sbt("f_ct%d" % i, [128, 2, TT], F32)) for i in range(NA)]
                Bct = [Buf("f_ct%d" % i) for i in range(NA)]
                sg = [ph.enter_context(sbt("f_sg%d" % i, [128, TT], F32)) for i in range(NA)]
                Bsg = [Buf("f_sg%d" % i) for i in range(NA)]
                carry = ph.enter_context(sbt("f_carry", [128, 2 * NPAIR, 2], F32))
                Bcarry = [Buf("f_carry%d" % i) for i in range(NPAIR)]
                yT = ph.enter_context(sbt("f_yT", [128, NPAIR, TT], BF16))
                ByT = [Buf("f_yT%d" % i) for i in range(NPAIR)]
                xres = ph.enter_context(sbt("f_xres", [128, KC, TT], F32))
                Bxres = Buf("f_xres")
                psa = [ph.enter_context(pst_("f_psa%d" % i, [128, 2, TT], F32)) for i in range(2)]
                Bpsa = [[Buf("f_psa%d_%d" % (i, h), excl=True) for h in range(2)] for i in range(2)]
                pso = [ph.enter_context(pst_("f_pso%d" % i, [128, TT], F32)) for i in range(2)]
                Bpso = [Buf("f_pso%d" % i, excl=True) for i in range(2)]
                cwo = pl.off["convw"][0] + li * 44 * 4

                T.op("pool", lambda e: e.memset(carry[:], 0.0), writes=Bcarry)
                ctr = [0, 0, 0]
                pend = []
                wout_loaded = [False]

                def win_stage(j, slot):
                    virt = (j == J)
                    hT, BhT = N.hT[slot], N.BhT[slot]
                    n = 1 if virt else TT
                    prev = None
                    for pr in range(NPAIR + 1):
                        if pr == 3:
                            while pend:
                                pend.pop()()
                        if pr == NPAIR // 2 and j + 1 < J:
                            N.emit((j + 1) % 2, srcB, src, (j + 1) * TT, TT, "nffn", li)
                        if pr < NPAIR:
                            ws = ctr[0] % NW
                            ctr[0] += 1
                            a = ctr[1] % NA
                            pb = ctr[1] % 2
                            ctr[1] += 1
                            if not virt:
                                T.dma("sp", Bwin[ws], win[ws][:].rearrange("p a b c -> p (a b c)"), Bwbf,
                                      wslice("f%d_win" % li, pr * 2048, 2048), Bwin[ws])
                                for hf in range(2):
                                    for kc in range(KC):
                                        T.op("pe", lambda e: e.matmul(psa[pb][:, hf, :], win[ws][:, hf, kc, :], hT[:, kc, :],
                                                                      start=(kc == 0), stop=(kc == KC - 1)),
                                             reads=[Bwin[ws], BhT], writes=[Bpsa[pb][hf]])
                            T.op("pool", lambda e: e.tensor_copy(out=aext[a][:, :, 0:2], in_=carry[:, 2 * pr:2 * pr + 2, :]),
                                 reads=[Bcarry[pr]], writes=[Baext[a]])
                            if virt:
                                T.op("pool", lambda e: e.memset(aext[a][:, :, 2:TT + 2], 0.0), writes=[Baext[a]])
                            else:
                                T.op("act", lambda e: e.activation(out=aext[a][:, :, 2:TT + 2], in_=psa[pb][:, :, :], func=AF.Copy),
                                     reads=Bpsa[pb], writes=[Baext[a]])
                                T.op("pool", lambda e: e.tensor_copy(out=carry[:, 2 * pr:2 * pr + 2, :], in_=aext[a][:, :, TT:TT + 2]),
                                     reads=[Baext[a]], writes=[Bcarry[pr]])
                            for hf in range(2):
                                co = cwo + (pr * 2 + hf) * 4
                                T.op("act", lambda e: e.activation(out=ct[a][:, hf, 0:n], in_=aext[a][:, hf, 0:n], func=AF.Identity,
                                                                   scale=pall[:, co:co + 1], bias=pall[:, co + 3:co + 4]),
                                     reads=[Baext[a], Bpall], writes=[Bct[a]])
                        if prev is not None:
                            pa, ppr = prev
                            T.op("act", lambda e: e.activation(out=sg[pa][:, 0:n], in_=ct[pa][:, 0, 0:n], func=AF.Silu),
                                 reads=[Bct[pa]], writes=[Bsg[pa]])
                        if pr < NPAIR:
                            for hf in range(2):
                                co = cwo + (pr * 2 + hf) * 4
                                for t in (1, 2):
                                    T.op("dve", lambda e: e.scalar_tensor_tensor(out=ct[a][:, hf, 0:n], in0=aext[a][:, hf, t:t + n],
                                                                                 scalar=pall[:, co + t:co + t + 1], in1=ct[a][:, hf, 0:n],
                                                                                 op0=ALU.mult, op1=ALU.add),
                                         reads=[Baext[a], Bct[a], Bpall], writes=[Bct[a]])
                        if prev is not None:
                            pa, ppr = prev
                            T.op("dve", lambda e: e.tensor_tensor(out=yT[:, ppr, 0:n], in0=sg[pa][:, 0:n], in1=ct[pa][:, 1, 0:n], op=ALU.mult),
                                 reads=[Bsg[pa], Bct[pa]], writes=[ByT[ppr]])
                        prev = (a, pr) if pr < NPAIR else None

                def wout_stage(j):
                    virt = (j == J)
                    lo = 1 if j == 0 else 0
                    hi = 1 if virt else TT
                    n = hi - lo
                    tok0 = j * TT - 1 + lo
                    T.dma("sp", Bxres, xres[:, :, 0:n], srcB, xT(src, tok0, n), Bxres, slow=(n == 1))
                    for dc in range(KC):
                        o = dc % 2
                        if not wout_loaded[0]:
                            T.dma("sp", Bwout[dc], wout[:, dc, :, :].rearrange("p a b -> p (a b)"), Bwbf,
                                  wslice("f%d_wout" % li, dc * NPAIR * 128, NPAIR * 128), Bwout[dc])
                        for pr in range(NPAIR):
                            T.op("pe", lambda e: e.matmul(pso[o][:, 0:n], wout[:, dc, pr, :], yT[:, pr, lo:hi],
                                                          start=(pr == 0), stop=(pr == NPAIR - 1)),
                                 reads=[Bwout[dc], ByT[pr]], writes=[Bpso[o]])
                        T.op("dve", lambda e: e.tensor_tensor(out=xres[:, dc, 0:n], in0=pso[o][:, 0:n], in1=xres[:, dc, 0:n], op=ALU.add),
                             reads=[Bpso[o], Bxres], writes=[Bxres])
                    wout_loaded[0] = True
                    pend.append(lambda: T.dma("sp", dstB, xT(dst, tok0, n), Bxres, xres[:, :, 0:n], Bxres, slow=(n == 1)))

                N.emit(0, srcB, src, 0, TT, "nffn", li)
                for j in range(J + 1):
                    win_stage(j, j % 2)
                    wout_stage(j)
                while pend:
                    pend.pop()()
                T.barrier()

        def phase_gmlp(sq, li, gj, srcB, src, dstB, dst):
            S = sq["S"]
            J = S // TT
            with contextlib.ExitStack() as ph:
                N = NormCtx(ph)
                wu = ph.enter_context(sbt("g_wu", [128, 8, KC, 128], BF16))
                wv = ph.enter_context(sbt("g_wv", [128, KC, 1024], BF16))
                ws = ph.enter_context(sbt("g_ws", [128, 8, 128], BF16))
                wo = ph.enter_context(sbt("g_wo", [128, 8, KC, 128], BF16))
                Bw = Buf("g_w")
                Bw2 = [Buf("g_w2_%d" % i) for i in range(3)]
                uT = ph.enter_context(sbt("g_uT", [128, 8, TT], F32))
                BuT = [Buf("g_uT%d" % i) for i in range(8)]
                vg = [ph.enter_context(sbt("g_vg%d" % i, [128, 1024], F32)) for i in range(2)]
                Bvg = [Buf("g_vg%d" % i) for i in range(2)]
                vsq = ph.enter_context(sbt("g_vsq", [128, 1024], BF16))
                Bvsq = Buf("g_vsq")
                vn = ph.enter_context(sbt("g_vn", [128, 4, 1024], BF16))
                Bvn = [Buf("g_vn%d" % i) for i in range(4)]
                st = [ph.enter_context(sbt("g_st%d" % i, [128, 8], F32)) for i in range(2)]
                Bst = [Buf("g_st%d" % i) for i in range(2)]
                v2 = [ph.enter_context(sbt("g_v2_%d" % i, [128, TT], F32)) for i in range(2)]
                Bv2 = [Buf("g_v2_%d" % i) for i in range(2)]
                yT = ph.enter_context(sbt("g_yT", [128, 8, TT], BF16))
                ByT = [Buf("g_yT%d" % i) for i in range(8)]
                psu = [ph.enter_context(pst_("g_psu%d" % i, [128, TT], F32)) for i in range(2)]
                Bpsu = [Buf("g_psu%d" % i, excl=True) for i in range(2)]
                psv = [ph.enter_context(pst_("g_psv%d" % i, [128, TT], F32)) for i in range(2)]
                Bpsv = [Buf("g_psv%d" % i, excl=True) for i in range(2)]
                pss = [ph.enter_context(pst_("g_pss%d" % i, [128, TT], F32)) for i in range(2)]
                Bpss = [Buf("g_pss%d" % i, excl=True) for i in range(2)]
                T.dma("sp", Bw, wu[:].rearrange("p a b c -> p (a b c)"), Bwbf, wslice("g%d_wu" % gj), Bw)
                T.dma("sp", Bw2[0], wv[:].rearrange("p a b -> p (a b)"), Bwbf, wslice("g%d_wv" % gj), Bw2[0])
                T.dma("sp", Bw2[1], ws[:].rearrange("p a b -> p (a b)"), Bwbf, wslice("g%d_ws" % gj), Bw2[1])
                T.dma("sp", Bw2[2], wo[:].rearrange("p a b c -> p (a b c)"), Bwbf, wslice("g%d_wo" % gj), Bw2[2])
                bso = pl.off["bsrep"][0] + gj * 8 * 128
                for j in range(J):
                    slot = j % 2
                    N.emit(slot, srcB, src, j * TT, TT, "nmix", li)
                    hT, BhT, xin, Bxin = N.hT[slot], N.BhT[slot], N.xin[slot], N.Bxin[slot]
                    for m in range(8):
                        o = m % 2
                        for kc in range(KC):
                            T.op("pe", lambda e: e.matmul(psu[o][:], wu[:, m, kc, :], hT[:, kc, :], start=(kc == 0), stop=(kc == KC - 1)),
                                 reads=[Bw, BhT], writes=[Bpsu[o]])
                        T.op("act", lambda e: e.activation(out=uT[:, m, :], in_=psu[o][:], func=AF.Gelu_apprx_tanh),
                             reads=[Bpsu[o]], writes=[BuT[m]])
                    for sub in range(4):
                        a = sub % 2
                        for hf in range(2):
                            for kc in range(KC):
                                T.op("pe", lambda e: e.matmul(psv[hf][:], hT[:, kc, sub * 128:(sub + 1) * 128], wv[:, kc, hf * 512:(hf + 1) * 512],
                                                              start=(kc == 0), stop=(kc == KC - 1)),
                                     reads=[Bw2[0], BhT], writes=[Bpsv[hf]])
                            T.op("act", lambda e: e.activation(out=vg[a][:, hf * 512:(hf + 1) * 512], in_=psv[hf][:], func=AF.Gelu_apprx_tanh),
                                 reads=[Bpsv[hf]], writes=[Bvg[a]])
                        T.op("act", lambda e: e.activation(out=vsq[:], in_=vg[a][:], func=AF.Square, accum_out=st[a][:, 0:1]),
                             reads=[Bvg[a]], writes=[Bvsq, Bst[a]])
                        T.op("act", lambda e: e.activation(out=st[a][:, 1:2], in_=st[a][:, 0:1], func=AF.Ln, scale=1.0 / 1024, bias=NORM_EPS),
                             reads=[Bst[a]], writes=[Bst[a]])
                        T.op("act", lambda e: e.activation(out=st[a][:, 2:3], in_=st[a][:, 1:2], func=AF.Exp, scale=-0.5),
                             reads=[Bst[a]], writes=[Bst[a]])
                        T.op("dve", lambda e: e.tensor_scalar(out=vn[:, sub, :], in0=vg[a][:], scalar1=st[a][:, 2:3], scalar2=None, op0=ALU.mult),
                             reads=[Bvg[a], Bst[a]], writes=[Bvn[sub]])
                    for g in range(8):
                        o = g % 2
                        for sub in range(4):
                            T.op("pe", lambda e: e.matmul(pss[o][:, sub * 128:(sub + 1) * 128], vn[:, sub, g * 128:(g + 1) * 128], ws[:, g, :],
                                                          start=True, stop=True, skip_group_check=True),
                                 reads=[Bw2[1], Bvn[sub]], writes=[Bpss[o]])
                        for sub in range(4):
                            T.op("dve", lambda e: e.scalar_tensor_tensor(out=v2[o][:, sub * 128:(sub + 1) * 128], in0=pss[o][:, sub * 128:(sub + 1) * 128],
                                                                         scalar=pcol("vgain", gj * 8 + g),
                                                                         in1=pall[:, bso + g * 128:bso + (g + 1) * 128], op0=ALU.mult, op1=ALU.add),
                                 reads=[Bpss[o], Bpall], writes=[Bv2[o]])
                        T.op("pool", lambda e: e.tensor_tensor(out=yT[:, g, :], in0=uT[:, g, :], in1=v2[o][:], op=ALU.mult),
                             reads=[BuT[g], Bv2[o]], writes=[ByT[g]])
                    for dc in range(KC):
                        o = dc % 2
                        for g in range(8):
                            T.op("pe", lambda e: e.matmul(psu[o][:], wo[:, dc, g, :], yT[:, g, :], start=(g == 0), stop=(g == 7)),
                                 reads=[Bw2[2], ByT[g]], writes=[Bpsu[o]])
                        T.op("dve", lambda e: e.tensor_tensor(out=xin[:, dc, :], in0=psu[o][:], in1=xin[:, dc, :], op=ALU.add),
                             reads=[Bpsu[o], Bxin], writes=[Bxin])
                    T.dma("pool", dstB, xT(dst, j * TT, TT), Bxin, xin[:], Bxin)
                T.barrier()

        def phase_qkv(sq, li, aj, srcB, src):
            S = sq["S"]
            J = S // TT
            with contextlib.ExitStack() as ph:
                N = NormCtx(ph)
                NWS = 3
                wqk = [ph.enter_context(sbt("a_wqk%d" % i, [128, 2, KC, 128], BF16)) for i in range(NWS)]
                Bwqk = [Buf("a_wqk%d" % i) for i in range(NWS)]
                wv = ph.enter_context(sbt("a_wv", [128, KC, 1024], BF16))
                Bwv = Buf("a_wv")
                cs = [ph.enter_context(sbt("a_cs%d" % i, [128, 2, TT], F32)) for i in range(2)]
                Bcs = [Buf("a_cs%d" % i) for i in range(2)]
                t1 = [ph.enter_context(sbt("a_t1_%d" % i, [128, 2, TT], F32)) for i in range(2)]
                Bt1 = [Buf("a_t1_%d" % i) for i in range(2)]
                qo = [ph.enter_context(sbt("a_qo%d" % i, [128, TT], BF16)) for i in range(3)]
                Bqo = [Buf("a_qo%d" % i) for i in range(3)]
                vt = [ph.enter_context(sbt("a_vt%d" % i, [128, 1024], BF16)) for i in range(2)]
                Bvt = [Buf("a_vt%d" % i) for i in range(2)]
                psz = [ph.enter_context(pst_("a_psz%d" % i, [128, 2, TT], F32)) for i in range(2)]
                Bpsz = [[Buf("a_psz%d_%d" % (i, v), excl=True) for v in range(2)] for i in range(2)]
                psv = [ph.enter_context(pst_("a_psv%d" % i, [128, TT], F32)) for i in range(2)]
                Bpsv = [Buf("a_psv%d" % i, excl=True) for i in range(2)]
                T.dma("sp", Bwv, wv[:].rearrange("p a b -> p (a b)"), Bwbf, wslice("a%d_wv" % aj), Bwv)
                c0 = 0
                c1 = 0
                for j in range(J):
                    slot = j % 2
                    t0 = j * TT
                    N.emit(slot, srcB, src, t0, TT, "nmix", li)
                    hT, BhT = N.hT[slot], N.BhT[slot]
                    T.dma("sp", Bcs[slot], cs[slot][:, 0, :], Bconst, cos_d[:, t0:t0 + TT], Bcs[slot])
                    T.dma("sp", Bcs[slot], cs[slot][:, 1, :], Bconst, sin_d[:, t0:t0 + TT], Bcs[slot])
                    for m in range(16):
                        ws_ = c0 % NWS
                        a = c0 % 2
                        q = c0 % 3
                        c0 += 1
                        T.dma("sp", Bwqk[ws_], wqk[ws_][:].rearrange("p a b c -> p (a b c)"), Bwbf,
                              wslice("a%d_wqk" % aj, m * 2048, 2048), Bwqk[ws_])
                        for var in range(2):
                            for kc in range(KC):
                                T.op("pe", lambda e: e.matmul(psz[a][:, var, :], wqk[ws_][:, var, kc, :], hT[:, kc, :],
                                                              start=(kc == 0), stop=(kc == KC - 1)),
                                     reads=[Bwqk[ws_], BhT], writes=[Bpsz[a][var]])
                        for var in range(2):
                            T.op("dve", lambda e: e.tensor_tensor(out=t1[a][:, var, :], in0=psz[a][:, var, :], in1=cs[slot][:, var, :], op=ALU.mult),
                                 reads=[Bpsz[a][var], Bcs[slot]], writes=[Bt1[a]])
                        T.op("pool", lambda e: e.tensor_tensor(out=qo[q][:], in0=t1[a][:, 0, :], in1=t1[a][:, 1, :], op=ALU.add),
                             reads=[Bt1[a]], writes=[Bqo[q]])
                        T.dma("pool", sq["Bqk"], sq["qk"][m, :, t0:t0 + TT], Bqo[q], qo[q][:], Bqo[q])
                    for sub in range(4):
                        v_ = c1 % 2
                        c1 += 1
                        for hf in range(2):
                            for kc in range(KC):
                                T.op("pe", lambda e: e.matmul(psv[hf][:], hT[:, kc, sub * 128:(sub + 1) * 128], wv[:, kc, hf * 512:(hf + 1) * 512],
                                                              start=(kc == 0), stop=(kc == KC - 1)),
                                     reads=[Bwv, BhT], writes=[Bpsv[hf]])
                            T.op("act", lambda e: e.activation(out=vt[v_][:, hf * 512:(hf + 1) * 512], in_=psv[hf][:], func=AF.Copy),
                                 reads=[Bpsv[hf]], writes=[Bvt[v_]])
                        tk = t0 + sub * 128
                        T.dma("pool", sq["Bvs"], sq["vs"][:, tk:tk + 128, :].rearrange("h t e -> t h e"),
                              Bvt[v_], vt[v_][:].rearrange("p (h e) -> p h e", h=H), Bvt[v_])
                T.barrier(N.bufs() + Bwqk + [Bwv] + Bcs + Bqo + Bvt)

        def phase_attn(sq, li, aj):
            S = sq["S"]
            QT = 256
            NQB = QT // 128
            NQ = S // QT
            NK = S // 128
            LA = 3
            NPST = 5
            scale = 64 ** -0.5
            with contextlib.ExitStack() as ph:
                kT = [ph.enter_context(sbt("b_kT%d" % i, [128, S], BF16)) for i in range(2)]
                qT = [[ph.enter_context(sbt("b_qT%d_%d" % (i, c), [128, S], BF16)) for c in range(2)] for i in range(2)]
                vh = [ph.enter_context(sbt("b_vh%d" % i, [128, NK, 130], BF16)) for i in range(2)]
                BkT = [Buf("b_kT%d" % i) for i in range(2)]
                BqT = [Buf("b_qT%d" % i) for i in range(2)]
                Bvh = [Buf("b_vh%d" % i) for i in range(2)]
                NP_ = 4
                pT = [ph.enter_context(sbt("b_pT%d" % i, [128, 2, QT], BF16)) for i in range(NP_)]
                BpT = [Buf("b_pT%d" % i) for i in range(NP_)]
                accs = ph.enter_context(sbt("b_accs", [128, 2 * NQB, 129], F32))
                Baccs = Buf("b_accs")
                sm = ph.enter_context(sbt("b_sm", [128, 32], F32))
                Bsm = Buf("b_sm")
                o1 = [ph.enter_context(sbt("b_o1_%d" % i, [128, 128], F32)) for i in range(2)]
                o2 = [ph.enter_context(sbt("b_o2_%d" % i, [128, 128], F32)) for i in range(2)]
                ob = [ph.enter_context(sbt("b_ob_%d" % i, [128, 128], BF16)) for i in range(2 * NQB)]
                Bob = [Buf("b_ob%d" % i) for i in range(2 * NQB)]
                Bo = [Buf("b_o%d" % i) for i in range(2)]
                oT = [ph.enter_context(sbt("b_oT%d" % i, [128, QT], BF16)) for i in range(2)]
                BoT = [Buf("b_oT%d" % i) for i in range(2)]
                pst = [ph.enter_context(pst_("b_pst%d" % i, [128, 2, QT], F32)) for i in range(NPST)]
                Bpst = [Buf("b_pst%d" % i, excl=True) for i in range(NPST)]
                pacc = ph.enter_context(pst_("b_pacc", [128, 2, TT], F32))
                Bpacc = [Buf("b_pacc%d" % i, excl=True) for i in range(2)]
                ptr = ph.enter_context(pst_("b_ptr", [128, QT], BF16))
                Bptr = Buf("b_ptr", excl=True)
                for i in range(2):
                    T.op("pool", lambda e: e.memset(vh[i][:, :, 128:130], 1.0), writes=[Bvh[i]])
                    T.op("pool", lambda e: e.memset(qT[i][0][64:128, :], 0.0), writes=[BqT[i]])
                    T.op("pool", lambda e: e.memset(qT[i][1][0:64, :], 0.0), writes=[BqT[i]])
                cc = 0
                eo = 0
                pending = []
                for h in range(H):
                    s = h % 2
                    T.dma("sp", BkT[s], kT[s][:], sq["Bqk"], sq["qk"][8 + h, :, :], BkT[s])
                    T.dma("sp", BqT[s], qT[s][0][0:64, :], sq["Bqk"], sq["qk"][h, 0:64, :], BqT[s])
                    T.dma("sp", BqT[s], qT[s][1][64:128, :], sq["Bqk"], sq["qk"][h, 64:128, :], BqT[s])
                    T.dma("sp", Bvh[s], vh[s][:, :, 0:128], sq["Bvs"], sq["vs"][h, :, :].rearrange("(kt p) e -> p kt e", p=128), Bvh[s])
                    units = [(qt_, kt_) for qt_ in range(NQ) for kt_ in range(NK)]

                    def emit_S(i):
                        qt_, kt_ = units[i]
                        a_ = (cc + i) % NPST
                        for c in range(2):
                            T.op("pe", lambda e: e.matmul(pst[a_][:, c, :], kT[s][:, kt_ * 128:(kt_ + 1) * 128],
                                                          qT[s][c][:, qt_ * QT:(qt_ + 1) * QT], start=True, stop=True, skip_group_check=True),
                                 reads=[BkT[s], BqT[s]], writes=[Bpst[a_]])

                    for i0 in range(min(LA, len(units))):
                        emit_S(i0)
                    for ui in range(len(units)):
                        qt, kt = units[ui]
                        a = (cc + ui) % NPST
                        p = (cc + ui) % NP_
                        if ui + LA < len(units):
                            emit_S(ui + LA)
                        T.op("act", lambda e: e.activation(out=pT[p][:].rearrange("p a b -> p (a b)"),
                                                           in_=pst[a][:].rearrange("p a b -> p (a b)"), func=AF.Exp, scale=scale),
                             reads=[Bpst[a]], writes=[BpT[p]])
                        for c in range(2):
                            for qb in range(NQB):
                                ai = c * NQB + qb
                                bk, col = ai // 2, (ai % 2) * 129
                                T.op("pe", lambda e: e.matmul(pacc[:, bk, col:col + 129], pT[p][:, c, qb * 128:(qb + 1) * 128], vh[s][:, kt, 0:129],
                                                              start=(kt == 0 and ai % 2 == 0), stop=(kt == NK - 1), skip_group_check=True),
                                     reads=[BpT[p], Bvh[s]], writes=[Bpacc[bk]])
                        if kt == min(4, NK - 2) and pending:
                            for f_ in pending:
                                f_()
                            pending = []
                        if kt != NK - 1:
                            continue
                        for bk in range(2):
                            T.op("dve", lambda e: e.tensor_copy(out=accs[:, bk * 2:bk * 2 + 2, :].rearrange("p a b -> p (a b)"), in_=pacc[:, bk, 0:2 * 129]),
                                 reads=[Bpacc[bk]], writes=[Baccs])
                        T.op("dve", lambda e: e.reciprocal(out=sm[:, 0:2 * NQB], in_=accs[:, :, 128]), reads=[Baccs], writes=[Bsm])
                        T.op("dve", lambda e: e.tensor_scalar(out=sm[:, 8:8 + NQB], in0=sm[:, NQB:2 * NQB], scalar1=lamt[:, aj:aj + 1], scalar2=None, op0=ALU.mult),
                             reads=[Bsm, Bmisc], writes=[Bsm])
                        ot_s = eo % 2
                        eo += 1
                        for qb in range(NQB):
                            w_ = qb % 2
                            T.op("dve", lambda e: e.tensor_scalar(out=o2[w_][:], in0=accs[:, NQB + qb, 0:128], scalar1=sm[:, 8 + qb:9 + qb], scalar2=None, op0=ALU.mult),
                                 reads=[Baccs, Bsm], writes=[Bo[w_]])
                            T.op("dve", lambda e: e.scalar_tensor_tensor(out=o1[w_][:], in0=accs[:, qb, 0:128], scalar=sm[:, qb:qb + 1], in1=o2[w_][:],
                                                                         op0=ALU.mult, op1=ALU.add),
                                 reads=[Baccs, Bsm, Bo[w_]], writes=[Bo[w_]])
                            T.op("act", lambda e: e.activation(out=o2[w_][:], in_=o1[w_][:], func=AF.Square, accum_out=sm[:, 12 + qb:13 + qb]),
                                 reads=[Bo[w_], Bsm], writes=[Bo[w_], Bsm])
                            T.op("act", lambda e: e.activation(out=sm[:, 16 + qb:17 + qb], in_=sm[:, 12 + qb:13 + qb], func=AF.Ln, scale=1.0 / 128, bias=SUBLN_EPS),
                                 reads=[Bsm], writes=[Bsm])
                            T.op("act", lambda e: e.activation(out=sm[:, 20 + qb:21 + qb], in_=sm[:, 16 + qb:17 + qb], func=AF.Exp, scale=-0.5),
                                 reads=[Bsm], writes=[Bsm])
                            T.op("dve", lambda e: e.scalar_tensor_tensor(out=ob[(eo % 2) * NQB + qb][:], in0=o1[w_][:], scalar=sm[:, 20 + qb:21 + qb], in1=subgs[:, aj, :],
                                                                         op0=ALU.mult, op1=ALU.mult),
                                 reads=[Bo[w_], Bsm, Bmisc], writes=[Bob[(eo % 2) * NQB + qb]])

                        def part2(qt=qt, ot_s=ot_s, h=h, ob0=(eo % 2) * NQB):
                            for qb in range(NQB):
                                T.op("pe", lambda e: e.transpose(ptr[:, qb * 128:(qb + 1) * 128], ob[ob0 + qb][:], ident[:]),
                                     reads=[Bob[ob0 + qb], Bmisc], writes=[Bptr])
                            T.op("act", lambda e: e.activation(out=oT[ot_s][:], in_=ptr[:], func=AF.Copy), reads=[Bptr], writes=[BoT[ot_s]])
                            T.dma("pool", sq["Bot"], sq["ot"][h, :, qt * QT:(qt + 1) * QT], BoT[ot_s], oT[ot_s][:], BoT[ot_s])

                        pending.append(part2)
                    for f_ in pending:
                        f_()
                    pending = []
                    cc += len(units)
                T.barrier()

        def phase_aout(sq, li, aj, srcB, src, dstB, dst):
            S = sq["S"]
            J = S // TT
            with contextlib.ExitStack() as ph:
                wo = ph.enter_context(sbt("c_wo", [128, 8, KC, 128], BF16))
                Bwo = Buf("c_wo")
                oT = [ph.enter_context(sbt("c_oT%d" % i, [128, H, TT], BF16)) for i in range(2)]
                BoT = [Buf("c_oT%d" % i) for i in range(2)]
                xr = [ph.enter_context(sbt("c_xr%d" % i, [128, KC, TT], F32)) for i in range(2)]
                Bxr = [Buf("c_xr%d" % i) for i in range(2)]
                xn = [ph.enter_context(sbt("c_xn%d" % i, [128, KC, TT], F32)) for i in range(2)]
                Bxn = [Buf("c_xn%d" % i) for i in range(2)]
                ps = [ph.enter_context(pst_("c_ps%d" % i, [128, TT], F32)) for i in range(2)]
                Bps = [Buf("c_ps%d" % i, excl=True) for i in range(2)]
                T.dma("sp", Bwo, wo[:].rearrange("p a b c -> p (a b c)"), Bwbf, wslice("a%d_wo" % aj), Bwo)
                for j in range(J):
                    s = j % 2
                    t0 = j * TT
                    T.dma("sp", BoT[s], oT[s][:], sq["Bot"], sq["ot"][:, :, t0:t0 + TT].rearrange("h p t -> p h t"), BoT[s])
                    T.dma("sp", Bxr[s], xr[s][:], srcB, xT(src, t0, TT), Bxr[s])
                    for dc in range(KC):
                        o = dc % 2
                        for h in range(H):
                            T.op("pe", lambda e: e.matmul(ps[o][:], wo[:, dc, h, :], oT[s][:, h, :], start=(h == 0), stop=(h == H - 1)),
                                 reads=[Bwo, BoT[s]], writes=[Bps[o]])
                        T.op("dve", lambda e: e.tensor_tensor(out=xn[s][:, dc, :], in0=ps[o][:], in1=xr[s][:, dc, :], op=ALU.add),
                             reads=[Bps[o], Bxr[s]], writes=[Bxn[s]])
                    T.dma("pool", dstB, xT(dst, t0, TT), Bxn[s], xn[s][:], Bxn[s])
                T.barrier([Bwo] + BoT + Bxr + Bxn)

        def phase_final(sq, srcB, src):
            S = sq["S"]
            J = S // TT
            with contextlib.ExitStack() as ph:
                N = NormCtx(ph)
                yo = [ph.enter_context(sbt("z_yo%d" % i, [128, KC, TT], F32)) for i in range(2)]
                Byo = [Buf("z_yo%d" % i) for i in range(2)]
                for j in range(J):
                    s = j % 2
                    N.BhT[s] = Byo[s]
                    N.emit(s, srcB, src, j * TT, TT, "nfin", 0, out_f32=yo[s])
                    T.dma("pool", sq["Byout"], xT(sq["yout"], j * TT, TT), Byo[s], yo[s][:], Byo[s])
                T.barrier(N.bufs() + Byo)

        cur = {tag: ("Bxin", "xin") for tag, _ in seqs}

        def nxt(tag):
            return ("Bxb", "xb") if cur[tag][1] in ("xin", "xa") else ("Bxa", "xa")

        for li in range(DEPTH):
            for tag, _ in seqs:
                sq = SQ[tag]
                sB, s = cur[tag]
                dB, d = nxt(tag)
                if li % 2 == 0:
                    phase_gmlp(sq, li, li // 2, sq[sB], sq[s], sq[dB], sq[d])
                else:
                    phase_qkv(sq, li, li // 2, sq[sB], sq[s])
                    phase_attn(sq, li, li // 2)
                    phase_aout(sq, li, li // 2, sq[sB], sq[s], sq[dB], sq[d])
                cur[tag] = (dB, d)
            for tag, _ in seqs:
                sq = SQ[tag]
                sB, s = cur[tag]
                dB, d = nxt(tag)
                phase_ffn(sq, li, sq[sB], sq[s], sq[dB], sq[d])
                cur[tag] = (dB, d)
        for tag, _ in seqs:
            sq = SQ[tag]
            sB, s = cur[tag]
            phase_final(sq, sq[sB], sq[s])
        T.barrier([], release=False)
        build_program.stats = (dict(T.nins), T.nwait, len(T.semh))
    return nc


N_CORES = 8


def kernel(**inputs):
    xs = np.asarray(inputs["x_sample"], dtype=np.float32)
    xp = np.asarray(inputs["x_prompt"], dtype=np.float32)
    Ss, Sp = xs.shape[1], xp.shape[1]
    wall, pall, cosT, sinT = host_pack(inputs, max(Ss, Sp))
    nc = build_program([("s", Ss), ("p", Sp)])
    in_maps = []
    for c in range(N_CORES):
        in_maps.append({
            "wall": wall, "pall": pall, "cosT": cosT, "sinT": sinT,
            "x_s": np.ascontiguousarray(xs[c].T),
            "x_p": np.ascontiguousarray(xp[c // 2].T),
        })
    res = run_bass_kernel_spmd(nc, in_maps, core_ids=list(range(N_CORES)))
    y_s = np.stack([np.ascontiguousarray(res.results[c]["y_s"].T) for c in range(N_CORES)], axis=0)
    y_p = np.stack([np.ascontiguousarray(res.results[2 * i]["y_p"].T) for i in range(xp.shape[0])], axis=0)
    return (y_p.astype(np.float32), y_s.astype(np.float32))
```

```python
import contextlib
import math
import numpy as np
import concourse.bass as bass
import concourse.mybir as mybir
from concourse.bass_utils import run_bass_kernel_spmd

F32 = mybir.dt.float32
BF16 = mybir.dt.bfloat16
AF = mybir.ActivationFunctionType
ALU = mybir.AluOpType

D = 1024
KC = 8
DEPTH = 4
DFF = 2816
NPAIR = 22
H = 8
NORM_EPS = 1e-6
SUBLN_EPS = 1e-5
ROPE_THETA = 10000.0
TT = 512
GELU_C = 0.044715


class Buf:
    __slots__ = ("name", "w", "r", "excl", "merge", "dsem")

    def __init__(self, name, excl=False, merge=False):
        self.name = name
        self.w = {}
        self.r = {}
        self.excl = excl
        self.merge = merge
        self.dsem = {}


class Trk:
    EPOCH = 30000

    def __init__(self, nc, es):
        self.nc = nc
        self.es = es
        self.eng = {"pe": nc.tensor, "act": nc.scalar, "dve": nc.vector, "pool": nc.gpsimd, "sp": nc.sync}
        self.semh = {}
        self.cnt = {e: 0 for e in self.eng}
        self.epoch = {e: 0 for e in self.eng}
        self.cur = {}
        self.waited = {e: {} for e in self.eng}
        self.dval = {}
        self.active = []
        self.dfree = {e: [] for e in self.eng}
        self.nd = 0
        self.nwait = 0
        self.nins = {e: 0 for e in self.eng}
        for e in self.eng:
            self._newsem(e)

    def _mk(self, name):
        self.semh[name] = self.es.enter_context(self.nc.semaphore(name))
        return name

    def _newsem(self, e):
        self.cur[e] = self._mk("q_%s_%d" % (e, self.epoch[e]))
        self.cnt[e] = 0

    def _deps(self, reads, writes):
        deps = {}
        reads = [b for b in reads if not b.merge]
        writes = [b for b in writes if not b.merge]
        for b in reads:
            for s, v in b.w.items():
                if deps.get(s, 0) < v:
                    deps[s] = v
            if b.excl:
                for s, v in b.r.items():
                    if deps.get(s, 0) < v:
                        deps[s] = v
        for b in writes:
            for s, v in b.w.items():
                if deps.get(s, 0) < v:
                    deps[s] = v
            for s, v in b.r.items():
                if deps.get(s, 0) < v:
                    deps[s] = v
        return deps

    def _wait(self, e, deps):
        engine = self.eng[e]
        wd = self.waited[e]
        own = self.cur[e]
        for s, v in deps.items():
            if e == "pe" and s == own:
                continue
            if wd.get(s, 0) >= v:
                continue
            engine.wait_ge(self.semh[s], v)
            wd[s] = v
            self.nwait += 1

    def _record(self, ev, reads, writes):
        s, v = ev
        for b in writes:
            if b.merge:
                continue
            b.w = {s: v}
            b.r = {}
        for b in reads:
            if b.merge:
                continue
            if b.excl:
                b.w = {s: v}
                b.r = {}
            elif b.r.get(s, 0) < v:
                b.r[s] = v

    def op(self, e, fn, reads=(), writes=()):
        if self.cnt[e] >= self.EPOCH:
            self.epoch[e] += 1
            self._newsem(e)
        self._wait(e, self._deps(reads, writes))
        ins = fn(self.eng[e])
        self.cnt[e] += 1
        ev = (self.cur[e], self.cnt[e])
        ins.then_inc(self.semh[ev[0]], 1)
        self._record(ev, reads, writes)
        self.nins[e] += 1
        return ins

    def dma(self, q, dst, dst_ap, src, src_ap, side, slow=False):
        if q not in side.dsem:
            if self.dfree[q]:
                side.dsem[q] = self.dfree[q].pop()
            else:
                side.dsem[q] = self._mk("d%s%d" % (q, self.nd))
                self.nd += 1
                self.dval[side.dsem[q]] = 0
            self.active.append((side, q))
        name = side.dsem[q]
        self._wait(q, self._deps((src,), (dst,)))
        if slow:
            ins = self.eng[q].dma_start(out=dst_ap, in_=src_ap, allow_slow_non_contiguous=True)
        else:
            ins = self.eng[q].dma_start(out=dst_ap, in_=src_ap)
        self.dval[name] += 16
        ev = (name, self.dval[name])
        ins.then_inc(self.semh[name], 16)
        self._record(ev, (src,), (dst,))
        self.nins[q] += 1
        return ins

    def barrier(self, bufs=(), release=True):
        deps = {}
        for e in self.eng:
            if self.cnt[e] > 0:
                deps[self.cur[e]] = self.cnt[e]
        for b, q in self.active:
            deps[b.dsem[q]] = self.dval[b.dsem[q]]
        for e in self.eng:
            self._wait(e, dict(deps))
        if release:
            for b, q in self.active:
                self.dfree[q].append(b.dsem.pop(q))
            self.active = []


class Lay:
    def __init__(self):
        self.off = {}
        self.n = 0

    def add(self, name, ncols):
        self.off[name] = (self.n, ncols)
        self.n += ncols


def make_layouts():
    wl = Lay()
    for j in range(2):
        wl.add("g%d_wu" % j, 8 * 8 * 128)
        wl.add("g%d_wv" % j, 8 * 1024)
        wl.add("g%d_ws" % j, 8 * 128)
        wl.add("g%d_wo" % j, 8 * 8 * 128)
    for j in range(2):
        wl.add("a%d_wqk" % j, 16 * 2 * 8 * 128)
        wl.add("a%d_wv" % j, 8 * 1024)
        wl.add("a%d_wo" % j, 8 * 8 * 128)
    for i in range(DEPTH):
        wl.add("f%d_win" % i, 44 * 8 * 128)
        wl.add("f%d_wout" % i, 8 * NPAIR * 128)
    pl = Lay()
    pl.add("nmix", DEPTH * 8)
    pl.add("nffn", DEPTH * 8)
    pl.add("nfin", 8)
    pl.add("vgain", 2 * 8)
    pl.add("convw", DEPTH * 44 * 4)
    pl.add("bsrep", 2 * 8 * 128)
    pl.add("subg", 2 * 128)
    pl.add("lam", 2 * 4 * 64)
    pl.add("ident", 128)
    return wl, pl


def pack_lhsT(w):
    K, M = w.shape
    return w.reshape(K // 128, 128, M // 128, 128).transpose(1, 2, 0, 3).reshape(128, -1)


def pack_rhs(w):
    K, N = w.shape
    return w.reshape(K // 128, 128, N).transpose(1, 0, 2).reshape(128, -1)


def pack_wout(w):
    K, N = w.shape
    return w.reshape(K // 128, 128, N // 128, 128).transpose(1, 2, 0, 3).reshape(128, -1)


def host_pack(inp, smax):
    wl, pl = make_layouts()
    wall = np.empty((128, wl.n), np.float32)
    pall = np.empty((128, pl.n), np.float32)

    def putw(name, arr):
        o, n = wl.off[name]
        assert arr.shape == (128, n), (name, arr.shape, n)
        wall[:, o:o + n] = arr

    def putp(name, arr):
        o, n = pl.off[name]
        assert arr.shape == (128, n), (name, arr.shape, n)
        pall[:, o:o + n] = arr

    for j in range(2):
        w = np.asarray(inp["gmlp_w_in"][j])
        putw("g%d_wu" % j, pack_lhsT(w[:, :1024]))
        putw("g%d_wv" % j, pack_rhs(w[:, 1024:]))
        putw("g%d_ws" % j, np.asarray(inp["gmlp_w_s"][j]).transpose(2, 0, 1).reshape(128, -1))
        putw("g%d_wo" % j, pack_wout(np.asarray(inp["gmlp_w_out"][j])))
    swap = np.arange(128)
    swap = (swap // 64) * 64 + ((swap % 64) + 32) % 64
    for j in range(2):
        w = np.asarray(inp["diff_w_qkv"][j])
        qk = w[:, :2048].reshape(1024, 16, 128)
        both = np.stack([qk, qk[:, :, swap]], axis=2)
        arr = both.reshape(8, 128, 16, 2, 128).transpose(1, 2, 3, 0, 4)
        putw("a%d_wqk" % j, arr.reshape(128, -1))
        putw("a%d_wv" % j, pack_rhs(w[:, 2048:]))
        putw("a%d_wo" % j, pack_wout(np.asarray(inp["diff_w_out"][j])))
    for i in range(DEPTH):
        w = np.asarray(inp["ffn_w_in"][i])
        wp = np.stack([w[:, :DFF].reshape(1024, NPAIR, 128), w[:, DFF:].reshape(1024, NPAIR, 128)], axis=2)
        putw("f%d_win" % i, pack_lhsT(wp.reshape(1024, 2 * DFF)))
        putw("f%d_wout" % i, pack_wout(np.asarray(inp["ffn_w_out"][i])))

    def pervec(v):
        v = np.asarray(v).reshape(-1, 8, 128)
        return v.transpose(2, 0, 1).reshape(128, -1)

    putp("nmix", pervec(inp["norm_mix"]))
    putp("nffn", pervec(inp["norm_ffn"]))
    putp("nfin", pervec(inp["norm_final"]))
    putp("vgain", pervec(inp["gmlp_v_gain"]))
    cw = np.asarray(inp["ffn_conv_w"])
    cb = np.asarray(inp["ffn_conv_b"])
    c4 = np.concatenate([cw, cb[:, None, :]], axis=1)
    c4 = c4.reshape(DEPTH, 4, 2, NPAIR, 128)
    putp("convw", c4.transpose(4, 0, 3, 2, 1).reshape(128, -1))
    bs = np.asarray(inp["gmlp_b_s"])
    bsr = np.broadcast_to(bs[None, :, :, :], (128, 2, 8, 128))
    putp("bsrep", np.ascontiguousarray(bsr).reshape(128, -1))
    sg = np.asarray(inp["diff_subln_g"])
    putp("subg", np.ascontiguousarray(np.broadcast_to(sg[None], (128, 2, 128))).reshape(128, -1))
    lam = np.stack([np.asarray(inp["diff_lam_q1"]), np.asarray(inp["diff_lam_k1"]),
                    np.asarray(inp["diff_lam_q2"]), np.asarray(inp["diff_lam_k2"])], axis=1)
    putp("lam", np.ascontiguousarray(np.broadcast_to(lam[None], (128, 2, 4, 64))).reshape(128, -1))
    putp("ident", np.eye(128, dtype=np.float32))
    pos = np.arange(smax, dtype=np.float32)
    inv_freq = (ROPE_THETA ** (-np.arange(0, 64, 2, dtype=np.float32) / np.float32(64))).astype(np.float32)
    ang = (pos[:, None] * inv_freq[None, :]).astype(np.float32)
    dd = np.arange(128) % 64
    cosT = np.cos(ang).astype(np.float32).T[dd % 32, :]
    sinT = np.sin(ang).astype(np.float32).T[dd % 32, :]
    sgn = np.where(dd < 32, -1.0, 1.0).astype(np.float32)[:, None]
    return wall, pall, np.ascontiguousarray(cosT), np.ascontiguousarray(sinT * sgn)


def build_program(seqs):
    wl, pl = make_layouts()
    smax = max(s for _, s in seqs)
    nc = bass.Bass("TRN2", target_bir_lowering=False)
    wall = nc.dram_tensor("wall", [128, wl.n], F32, kind="ExternalInput").ap()
    pall_d = nc.dram_tensor("pall", [128, pl.n], F32, kind="ExternalInput").ap()
    cos_d = nc.dram_tensor("cosT", [128, smax], F32, kind="ExternalInput").ap()
    sin_d = nc.dram_tensor("sinT", [128, smax], F32, kind="ExternalInput").ap()
    wbf = nc.dram_tensor("wbf", [128, wl.n], BF16, kind="Internal").ap()
    Bwall = Buf("wall", merge=True)
    Bwbf = Buf("wbf", merge=True)
    Bconst = Buf("constd", merge=True)
    SQ = {}
    for tag, S in seqs:
        d = {}
        d["S"] = S
        d["xin"] = nc.dram_tensor("x_" + tag, [D, S], F32, kind="ExternalInput").ap()
        d["yout"] = nc.dram_tensor("y_" + tag, [D, S], F32, kind="ExternalOutput").ap()
        d["xa"] = nc.dram_tensor("xa_" + tag, [D, S], F32, kind="Internal").ap()
        d["xb"] = nc.dram_tensor("xb_" + tag, [D, S], F32, kind="Internal").ap()
        d["qk"] = nc.dram_tensor("qk_" + tag, [16, 128, S], BF16, kind="Internal").ap()
        d["vs"] = nc.dram_tensor("vs_" + tag, [H, S, 128], BF16, kind="Internal").ap()
        d["ot"] = nc.dram_tensor("ot_" + tag, [H, 128, S], BF16, kind="Internal").ap()
        for k in ("xin", "yout", "xa", "xb", "qk", "vs", "ot"):
            d["B" + k] = Buf(k + "_" + tag, merge=True)
        SQ[tag] = d

    uid = [0]

    def sbt(name, shape, dt):
        uid[0] += 1
        return nc.sbuf_tensor("%s_u%d" % (name, uid[0]), shape, dt)

    def pst_(name, shape, dt):
        uid[0] += 1
        return nc.psum_tensor("%s_u%d" % (name, uid[0]), shape, dt)

    es = contextlib.ExitStack()
    with es:
        T = Trk(nc, es)

        def wslice(name, a=0, n=None):
            o, tot = wl.off[name]
            if n is None:
                n = tot - a
            return wbf[:, o + a:o + a + n]

        pall = es.enter_context(sbt("pall_sb", [128, pl.n], F32))
        Bpall = Buf("pall")
        T.dma("sp", Bpall, pall[:], Bconst, pall_d[:, :], Bpall)

        def pcol(name, a, n=1):
            o, _ = pl.off[name]
            return pall[:, o + a:o + a + n]

        ident = es.enter_context(sbt("ident", [128, 128], BF16))
        ones32 = es.enter_context(sbt("ones32", [128, 128], F32))
        lamt = es.enter_context(sbt("lamt", [128, 16], F32))
        subgs = es.enter_context(sbt("subgs", [128, 2, 128], F32))
        Bmisc = Buf("misc")
        T.op("dve", lambda e: e.tensor_copy(out=ident[:], in_=pcol("ident", 0, 128)), reads=[Bpall], writes=[Bmisc])
        T.op("pool", lambda e: e.memset(ones32[:], 1.0), writes=[Bmisc])
        mhalf = es.enter_context(sbt("mhalf", [128, 4], F32))
        T.op("pool", lambda e: e.memset(mhalf[:], -0.5), writes=[Bmisc])
        lam_inits = []
        for j in range(2):
            li = 0.8 - 0.6 * math.exp(-0.3 * (2 * j + 1))
            lam_inits.append(li)
            lo = pl.off["lam"][0] + j * 256
            for w_ in range(2):
                T.op("dve", lambda e: e.tensor_tensor(out=subgs[:, 0, 0:64], in0=pall[:, lo + w_ * 128:lo + w_ * 128 + 64],
                                                      in1=pall[:, lo + w_ * 128 + 64:lo + w_ * 128 + 128], op=ALU.mult),
                     reads=[Bpall, Bmisc], writes=[Bmisc])
                T.op("dve", lambda e: e.reduce_sum(out=lamt[:, 8 + w_:9 + w_], in_=subgs[:, 0, 0:64], axis=mybir.AxisListType.X),
                     reads=[Bmisc], writes=[Bmisc])
            T.op("act", lambda e: e.activation(out=lamt[:, 10:12], in_=lamt[:, 8:10], func=AF.Exp), reads=[Bmisc], writes=[Bmisc])
            T.op("dve", lambda e: e.scalar_tensor_tensor(out=lamt[:, j:j + 1], in0=lamt[:, 11:12], scalar=-li, in1=lamt[:, 10:11],
                                                         op0=ALU.add, op1=ALU.subtract), reads=[Bmisc], writes=[Bmisc])
        for j in range(2):
            T.op("dve", lambda e: e.tensor_scalar(out=subgs[:, j, :], in0=pcol("subg", j * 128, 128), scalar1=1.0 - lam_inits[j],
                                                  scalar2=None, op0=ALU.mult), reads=[Bpall, Bmisc], writes=[Bmisc])
        T.barrier([Bpall])

        def phase_cast():
            CW = 4096
            with contextlib.ExitStack() as ph:
                NS = 3
                f = [ph.enter_context(sbt("cf%d" % i, [128, CW], F32)) for i in range(NS)]
                b = [ph.enter_context(sbt("cb%d" % i, [128, CW], BF16)) for i in range(NS)]
                Bf = [Buf("cf%d" % i) for i in range(NS)]
                Bb = [Buf("cb%d" % i) for i in range(NS)]
                nch = (wl.n + CW - 1) // CW
                for c in range(nch):
                    s = c % NS
                    a = c * CW
                    n = min(CW, wl.n - a)
                    T.dma("sp", Bf[s], f[s][:, 0:n], Bwall, wall[:, a:a + n], Bf[s])
                    if c % 2 == 0:
                        T.op("dve", lambda e: e.tensor_copy(out=b[s][:, 0:n], in_=f[s][:, 0:n]), reads=[Bf[s]], writes=[Bb[s]])
                    else:
                        T.op("act", lambda e: e.activation(out=b[s][:, 0:n], in_=f[s][:, 0:n], func=AF.Copy), reads=[Bf[s]], writes=[Bb[s]])
                    T.dma("pool", Bwbf, wbf[:, a:a + n], Bb[s], b[s][:, 0:n], Bb[s])
                T.barrier(Bf + Bb)

        phase_cast()

        def xT(ap, t0, n):
            return ap.rearrange("(kc p) t -> p kc t", p=128)[:, :, t0:t0 + n]

        class NormCtx:
            def __init__(self, ph, nslot=2):
                self.ns = nslot
                self.xin = [ph.enter_context(sbt("n_xin%d" % i, [128, KC, TT], F32)) for i in range(nslot)]
                self.hT = [ph.enter_context(sbt("n_hT%d" % i, [128, KC, TT], BF16)) for i in range(nslot)]
                self.sq = ph.enter_context(sbt("n_sq", [128, 2, TT], F32))
                self.rs = ph.enter_context(sbt("n_rs", [128, TT], F32))
                self.ps = ph.enter_context(pst_("n_ps", [128, TT], F32))
                self.Bxin = [Buf("n_xin%d" % i) for i in range(nslot)]
                self.BhT = [Buf("n_hT%d" % i) for i in range(nslot)]
                self.Bsq = [Buf("n_sq0"), Buf("n_sq1")]
                self.Brs = Buf("n_rs")
                self.Bps = Buf("n_ps", excl=True)

            def bufs(self):
                return self.Bxin + self.BhT + self.Bsq + [self.Brs]

            def emit(self, slot, srcB, src_ap, t0, n, gname, gidx, out_f32=None):
                xin, hT = self.xin[slot], self.hT[slot]
                Bxin, BhT = self.Bxin[slot], self.BhT[slot]
                T.dma("sp", Bxin, xin[:, :, 0:n], srcB, xT(src_ap, t0, n), Bxin)
                for kc in range(KC):
                    T.op("act", lambda e: e.activation(out=self.sq[:, kc % 2, 0:n], in_=xin[:, kc, 0:n], func=AF.Square),
                         reads=[Bxin], writes=[self.Bsq[kc % 2]])
                    T.op("pe", lambda e: e.matmul(self.ps[:, 0:n], ones32[:], self.sq[:, kc % 2, 0:n], start=(kc == 0), stop=(kc == KC - 1)),
                         reads=[self.Bsq[kc % 2], Bmisc], writes=[self.Bps])
                T.op("act", lambda e: e.activation(out=self.rs[:, 0:n], in_=self.ps[:, 0:n], func=AF.Ln, scale=1.0 / D, bias=NORM_EPS),
                     reads=[self.Bps], writes=[self.Brs])
                T.op("act", lambda e: e.activation(out=self.rs[:, 0:n], in_=self.rs[:, 0:n], func=AF.Exp, scale=-0.5),
                     reads=[self.Brs], writes=[self.Brs])
                for kc in range(KC):
                    dst = hT[:, kc, 0:n] if out_f32 is None else out_f32[:, kc, 0:n]
                    T.op("dve", lambda e: e.scalar_tensor_tensor(out=dst, in0=xin[:, kc, 0:n], scalar=pcol(gname, gidx * 8 + kc),
                                                                 in1=self.rs[:, 0:n], op0=ALU.mult, op1=ALU.mult),
                         reads=[Bxin, self.Brs, Bpall], writes=[BhT])

        def phase_ffn(sq, li, srcB, src, dstB, dst):
            S = sq["S"]
            J = S // TT
            with contextlib.ExitStack() as ph:
                N = NormCtx(ph)
                NW = 3
                win = [ph.enter_context(sbt("f_win%d" % i, [128, 2, KC, 128], BF16)) for i in range(NW)]
                Bwin = [Buf("f_win%d" % i) for i in range(NW)]
                wout = ph.enter_context(sbt("f_wout", [128, KC, NPAIR, 128], BF16))
                Bwout = [Buf("f_wout%d" % i) for i in range(KC)]
                NA = 3
                aext = [ph.enter_context(sbt("f_aext%d" % i, [128, 2, TT + 2], F32)) for i in range(NA)]
                Baext = [Buf("f_aext%d" % i) for i in range(NA)]
                ct = [ph.enter_context(sbt("f_ct%d" % i, [128, 2, TT], F32)) for i in range(NA)]
                Bct = [Buf("f_ct%d" % i) for i in range(NA)]
                sg = [ph.enter_context(sbt("f_sg%d" % i, [128, TT], F32)) for i in range(NA)]
                Bsg = [Buf("f_sg%d" % i) for i in range(NA)]
                carry = ph.enter_context(sbt("f_carry", [128, 2 * NPAIR, 2], F32))
                Bcarry = [Buf("f_carry%d" % i) for i in range(NPAIR)]
                yT = ph.enter_context(sbt("f_yT", [128, NPAIR, TT], BF16))
                ByT = [Buf("f_yT%d" % i) for i in range(NPAIR)]
                xres = ph.enter_context(sbt("f_xres", [128, KC, TT], F32))
                Bxres = Buf("f_xres")
                psa = [ph.enter_context(pst_("f_psa%d" % i, [128, 2, TT], F32)) for i in range(2)]
                Bpsa = [[Buf("f_psa%d_%d" % (i, h), excl=True) for h in range(2)] for i in range(2)]
                pso = [ph.enter_context(pst_("f_pso%d" % i, [128, TT], F32)) for i in range(2)]
                Bpso = [Buf("f_pso%d" % i, excl=True) for i in range(2)]
                cwo = pl.off["convw"][0] + li * 44 * 4

                T.op("pool", lambda e: e.memset(carry[:], 0.0), writes=Bcarry)
                ctr = [0, 0, 0]
                pend = []
                wout_loaded = [False]

                def win_stage(j, slot):
                    virt = (j == J)
                    hT, BhT = N.hT[slot], N.BhT[slot]
                    n = 1 if virt else TT
                    prev = None
                    for pr in range(NPAIR + 1):
                        if pr == 3:
                            while pend:
                                pend.pop()()
                        if pr == NPAIR // 2 and j + 1 < J:
                            N.emit((j + 1) % 2, srcB, src, (j + 1) * TT, TT, "nffn", li)
                        if pr < NPAIR:
                            ws = ctr[0] % NW
                            ctr[0] += 1
                            a = ctr[1] % NA
                            pb = ctr[1] % 2
                            ctr[1] += 1
                            if not virt:
                                T.dma("sp", Bwin[ws], win[ws][:].rearrange("p a b c -> p (a b c)"), Bwbf,
                                      wslice("f%d_win" % li, pr * 2048, 2048), Bwin[ws])
                                for hf in range(2):
                                    for kc in range(KC):
                                        T.op("pe", lambda e: e.matmul(psa[pb][:, hf, :], win[ws][:, hf, kc, :], hT[:, kc, :],
                                                                      start=(kc == 0), stop=(kc == KC - 1)),
                                             reads=[Bwin[ws], BhT], writes=[Bpsa[pb][hf]])
                            T.op("pool", lambda e: e.tensor_copy(out=aext[a][:, :, 0:2], in_=carry[:, 2 * pr:2 * pr + 2, :]),
                                 reads=[Bcarry[pr]], writes=[Baext[a]])
                            if virt:
                                T.op("pool", lambda e: e.memset(aext[a][:, :, 2:TT + 2], 0.0), writes=[Baext[a]])
                            else:
                                T.op("act", lambda e: e.activation(out=aext[a][:, :, 2:TT + 2], in_=psa[pb][:, :, :], func=AF.Copy),
                                     reads=Bpsa[pb], writes=[Baext[a]])
                                T.op("pool", lambda e: e.tensor_copy(out=carry[:, 2 * pr:2 * pr + 2, :], in_=aext[a][:, :, TT:TT + 2]),
                                     reads=[Baext[a]], writes=[Bcarry[pr]])
                            for hf in range(2):
                                co = cwo + (pr * 2 + hf) * 4
                                T.op("act", lambda e: e.activation(out=ct[a][:, hf, 0:n], in_=aext[a][:, hf, 0:n], func=AF.Identity,
                                                                   scale=pall[:, co:co + 1], bias=pall[:, co + 3:co + 4]),
                                     reads=[Baext[a], Bpall], writes=[Bct[a]])
                        if prev is not None:
                            pa, ppr = prev
                            T.op("act", lambda e: e.activation(out=sg[pa][:, 0:n], in_=ct[pa][:, 0, 0:n], func=AF.Silu),
                                 reads=[Bct[pa]], writes=[Bsg[pa]])
                        if pr < NPAIR:
                            for hf in range(2):
                                co = cwo + (pr * 2 + hf) * 4
                                for t in (1, 2):
                                    T.op("dve", lambda e: e.scalar_tensor_tensor(out=ct[a][:, hf, 0:n], in0=aext[a][:, hf, t:t + n],
                                                                                 scalar=pall[:, co + t:co + t + 1], in1=ct[a][:, hf, 0:n],
                                                                                 op0=ALU.mult, op1=ALU.add),
                                         reads=[Baext[a], Bct[a], Bpall], writes=[Bct[a]])
                        if prev is not None:
                            pa, ppr = prev
                            T.op("dve", lambda e: e.tensor_tensor(out=yT[:, ppr, 0:n], in0=sg[pa][:, 0:n], in1=ct[pa][:, 1, 0:n], op=ALU.mult),
                                 reads=[Bsg[pa], Bct[pa]], writes=[ByT[ppr]])
                        prev = (a, pr) if pr < NPAIR else None

                def wout_stage(j):
                    virt = (j == J)
                    lo = 1 if j == 0 else 0
                    hi = 1 if virt else TT
                    n = hi - lo
                    tok0 = j * TT - 1 + lo
                    T.dma("sp", Bxres, xres[:, :, 0:n], srcB, xT(src, tok0, n), Bxres, slow=(n == 1))
                    for dc in range(KC):
                        o = dc % 2
                        if not wout_loaded[0]:
                            T.dma("sp", Bwout[dc], wout[:, dc, :, :].rearrange("p a b -> p (a b)"), Bwbf,
                                  wslice("f%d_wout" % li, dc * NPAIR * 128, NPAIR * 128), Bwout[dc])
                        for pr in range(NPAIR):
                            T.op("pe", lambda e: e.matmul(pso[o][:, 0:n], wout[:, dc, pr, :], yT[:, pr, lo:hi],
                                                          start=(pr == 0), stop=(pr == NPAIR - 1)),
                                 reads=[Bwout[dc], ByT[pr]], writes=[Bpso[o]])
                        T.op("dve", lambda e: e.tensor_tensor(out=xres[:, dc, 0:n], in0=pso[o][:, 0:n], in1=xres[:, dc, 0:n], op=ALU.add),
                             reads=[Bpso[o], Bxres], writes=[Bxres])
                    wout_loaded[0] = True
                    pend.append(lambda: T.dma("sp", dstB, xT(dst, tok0, n), Bxres, xres[:, :, 0:n], Bxres, slow=(n == 1)))

                N.emit(0, srcB, src, 0, TT, "nffn", li)
                for j in range(J + 1):
                    win_stage(j, j % 2)
                    wout_stage(j)
                while pend:
                    pend.pop()()
                T.barrier()

        def phase_gmlp(sq, li, gj, srcB, src, dstB, dst):
            S = sq["S"]
            J = S // TT
            with contextlib.ExitStack() as ph:
                N = NormCtx(ph)
                wu = ph.enter_context(sbt("g_wu", [128, 8, KC, 128], BF16))
                wv = ph.enter_context(sbt("g_wv", [128, KC, 1024], BF16))
                ws = ph.enter_context(sbt("g_ws", [128, 8, 128], BF16))
                wo = ph.enter_context(sbt("g_wo", [128, 8, KC, 128], BF16))
                Bw = Buf("g_w")
                Bw2 = [Buf("g_w2_%d" % i) for i in range(3)]
                uT = ph.enter_context(sbt("g_uT", [128, 8, TT], F32))
                BuT = [Buf("g_uT%d" % i) for i in range(8)]
                vg = [ph.enter_context(sbt("g_vg%d" % i, [128, 1024], F32)) for i in range(2)]
                Bvg = [Buf("g_vg%d" % i) for i in range(2)]
                vsq = ph.enter_context(sbt("g_vsq", [128, 1024], BF16))
                Bvsq = Buf("g_vsq")
                vn = ph.enter_context(sbt("g_vn", [128, 4, 1024], BF16))
                Bvn = [Buf("g_vn%d" % i) for i in range(4)]
                st = [ph.enter_context(sbt("g_st%d" % i, [128, 8], F32)) for i in range(2)]
                Bst = [Buf("g_st%d" % i) for i in range(2)]
                v2 = [ph.enter_context(sbt("g_v2_%d" % i, [128, TT], F32)) for i in range(2)]
                Bv2 = [Buf("g_v2_%d" % i) for i in range(2)]
                yT = ph.enter_context(sbt("g_yT", [128, 8, TT], BF16))
                ByT = [Buf("g_yT%d" % i) for i in range(8)]
                psu = [ph.enter_context(pst_("g_psu%d" % i, [128, TT], F32)) for i in range(2)]
                Bpsu = [Buf("g_psu%d" % i, excl=True) for i in range(2)]
                psv = [ph.enter_context(pst_("g_psv%d" % i, [128, TT], F32)) for i in range(2)]
                Bpsv = [Buf("g_psv%d" % i, excl=True) for i in range(2)]
                pss = [ph.enter_context(pst_("g_pss%d" % i, [128, TT], F32)) for i in range(2)]
                Bpss = [Buf("g_pss%d" % i, excl=True) for i in range(2)]
                T.dma("sp", Bw, wu[:].rearrange("p a b c -> p (a b c)"), Bwbf, wslice("g%d_wu" % gj), Bw)
                T.dma("sp", Bw2[0], wv[:].rearrange("p a b -> p (a b)"), Bwbf, wslice("g%d_wv" % gj), Bw2[0])
                T.dma("sp", Bw2[1], ws[:].rearrange("p a b -> p (a b)"), Bwbf, wslice("g%d_ws" % gj), Bw2[1])
                T.dma("sp", Bw2[2], wo[:].rearrange("p a b c -> p (a b c)"), Bwbf, wslice("g%d_wo" % gj), Bw2[2])
                bso = pl.off["bsrep"][0] + gj * 8 * 128
                for j in range(J):
                    slot = j % 2
                    N.emit(slot, srcB, src, j * TT, TT, "nmix", li)
                    hT, BhT, xin, Bxin = N.hT[slot], N.BhT[slot], N.xin[slot], N.Bxin[slot]
                    for m in range(8):
                        o = m % 2
                        for kc in range(KC):
                            T.op("pe", lambda e: e.matmul(psu[o][:], wu[:, m, kc, :], hT[:, kc, :], start=(kc == 0), stop=(kc == KC - 1)),
                                 reads=[Bw, BhT], writes=[Bpsu[o]])
                        T.op("act", lambda e: e.activation(out=uT[:, m, :], in_=psu[o][:], func=AF.Gelu_apprx_tanh),
                             reads=[Bpsu[o]], writes=[BuT[m]])
                    for sub in range(4):
                        a = sub % 2
                        for hf in range(2):
                            for kc in range(KC):
                                T.op("pe", lambda e: e.matmul(psv[hf][:], hT[:, kc, sub * 128:(sub + 1) * 128], wv[:, kc, hf * 512:(hf + 1) * 512],
                                                              start=(kc == 0), stop=(kc == KC - 1)),
                                     reads=[Bw2[0], BhT], writes=[Bpsv[hf]])
                            T.op("act", lambda e: e.activation(out=vg[a][:, hf * 512:(hf + 1) * 512], in_=psv[hf][:], func=AF.Gelu_apprx_tanh),
                                 reads=[Bpsv[hf]], writes=[Bvg[a]])
                        T.op("act", lambda e: e.activation(out=vsq[:], in_=vg[a][:], func=AF.Square, accum_out=st[a][:, 0:1]),
                             reads=[Bvg[a]], writes=[Bvsq, Bst[a]])
                        T.op("act", lambda e: e.activation(out=st[a][:, 1:2], in_=st[a][:, 0:1], func=AF.Ln, scale=1.0 / 1024, bias=NORM_EPS),
                             reads=[Bst[a]], writes=[Bst[a]])
                        T.op("act", lambda e: e.activation(out=st[a][:, 2:3], in_=st[a][:, 1:2], func=AF.Exp, scale=-0.5),
                             reads=[Bst[a]], writes=[Bst[a]])
                        T.op("dve", lambda e: e.tensor_scalar(out=vn[:, sub, :], in0=vg[a][:], scalar1=st[a][:, 2:3], scalar2=None, op0=ALU.mult),
                             reads=[Bvg[a], Bst[a]], writes=[Bvn[sub]])
                    for g in range(8):
                        o = g % 2
                        for sub in range(4):
                            T.op("pe", lambda e: e.matmul(pss[o][:, sub * 128:(sub + 1) * 128], vn[:, sub, g * 128:(g + 1) * 128], ws[:, g, :],
                                                          start=True, stop=True, skip_group_check=True),
                                 reads=[Bw2[1], Bvn[sub]], writes=[Bpss[o]])
                        for sub in range(4):
                            T.op("dve", lambda e: e.scalar_tensor_tensor(out=v2[o][:, sub * 128:(sub + 1) * 128], in0=pss[o][:, sub * 128:(sub + 1) * 128],
                                                                         scalar=pcol("vgain", gj * 8 + g),
                                                                         in1=pall[:, bso + g * 128:bso + (g + 1) * 128], op0=ALU.mult, op1=ALU.add),
                                 reads=[Bpss[o], Bpall], writes=[Bv2[o]])
                        T.op("pool", lambda e: e.tensor_tensor(out=yT[:, g, :], in0=uT[:, g, :], in1=v2[o][:], op=ALU.mult),
                             reads=[BuT[g], Bv2[o]], writes=[ByT[g]])
                    for dc in range(KC):
                        o = dc % 2
                        for g in range(8):
                            T.op("pe", lambda e: e.matmul(psu[o][:], wo[:, dc, g, :], yT[:, g, :], start=(g == 0), stop=(g == 7)),
                                 reads=[Bw2[2], ByT[g]], writes=[Bpsu[o]])
                        T.op("dve", lambda e: e.tensor_tensor(out=xin[:, dc, :], in0=psu[o][:], in1=xin[:, dc, :], op=ALU.add),
                             reads=[Bpsu[o], Bxin], writes=[Bxin])
                    T.dma("pool", dstB, xT(dst, j * TT, TT), Bxin, xin[:], Bxin)
                T.barrier()

        def phase_qkv(sq, li, aj, srcB, src):
            S = sq["S"]
            J = S // TT
            with contextlib.ExitStack() as ph:
                N = NormCtx(ph)
                NWS = 3
                wqk = [ph.enter_context(sbt("a_wqk%d" % i, [128, 2, KC, 128], BF16)) for i in range(NWS)]
                Bwqk = [Buf("a_wqk%d" % i) for i in range(NWS)]
                wv = ph.enter_context(sbt("a_wv", [128, KC, 1024], BF16))
                Bwv = Buf("a_wv")
                cs = [ph.enter_context(sbt("a_cs%d" % i, [128, 2, TT], F32)) for i in range(2)]
                Bcs = [Buf("a_cs%d" % i) for i in range(2)]
                t1 = [ph.enter_context(sbt("a_t1_%d" % i, [128, 2, TT], F32)) for i in range(2)]
                Bt1 = [Buf("a_t1_%d" % i) for i in range(2)]
                qo = [ph.enter_context(sbt("a_qo%d" % i, [128, TT], BF16)) for i in range(3)]
                Bqo = [Buf("a_qo%d" % i) for i in range(3)]
                vt = [ph.enter_context(sbt("a_vt%d" % i, [128, 1024], BF16)) for i in range(2)]
                Bvt = [Buf("a_vt%d" % i) for i in range(2)]
                psz = [ph.enter_context(pst_("a_psz%d" % i, [128, 2, TT], F32)) for i in range(2)]
                Bpsz = [[Buf("a_psz%d_%d" % (i, v), excl=True) for v in range(2)] for i in range(2)]
                psv = [ph.enter_context(pst_("a_psv%d" % i, [128, TT], F32)) for i in range(2)]
                Bpsv = [Buf("a_psv%d" % i, excl=True) for i in range(2)]
                T.dma("sp", Bwv, wv[:].rearrange("p a b -> p (a b)"), Bwbf, wslice("a%d_wv" % aj), Bwv)
                c0 = 0
                c1 = 0
                for j in range(J):
                    slot = j % 2
                    t0 = j * TT
                    N.emit(slot, srcB, src, t0, TT, "nmix", li)
                    hT, BhT = N.hT[slot], N.BhT[slot]
                    T.dma("sp", Bcs[slot], cs[slot][:, 0, :], Bconst, cos_d[:, t0:t0 + TT], Bcs[slot])
                    T.dma("sp", Bcs[slot], cs[slot][:, 1, :], Bconst, sin_d[:, t0:t0 + TT], Bcs[slot])
                    for m in range(16):
                        ws_ = c0 % NWS
                        a = c0 % 2
                        q = c0 % 3
                        c0 += 1
                        T.dma("sp", Bwqk[ws_], wqk[ws_][:].rearrange("p a b c -> p (a b c)"), Bwbf,
                              wslice("a%d_wqk" % aj, m * 2048, 2048), Bwqk[ws_])
                        for var in range(2):
                            for kc in range(KC):
                                T.op("pe", lambda e: e.matmul(psz[a][:, var, :], wqk[ws_][:, var, kc, :], hT[:, kc, :],
                                                              start=(kc == 0), stop=(kc == KC - 1)),
                                     reads=[Bwqk[ws_], BhT], writes=[Bpsz[a][var]])
                        for var in range(2):
                            T.op("dve", lambda e: e.tensor_tensor(out=t1[a][:, var, :], in0=psz[a][:, var, :], in1=cs[slot][:, var, :], op=ALU.mult),
                                 reads=[Bpsz[a][var], Bcs[slot]], writes=[Bt1[a]])
                        T.op("pool", lambda e: e.tensor_tensor(out=qo[q][:], in0=t1[a][:, 0, :], in1=t1[a][:, 1, :], op=ALU.add),
                             reads=[Bt1[a]], writes=[Bqo[q]])
                        T.dma("pool", sq["Bqk"], sq["qk"][m, :, t0:t0 + TT], Bqo[q], qo[q][:], Bqo[q])
                    for sub in range(4):
                        v_ = c1 % 2
                        c1 += 1
                        for hf in range(2):
                            for kc in range(KC):
                                T.op("pe", lambda e: e.matmul(psv[hf][:], hT[:, kc, sub * 128:(sub + 1) * 128], wv[:, kc, hf * 512:(hf + 1) * 512],
                                                              start=(kc == 0), stop=(kc == KC - 1)),
                                     reads=[Bwv, BhT], writes=[Bpsv[hf]])
                            T.op("act", lambda e: e.activation(out=vt[v_][:, hf * 512:(hf + 1) * 512], in_=psv[hf][:], func=AF.Copy),
                                 reads=[Bpsv[hf]], writes=[Bvt[v_]])
                        tk = t0 + sub * 128
                        T.dma("pool", sq["Bvs"], sq["vs"][:, tk:tk + 128, :].rearrange("h t e -> t h e"),
                              Bvt[v_], vt[v_][:].rearrange("p (h e) -> p h e", h=H), Bvt[v_])
                T.barrier(N.bufs() + Bwqk + [Bwv] + Bcs + Bqo + Bvt)

        def phase_attn(sq, li, aj):
            S = sq["S"]
            NQ = S // TT
            NK = S // 128
            scale = 64 ** -0.5
            with contextlib.ExitStack() as ph:
                kT = [ph.enter_context(sbt("b_kT%d" % i, [128, S], BF16)) for i in range(2)]
                qT = [ph.enter_context(sbt("b_qT%d" % i, [128, S], BF16)) for i in range(2)]
                vh = [ph.enter_context(sbt("b_vh%d" % i, [128, NK, 130], BF16)) for i in range(2)]
                BkT = [Buf("b_kT%d" % i) for i in range(2)]
                BqT = [Buf("b_qT%d" % i) for i in range(2)]
                Bvh = [Buf("b_vh%d" % i) for i in range(2)]
                NP_ = 3
                pT = [ph.enter_context(sbt("b_pT%d" % i, [128, 2, TT], BF16)) for i in range(NP_)]
                BpT = [Buf("b_pT%d" % i) for i in range(NP_)]
                accs = ph.enter_context(sbt("b_accs", [128, 8, 129], F32))
                Baccs = Buf("b_accs")
                sm = ph.enter_context(sbt("b_sm", [128, 32], F32))
                Bsm = Buf("b_sm")
                o1 = [ph.enter_context(sbt("b_o1_%d" % i, [128, 128], F32)) for i in range(2)]
                o2 = [ph.enter_context(sbt("b_o2_%d" % i, [128, 128], F32)) for i in range(2)]
                ob = [ph.enter_context(sbt("b_ob_%d" % i, [128, 128], BF16)) for i in range(4)]
                Bob = [Buf("b_ob%d" % i) for i in range(4)]
                Bo = [Buf("b_o%d" % i) for i in range(2)]
                o1a = ph.enter_context(sbt("b_o1a", [128, 4, 128], F32))
                o2a = ph.enter_context(sbt("b_o2a", [128, 4, 128], F32))
                Bo1a = Buf("b_o1a")
                Bo2a = Buf("b_o2a")
                sm2 = ph.enter_context(sbt("b_sm2", [128, 8], F32))
                Bsm2 = Buf("b_sm2")
                oT = [ph.enter_context(sbt("b_oT%d" % i, [128, TT], BF16)) for i in range(2)]
                BoT = [Buf("b_oT%d" % i) for i in range(2)]
                pst = [ph.enter_context(pst_("b_pst%d" % i, [128, 2, TT], F32)) for i in range(2)]
                Bpst = [Buf("b_pst%d" % i, excl=True) for i in range(2)]
                pacc = ph.enter_context(pst_("b_pacc", [128, 3, TT], F32))
                Bpacc = [Buf("b_pacc%d" % i, excl=True) for i in range(3)]
                ptr = ph.enter_context(pst_("b_ptr", [128, TT], BF16))
                Bptr = Buf("b_ptr", excl=True)
                for i in range(2):
                    T.op("pool", lambda e: e.memset(vh[i][:, :, 128:130], 1.0), writes=[Bvh[i]])
                cc = 0
                eo = 0
                pending = []
                for h in range(H):
                    s = h % 2
                    T.dma("sp", BkT[s], kT[s][:], sq["Bqk"], sq["qk"][8 + h, :, :], BkT[s])
                    T.dma("sp", BqT[s], qT[s][:], sq["Bqk"], sq["qk"][h, :, :], BqT[s])
                    T.dma("sp", Bvh[s], vh[s][:, :, 0:128], sq["Bvs"], sq["vs"][h, :, :].rearrange("(kt p) e -> p kt e", p=128), Bvh[s])
                    units = [(qt_, kt_) for qt_ in range(NQ) for kt_ in range(NK)]

                    def emit_S(i):
                        qt_, kt_ = units[i]
                        a_ = (cc + i) % 2
                        for c in range(2):
                            T.op("pe", lambda e: e.matmul(pst[a_][:, c, :], kT[s][c * 64:(c + 1) * 64, kt_ * 128:(kt_ + 1) * 128],
                                                          qT[s][c * 64:(c + 1) * 64, qt_ * TT:(qt_ + 1) * TT], start=True, stop=True),
                                 reads=[BkT[s], BqT[s]], writes=[Bpst[a_]])

                    emit_S(0)
                    for ui in range(len(units)):
                        qt, kt = units[ui]
                        a = (cc + ui) % 2
                        p = (cc + ui) % NP_
                        if ui + 1 < len(units):
                            emit_S(ui + 1)
                        T.op("act", lambda e: e.activation(out=pT[p][:].rearrange("p a b -> p (a b)"),
                                                           in_=pst[a][:].rearrange("p a b -> p (a b)"), func=AF.Exp, scale=scale),
                             reads=[Bpst[a]], writes=[BpT[p]])
                        for c in range(2):
                            for qb in range(4):
                                ai = c * 4 + qb
                                bk, col = ai // 3, (ai % 3) * 129
                                T.op("pe", lambda e: e.matmul(pacc[:, bk, col:col + 129], pT[p][:, c, qb * 128:(qb + 1) * 128], vh[s][:, kt, 0:129],
                                                              start=(kt == 0 and ai % 3 == 0), stop=(kt == NK - 1), skip_group_check=True),
                                     reads=[BpT[p], Bvh[s]], writes=[Bpacc[bk]])
                        if kt == min(8, NK - 2) and pending:
                            for f_ in pending:
                                f_()
                            pending = []
                        if kt != NK - 1:
                            continue
                        for bk in range(3):
                            na = 3 if bk < 2 else 2
                            T.op("dve", lambda e: e.tensor_copy(out=accs[:, bk * 3:bk * 3 + na, :].rearrange("p a b -> p (a b)"), in_=pacc[:, bk, 0:na * 129]),
                                 reads=[Bpacc[bk]], writes=[Baccs])
                        T.op("dve", lambda e: e.reciprocal(out=sm[:, 0:8], in_=accs[:, :, 128]), reads=[Baccs], writes=[Bsm])
                        T.op("dve", lambda e: e.tensor_scalar(out=sm[:, 8:12], in0=sm[:, 4:8], scalar1=lamt[:, aj:aj + 1], scalar2=None, op0=ALU.mult),
                             reads=[Bsm, Bmisc], writes=[Bsm])
                        ot_s = eo % 2
                        eo += 1
                        for qb in range(4):
                            T.op("dve", lambda e: e.tensor_scalar(out=o2a[:, qb, :], in0=accs[:, 4 + qb, 0:128], scalar1=sm[:, 8 + qb:9 + qb], scalar2=None, op0=ALU.mult),
                                 reads=[Baccs, Bsm], writes=[Bo2a])
                            T.op("dve", lambda e: e.scalar_tensor_tensor(out=o1a[:, qb, :], in0=accs[:, qb, 0:128], scalar=sm[:, qb:qb + 1], in1=o2a[:, qb, :],
                                                                         op0=ALU.mult, op1=ALU.add),
                                 reads=[Baccs, Bsm, Bo2a], writes=[Bo1a])
                        T.op("dve", lambda e: e.tensor_tensor(out=o2a[:], in0=o1a[:], in1=o1a[:], op=ALU.mult), reads=[Bo1a, Bo2a], writes=[Bo2a])
                        T.op("dve", lambda e: e.reduce_sum(out=sm[:, 12:16], in_=o2a[:], axis=mybir.AxisListType.X), reads=[Bo2a, Bsm], writes=[Bsm])
                        T.op("pool", lambda e: e.tensor_scalar(out=sm2[:, 0:4], in0=sm[:, 12:16], scalar1=1.0 / 128, scalar2=SUBLN_EPS, op0=ALU.mult, op1=ALU.add),
                             reads=[Bsm], writes=[Bsm2])
                        T.op("pool", lambda e: e.tensor_tensor(out=sm2[:, 4:8], in0=sm2[:, 0:4], in1=mhalf[:, 0:4], op=ALU.pow),
                             reads=[Bsm2, Bmisc], writes=[Bsm2])
                        for qb in range(4):
                            T.op("dve", lambda e: e.scalar_tensor_tensor(out=ob[qb][:], in0=o1a[:, qb, :], scalar=sm2[:, 4 + qb:5 + qb], in1=subgs[:, aj, :],
                                                                         op0=ALU.mult, op1=ALU.mult),
                                 reads=[Bo1a, Bsm2, Bmisc], writes=[Bob[qb]])

                        def part2(qt=qt, ot_s=ot_s, h=h):
                            for qb in range(4):
                                T.op("pe", lambda e: e.transpose(ptr[:, qb * 128:(qb + 1) * 128], ob[qb][:], ident[:]),
                                     reads=[Bob[qb], Bmisc], writes=[Bptr])
                            T.op("dve", lambda e: e.tensor_copy(out=oT[ot_s][:], in_=ptr[:]), reads=[Bptr], writes=[BoT[ot_s]])
                            T.dma("pool", sq["Bot"], sq["ot"][h, :, qt * TT:(qt + 1) * TT], BoT[ot_s], oT[ot_s][:], BoT[ot_s])

                        pending.append(part2)
                    for f_ in pending:
                        f_()
                    pending = []
                    cc += len(units)
                T.barrier(BkT + BqT + Bvh + BoT)

        def phase_aout(sq, li, aj, srcB, src, dstB, dst):
            S = sq["S"]
            J = S // TT
            with contextlib.ExitStack() as ph:
                wo = ph.enter_context(sbt("c_wo", [128, 8, KC, 128], BF16))
                Bwo = Buf("c_wo")
                oT = [ph.enter_context(sbt("c_oT%d" % i, [128, H, TT], BF16)) for i in range(2)]
                BoT = [Buf("c_oT%d" % i) for i in range(2)]
                xr = [ph.enter_context(sbt("c_xr%d" % i, [128, KC, TT], F32)) for i in range(2)]
                Bxr = [Buf("c_xr%d" % i) for i in range(2)]
                xn = [ph.enter_context(sbt("c_xn%d" % i, [128, KC, TT], F32)) for i in range(2)]
                Bxn = [Buf("c_xn%d" % i) for i in range(2)]
                ps = [ph.enter_context(pst_("c_ps%d" % i, [128, TT], F32)) for i in range(2)]
                Bps = [Buf("c_ps%d" % i, excl=True) for i in range(2)]
                T.dma("sp", Bwo, wo[:].rearrange("p a b c -> p (a b c)"), Bwbf, wslice("a%d_wo" % aj), Bwo)
                for j in range(J):
                    s = j % 2
                    t0 = j * TT
                    T.dma("sp", BoT[s], oT[s][:], sq["Bot"], sq["ot"][:, :, t0:t0 + TT].rearrange("h p t -> p h t"), BoT[s])
                    T.dma("sp", Bxr[s], xr[s][:], srcB, xT(src, t0, TT), Bxr[s])
                    for dc in range(KC):
                        o = dc % 2
                        for h in range(H):
                            T.op("pe", lambda e: e.matmul(ps[o][:], wo[:, dc, h, :], oT[s][:, h, :], start=(h == 0), stop=(h == H - 1)),
                                 reads=[Bwo, BoT[s]], writes=[Bps[o]])
                        T.op("dve", lambda e: e.tensor_tensor(out=xn[s][:, dc, :], in0=ps[o][:], in1=xr[s][:, dc, :], op=ALU.add),
                             reads=[Bps[o], Bxr[s]], writes=[Bxn[s]])
                    T.dma("pool", dstB, xT(dst, t0, TT), Bxn[s], xn[s][:], Bxn[s])
                T.barrier([Bwo] + BoT + Bxr + Bxn)

        def phase_final(sq, srcB, src):
            S = sq["S"]
            J = S // TT
            with contextlib.ExitStack() as ph:
                N = NormCtx(ph)
                yo = [ph.enter_context(sbt("z_yo%d" % i, [128, KC, TT], F32)) for i in range(2)]
                Byo = [Buf("z_yo%d" % i) for i in range(2)]
                for j in range(J):
                    s = j % 2
                    N.BhT[s] = Byo[s]
                    N.emit(s, srcB, src, j * TT, TT, "nfin", 0, out_f32=yo[s])
                    T.dma("pool", sq["Byout"], xT(sq["yout"], j * TT, TT), Byo[s], yo[s][:], Byo[s])
                T.barrier(N.bufs() + Byo)

        cur = {tag: ("Bxin", "xin") for tag, _ in seqs}

        def nxt(tag):
            return ("Bxb", "xb") if cur[tag][1] in ("xin", "xa") else ("Bxa", "xa")

        for li in range(DEPTH):
            for tag, _ in seqs:
                sq = SQ[tag]
                sB, s = cur[tag]
                dB, d = nxt(tag)
                if li % 2 == 0:
                    phase_gmlp(sq, li, li // 2, sq[sB], sq[s], sq[dB], sq[d])
                else:
                    phase_qkv(sq, li, li // 2, sq[sB], sq[s])
                    phase_attn(sq, li, li // 2)
                    phase_aout(sq, li, li // 2, sq[sB], sq[s], sq[dB], sq[d])
                cur[tag] = (dB, d)
            for tag, _ in seqs:
                sq = SQ[tag]
                sB, s = cur[tag]
                dB, d = nxt(tag)
                phase_ffn(sq, li, sq[sB], sq[s], sq[dB], sq[d])
                cur[tag] = (dB, d)
        for tag, _ in seqs:
            sq = SQ[tag]
            sB, s = cur[tag]
            phase_final(sq, sq[sB], sq[s])
        T.barrier([], release=False)
        build_program.stats = (dict(T.nins), T.nwait, len(T.semh))
    return nc


N_CORES = 8


def kernel(**inputs):
    xs = np.asarray(inputs["x_sample"], dtype=np.float32)
    xp = np.asarray(inputs["x_prompt"], dtype=np.float32)
    Ss, Sp = xs.shape[1], xp.shape[1]
    wall, pall, cosT, sinT = host_pack(inputs, max(Ss, Sp))
    nc = build_program([("s", Ss), ("p", Sp)])
    in_maps = []
    for c in range(N_CORES):
        in_maps.append({
            "wall": wall, "pall": pall, "cosT": cosT, "sinT": sinT,
            "x_s": np.ascontiguousarray(xs[c].T),
            "x_p": np.ascontiguousarray(xp[c // 2].T),
        })
    res = run_bass_kernel_spmd(nc, in_maps, core_ids=list(range(N_CORES)))
    y_s = np.stack([np.ascontiguousarray(res.results[c]["y_s"].T) for c in range(N_CORES)], axis=0)
    y_p = np.stack([np.ascontiguousarray(res.results[2 * i]["y_p"].T) for i in range(xp.shape[0])], axis=0)
    return (y_p.astype(np.float32), y_s.astype(np.float32))
```

```python
import contextlib
import math
import numpy as np
import concourse.bass as bass
import concourse.mybir as mybir
from concourse.bass_utils import run_bass_kernel_spmd

F32 = mybir.dt.float32
BF16 = mybir.dt.bfloat16
AF = mybir.ActivationFunctionType
ALU = mybir.AluOpType

D = 1024
KC = 8
DEPTH = 4
DFF = 2816
NPAIR = 22
H = 8
NORM_EPS = 1e-6
SUBLN_EPS = 1e-5
ROPE_THETA = 10000.0
TT = 512
GELU_C = 0.044715


class Buf:
    __slots__ = ("name", "w", "r", "excl", "merge", "dsem")

    def __init__(self, name, excl=False, merge=False):
        self.name = name
        self.w = {}
        self.r = {}
        self.excl = excl
        self.merge = merge
        self.dsem = {}


class Trk:
    EPOCH = 30000

    def __init__(self, nc, es):
        self.nc = nc
        self.es = es
        self.eng = {"pe": nc.tensor, "act": nc.scalar, "dve": nc.vector, "pool": nc.gpsimd, "sp": nc.sync}
        self.semh = {}
        self.cnt = {e: 0 for e in self.eng}
        self.epoch = {e: 0 for e in self.eng}
        self.cur = {}
        self.waited = {e: {} for e in self.eng}
        self.dval = {}
        self.active = []
        self.dfree = {e: [] for e in self.eng}
        self.nd = 0
        self.nwait = 0
        self.nins = {e: 0 for e in self.eng}
        for e in self.eng:
            self._newsem(e)

    def _mk(self, name):
        self.semh[name] = self.es.enter_context(self.nc.semaphore(name))
        return name

    def _newsem(self, e):
        self.cur[e] = self._mk("q_%s_%d" % (e, self.epoch[e]))
        self.cnt[e] = 0

    def _deps(self, reads, writes):
        deps = {}
        reads = [b for b in reads if not b.merge]
        writes = [b for b in writes if not b.merge]
        for b in reads:
            for s, v in b.w.items():
                if deps.get(s, 0) < v:
                    deps[s] = v
            if b.excl:
                for s, v in b.r.items():
                    if deps.get(s, 0) < v:
                        deps[s] = v
        for b in writes:
            for s, v in b.w.items():
                if deps.get(s, 0) < v:
                    deps[s] = v
            for s, v in b.r.items():
                if deps.get(s, 0) < v:
                    deps[s] = v
        return deps

    def _wait(self, e, deps):
        engine = self.eng[e]
        wd = self.waited[e]
        own = self.cur[e]
        for s, v in deps.items():
            if e == "pe" and s == own:
                continue
            if wd.get(s, 0) >= v:
                continue
            engine.wait_ge(self.semh[s], v)
            wd[s] = v
            self.nwait += 1

    def _record(self, ev, reads, writes):
        s, v = ev
        for b in writes:
            if b.merge:
                continue
            b.w = {s: v}
            b.r = {}
        for b in reads:
            if b.merge:
                continue
            if b.excl:
                b.w = {s: v}
                b.r = {}
            elif b.r.get(s, 0) < v:
                b.r[s] = v

    def op(self, e, fn, reads=(), writes=()):
        if self.cnt[e] >= self.EPOCH:
            self.epoch[e] += 1
            self._newsem(e)
        self._wait(e, self._deps(reads, writes))
        ins = fn(self.eng[e])
        self.cnt[e] += 1
        ev = (self.cur[e], self.cnt[e])
        ins.then_inc(self.semh[ev[0]], 1)
        self._record(ev, reads, writes)
        self.nins[e] += 1
        return ins

    def dma(self, q, dst, dst_ap, src, src_ap, side, slow=False):
        if q not in side.dsem:
            if self.dfree[q]:
                side.dsem[q] = self.dfree[q].pop()
            else:
                side.dsem[q] = self._mk("d%s%d" % (q, self.nd))
                self.nd += 1
                self.dval[side.dsem[q]] = 0
            self.active.append((side, q))
        name = side.dsem[q]
        self._wait(q, self._deps((src,), (dst,)))
        if slow:
            ins = self.eng[q].dma_start(out=dst_ap, in_=src_ap, allow_slow_non_contiguous=True)
        else:
            ins = self.eng[q].dma_start(out=dst_ap, in_=src_ap)
        self.dval[name] += 16
        ev = (name, self.dval[name])
        ins.then_inc(self.semh[name], 16)
        self._record(ev, (src,), (dst,))
        self.nins[q] += 1
        return ins

    def barrier(self, bufs=(), release=True):
        deps = {}
        for e in self.eng:
            if self.cnt[e] > 0:
                deps[self.cur[e]] = self.cnt[e]
        for b, q in self.active:
            deps[b.dsem[q]] = self.dval[b.dsem[q]]
        for e in self.eng:
            self._wait(e, dict(deps))
        if release:
            for b, q in self.active:
                self.dfree[q].append(b.dsem.pop(q))
            self.active = []


class Lay:
    def __init__(self):
        self.off = {}
        self.n = 0

    def add(self, name, ncols):
        self.off[name] = (self.n, ncols)
        self.n += ncols


def make_layouts():
    wl = Lay()
    for j in range(2):
        wl.add("g%d_wu" % j, 8 * 8 * 128)
        wl.add("g%d_wv" % j, 8 * 1024)
        wl.add("g%d_ws" % j, 8 * 128)
        wl.add("g%d_wo" % j, 8 * 8 * 128)
    for j in range(2):
        wl.add("a%d_wqk" % j, 16 * 2 * 8 * 128)
        wl.add("a%d_wv" % j, 8 * 1024)
        wl.add("a%d_wo" % j, 8 * 8 * 128)
    for i in range(DEPTH):
        wl.add("f%d_win" % i, 44 * 8 * 128)
        wl.add("f%d_wout" % i, 8 * NPAIR * 128)
    pl = Lay()
    pl.add("nmix", DEPTH * 8)
    pl.add("nffn", DEPTH * 8)
    pl.add("nfin", 8)
    pl.add("vgain", 2 * 8)
    pl.add("convw", DEPTH * 44 * 4)
    pl.add("bsrep", 2 * 8 * 128)
    pl.add("subg", 2 * 128)
    pl.add("lam", 2 * 4 * 64)
    pl.add("ident", 128)
    return wl, pl


def pack_lhsT(w):
    K, M = w.shape
    return w.reshape(K // 128, 128, M // 128, 128).transpose(1, 2, 0, 3).reshape(128, -1)


def pack_rhs(w):
    K, N = w.shape
    return w.reshape(K // 128, 128, N).transpose(1, 0, 2).reshape(128, -1)


def pack_wout(w):
    K, N = w.shape
    return w.reshape(K // 128, 128, N // 128, 128).transpose(1, 2, 0, 3).reshape(128, -1)


def host_pack(inp, smax):
    wl, pl = make_layouts()
    wall = np.empty((128, wl.n), np.float32)
    pall = np.empty((128, pl.n), np.float32)

    def putw(name, arr):
        o, n = wl.off[name]
        assert arr.shape == (128, n), (name, arr.shape, n)
        wall[:, o:o + n] = arr

    def putp(name, arr):
        o, n = pl.off[name]
        assert arr.shape == (128, n), (name, arr.shape, n)
        pall[:, o:o + n] = arr

    for j in range(2):
        w = np.asarray(inp["gmlp_w_in"][j])
        putw("g%d_wu" % j, pack_lhsT(w[:, :1024]))
        putw("g%d_wv" % j, pack_rhs(w[:, 1024:]))
        putw("g%d_ws" % j, np.asarray(inp["gmlp_w_s"][j]).transpose(2, 0, 1).reshape(128, -1))
        putw("g%d_wo" % j, pack_wout(np.asarray(inp["gmlp_w_out"][j])))
    swap = np.arange(128)
    swap = (swap // 64) * 64 + ((swap % 64) + 32) % 64
    for j in range(2):
        w = np.asarray(inp["diff_w_qkv"][j])
        qk = w[:, :2048].reshape(1024, 16, 128)
        both = np.stack([qk, qk[:, :, swap]], axis=2)
        arr = both.reshape(8, 128, 16, 2, 128).transpose(1, 2, 3, 0, 4)
        putw("a%d_wqk" % j, arr.reshape(128, -1))
        putw("a%d_wv" % j, pack_rhs(w[:, 2048:]))
        putw("a%d_wo" % j, pack_wout(np.asarray(inp["diff_w_out"][j])))
    for i in range(DEPTH):
        w = np.asarray(inp["ffn_w_in"][i])
        wp = np.stack([w[:, :DFF].reshape(1024, NPAIR, 128), w[:, DFF:].reshape(1024, NPAIR, 128)], axis=2)
        putw("f%d_win" % i, pack_lhsT(wp.reshape(1024, 2 * DFF)))
        putw("f%d_wout" % i, pack_wout(np.asarray(inp["ffn_w_out"][i])))

    def pervec(v):
        v = np.asarray(v).reshape(-1, 8, 128)
        return v.transpose(2, 0, 1).reshape(128, -1)

    putp("nmix", pervec(inp["norm_mix"]))
    putp("nffn", pervec(inp["norm_ffn"]))
    putp("nfin", pervec(inp["norm_final"]))
    putp("vgain", pervec(inp["gmlp_v_gain"]))
    cw = np.asarray(inp["ffn_conv_w"])
    cb = np.asarray(inp["ffn_conv_b"])
    c4 = np.concatenate([cw, cb[:, None, :]], axis=1)
    c4 = c4.reshape(DEPTH, 4, 2, NPAIR, 128)
    putp("convw", c4.transpose(4, 0, 3, 2, 1).reshape(128, -1))
    bs = np.asarray(inp["gmlp_b_s"])
    bsr = np.broadcast_to(bs[None, :, :, :], (128, 2, 8, 128))
    putp("bsrep", np.ascontiguousarray(bsr).reshape(128, -1))
    sg = np.asarray(inp["diff_subln_g"])
    putp("subg", np.ascontiguousarray(np.broadcast_to(sg[None], (128, 2, 128))).reshape(128, -1))
    lam = np.stack([np.asarray(inp["diff_lam_q1"]), np.asarray(inp["diff_lam_k1"]),
                    np.asarray(inp["diff_lam_q2"]), np.asarray(inp["diff_lam_k2"])], axis=1)
    putp("lam", np.ascontiguousarray(np.broadcast_to(lam[None], (128, 2, 4, 64))).reshape(128, -1))
    putp("ident", np.eye(128, dtype=np.float32))
    pos = np.arange(smax, dtype=np.float32)
    inv_freq = (ROPE_THETA ** (-np.arange(0, 64, 2, dtype=np.float32) / np.float32(64))).astype(np.float32)
    ang = (pos[:, None] * inv_freq[None, :]).astype(np.float32)
    dd = np.arange(128) % 64
    cosT = np.cos(ang).astype(np.float32).T[dd % 32, :]
    sinT = np.sin(ang).astype(np.float32).T[dd % 32, :]
    sgn = np.where(dd < 32, -1.0, 1.0).astype(np.float32)[:, None]
    return wall, pall, np.ascontiguousarray(cosT), np.ascontiguousarray(sinT * sgn)


def build_program(seqs):
    wl, pl = make_layouts()
    smax = max(s for _, s in seqs)
    nc = bass.Bass("TRN2", target_bir_lowering=False)
    wall = nc.dram_tensor("wall", [128, wl.n], F32, kind="ExternalInput").ap()
    pall_d = nc.dram_tensor("pall", [128, pl.n], F32, kind="ExternalInput").ap()
    cos_d = nc.dram_tensor("cosT", [128, smax], F32, kind="ExternalInput").ap()
    sin_d = nc.dram_tensor("sinT", [128, smax], F32, kind="ExternalInput").ap()
    wbf = nc.dram_tensor("wbf", [128, wl.n], BF16, kind="Internal").ap()
    Bwall = Buf("wall", merge=True)
    Bwbf = Buf("wbf", merge=True)
    Bconst = Buf("constd", merge=True)
    SQ = {}
    for tag, S in seqs:
        d = {}
        d["S"] = S
        d["xin"] = nc.dram_tensor("x_" + tag, [D, S], F32, kind="ExternalInput").ap()
        d["yout"] = nc.dram_tensor("y_" + tag, [D, S], F32, kind="ExternalOutput").ap()
        d["xa"] = nc.dram_tensor("xa_" + tag, [D, S], F32, kind="Internal").ap()
        d["xb"] = nc.dram_tensor("xb_" + tag, [D, S], F32, kind="Internal").ap()
        d["qk"] = nc.dram_tensor("qk_" + tag, [16, 128, S], BF16, kind="Internal").ap()
        d["vs"] = nc.dram_tensor("vs_" + tag, [H, S, 128], BF16, kind="Internal").ap()
        d["ot"] = nc.dram_tensor("ot_" + tag, [H, 128, S], BF16, kind="Internal").ap()
        for k in ("xin", "yout", "xa", "xb", "qk", "vs", "ot"):
            d["B" + k] = Buf(k + "_" + tag, merge=True)
        SQ[tag] = d

    uid = [0]

    def sbt(name, shape, dt):
        uid[0] += 1
        return nc.sbuf_tensor("%s_u%d" % (name, uid[0]), shape, dt)

    def pst_(name, shape, dt):
        uid[0] += 1
        return nc.psum_tensor("%s_u%d" % (name, uid[0]), shape, dt)

    es = contextlib.ExitStack()
    with es:
        T = Trk(nc, es)

        def wslice(name, a=0, n=None):
            o, tot = wl.off[name]
            if n is None:
                n = tot - a
            return wbf[:, o + a:o + a + n]

        pall = es.enter_context(sbt("pall_sb", [128, pl.n], F32))
        Bpall = Buf("pall")
        T.dma("sp", Bpall, pall[:], Bconst, pall_d[:, :], Bpall)

        def pcol(name, a, n=1):
            o, _ = pl.off[name]
            return pall[:, o + a:o + a + n]

        ident = es.enter_context(sbt("ident", [128, 128], BF16))
        ones32 = es.enter_context(sbt("ones32", [128, 128], F32))
        lamt = es.enter_context(sbt("lamt", [128, 16], F32))
        subgs = es.enter_context(sbt("subgs", [128, 2, 128], F32))
        Bmisc = Buf("misc")
        T.op("dve", lambda e: e.tensor_copy(out=ident[:], in_=pcol("ident", 0, 128)), reads=[Bpall], writes=[Bmisc])
        T.op("pool", lambda e: e.memset(ones32[:], 1.0), writes=[Bmisc])
        mhalf = es.enter_context(sbt("mhalf", [128, 4], F32))
        T.op("pool", lambda e: e.memset(mhalf[:], -0.5), writes=[Bmisc])
        lam_inits = []
        for j in range(2):
            li = 0.8 - 0.6 * math.exp(-0.3 * (2 * j + 1))
            lam_inits.append(li)
            lo = pl.off["lam"][0] + j * 256
            for w_ in range(2):
                T.op("dve", lambda e: e.tensor_tensor(out=subgs[:, 0, 0:64], in0=pall[:, lo + w_ * 128:lo + w_ * 128 + 64],
                                                      in1=pall[:, lo + w_ * 128 + 64:lo + w_ * 128 + 128], op=ALU.mult),
                     reads=[Bpall, Bmisc], writes=[Bmisc])
                T.op("dve", lambda e: e.reduce_sum(out=lamt[:, 8 + w_:9 + w_], in_=subgs[:, 0, 0:64], axis=mybir.AxisListType.X),
                     reads=[Bmisc], writes=[Bmisc])
            T.op("act", lambda e: e.activation(out=lamt[:, 10:12], in_=lamt[:, 8:10], func=AF.Exp), reads=[Bmisc], writes=[Bmisc])
            T.op("dve", lambda e: e.scalar_tensor_tensor(out=lamt[:, j:j + 1], in0=lamt[:, 11:12], scalar=-li, in1=lamt[:, 10:11],
                                                         op0=ALU.add, op1=ALU.subtract), reads=[Bmisc], writes=[Bmisc])
        for j in range(2):
            T.op("dve", lambda e: e.tensor_scalar(out=subgs[:, j, :], in0=pcol("subg", j * 128, 128), scalar1=1.0 - lam_inits[j],
                                                  scalar2=None, op0=ALU.mult), reads=[Bpall, Bmisc], writes=[Bmisc])
        T.barrier([Bpall])

        def phase_cast():
            CW = 4096
            with contextlib.ExitStack() as ph:
                NS = 3
                f = [ph.enter_context(sbt("cf%d" % i, [128, CW], F32)) for i in range(NS)]
                b = [ph.enter_context(sbt("cb%d" % i, [128, CW], BF16)) for i in range(NS)]
                Bf = [Buf("cf%d" % i) for i in range(NS)]
                Bb = [Buf("cb%d" % i) for i in range(NS)]
                nch = (wl.n + CW - 1) // CW
                for c in range(nch):
                    s = c % NS
                    a = c * CW
                    n = min(CW, wl.n - a)
                    T.dma("sp", Bf[s], f[s][:, 0:n], Bwall, wall[:, a:a + n], Bf[s])
                    if c % 2 == 0:
                        T.op("dve", lambda e: e.tensor_copy(out=b[s][:, 0:n], in_=f[s][:, 0:n]), reads=[Bf[s]], writes=[Bb[s]])
                    else:
                        T.op("act", lambda e: e.activation(out=b[s][:, 0:n], in_=f[s][:, 0:n], func=AF.Copy), reads=[Bf[s]], writes=[Bb[s]])
                    T.dma("pool", Bwbf, wbf[:, a:a + n], Bb[s], b[s][:, 0:n], Bb[s])
                T.barrier(Bf + Bb)

        phase_cast()

        def xT(ap, t0, n):
            return ap.rearrange("(kc p) t -> p kc t", p=128)[:, :, t0:t0 + n]

        class NormCtx:
            def __init__(self, ph, nslot=2):
                self.ns = nslot
                self.xin = [ph.enter_context(sbt("n_xin%d" % i, [128, KC, TT], F32)) for i in range(nslot)]
                self.hT = [ph.enter_context(sbt("n_hT%d" % i, [128, KC, TT], BF16)) for i in range(nslot)]
                self.sq = ph.enter_context(sbt("n_sq", [128, 2, TT], F32))
                self.rs = ph.enter_context(sbt("n_rs", [128, TT], F32))
                self.ps = ph.enter_context(pst_("n_ps", [128, TT], F32))
                self.Bxin = [Buf("n_xin%d" % i) for i in range(nslot)]
                self.BhT = [Buf("n_hT%d" % i) for i in range(nslot)]
                self.Bsq = [Buf("n_sq0"), Buf("n_sq1")]
                self.Brs = Buf("n_rs")
                self.Bps = Buf("n_ps", excl=True)

            def bufs(self):
                return self.Bxin + self.BhT + self.Bsq + [self.Brs]

            def emit(self, slot, srcB, src_ap, t0, n, gname, gidx, out_f32=None):
                xin, hT = self.xin[slot], self.hT[slot]
                Bxin, BhT = self.Bxin[slot], self.BhT[slot]
                T.dma("sp", Bxin, xin[:, :, 0:n], srcB, xT(src_ap, t0, n), Bxin)
                for kc in range(KC):
                    T.op("act", lambda e: e.activation(out=self.sq[:, kc % 2, 0:n], in_=xin[:, kc, 0:n], func=AF.Square),
                         reads=[Bxin], writes=[self.Bsq[kc % 2]])
                    T.op("pe", lambda e: e.matmul(self.ps[:, 0:n], ones32[:], self.sq[:, kc % 2, 0:n], start=(kc == 0), stop=(kc == KC - 1)),
                         reads=[self.Bsq[kc % 2], Bmisc], writes=[self.Bps])
                T.op("act", lambda e: e.activation(out=self.rs[:, 0:n], in_=self.ps[:, 0:n], func=AF.Ln, scale=1.0 / D, bias=NORM_EPS),
                     reads=[self.Bps], writes=[self.Brs])
                T.op("act", lambda e: e.activation(out=self.rs[:, 0:n], in_=self.rs[:, 0:n], func=AF.Exp, scale=-0.5),
                     reads=[self.Brs], writes=[self.Brs])
                for kc in range(KC):
                    dst = hT[:, kc, 0:n] if out_f32 is None else out_f32[:, kc, 0:n]
                    T.op("dve", lambda e: e.scalar_tensor_tensor(out=dst, in0=xin[:, kc, 0:n], scalar=pcol(gname, gidx * 8 + kc),
                                                                 in1=self.rs[:, 0:n], op0=ALU.mult, op1=ALU.mult),
                         reads=[Bxin, self.Brs, Bpall], writes=[BhT])

        def phase_ffn(sq, li, srcB, src, dstB, dst):
            S = sq["S"]
            J = S // TT
            with contextlib.ExitStack() as ph:
                N = NormCtx(ph)
                NW = 3
                win = [ph.enter_context(sbt("f_win%d" % i, [128, 2, KC, 128], BF16)) for i in range(NW)]
                Bwin = [Buf("f_win%d" % i) for i in range(NW)]
                wout = ph.enter_context(sbt("f_wout", [128, KC, NPAIR, 128], BF16))
                Bwout = [Buf("f_wout%d" % i) for i in range(KC)]
                NA = 3
                aext = [ph.enter_context(sbt("f_aext%d" % i, [128, 2, TT + 2], F32)) for i in range(NA)]
                Baext = [Buf("f_aext%d" % i) for i in range(NA)]
                ct = [ph.enter_context(sbt("f_ct%d" % i, [128, 2, TT], F32)) for i in range(NA)]
                Bct = [Buf("f_ct%d" % i) for i in range(NA)]
                sg = [ph.enter_context(sbt("f_sg%d" % i, [128, TT], F32)) for i in range(NA)]
                Bsg = [Buf("f_sg%d" % i) for i in range(NA)]
                carry = ph.enter_context(sbt("f_carry", [128, 2 * NPAIR, 2], F32))
                Bcarry = [Buf("f_carry%d" % i) for i in range(NPAIR)]
                yT = ph.enter_context(sbt("f_yT", [128, NPAIR, TT], BF16))
                ByT = [Buf("f_yT%d" % i) for i in range(NPAIR)]
                xres = ph.enter_context(sbt("f_xres", [128, KC, TT], F32))
                Bxres = Buf("f_xres")
                psa = [ph.enter_context(pst_("f_psa%d" % i, [128, 2, TT], F32)) for i in range(2)]
                Bpsa = [[Buf("f_psa%d_%d" % (i, h), excl=True) for h in range(2)] for i in range(2)]
                pso = [ph.enter_context(pst_("f_pso%d" % i, [128, TT], F32)) for i in range(2)]
                Bpso = [Buf("f_pso%d" % i, excl=True) for i in range(2)]
                cwo = pl.off["convw"][0] + li * 44 * 4

                T.op("pool", lambda e: e.memset(carry[:], 0.0), writes=Bcarry)
                ctr = [0, 0, 0]
                pend = []
                wout_loaded = [False]

                def win_stage(j, slot):
                    virt = (j == J)
                    hT, BhT = N.hT[slot], N.BhT[slot]
                    n = 1 if virt else TT
                    prev = None
                    for pr in range(NPAIR + 1):
                        if pr == 3:
                            while pend:
                                pend.pop()()
                        if pr == NPAIR // 2 and j + 1 < J:
                            N.emit((j + 1) % 2, srcB, src, (j + 1) * TT, TT, "nffn", li)
                        if pr < NPAIR:
                            ws = ctr[0] % NW
                            ctr[0] += 1
                            a = ctr[1] % NA
                            pb = ctr[1] % 2
                            ctr[1] += 1
                            if not virt:
                                T.dma("sp", Bwin[ws], win[ws][:].rearrange("p a b c -> p (a b c)"), Bwbf,
                                      wslice("f%d_win" % li, pr * 2048, 2048), Bwin[ws])
                                for hf in range(2):
                                    for kc in range(KC):
                                        T.op("pe", lambda e: e.matmul(psa[pb][:, hf, :], win[ws][:, hf, kc, :], hT[:, kc, :],
                                                                      start=(kc == 0), stop=(kc == KC - 1)),
                                             reads=[Bwin[ws], BhT], writes=[Bpsa[pb][hf]])
                            T.op("pool", lambda e: e.tensor_copy(out=aext[a][:, :, 0:2], in_=carry[:, 2 * pr:2 * pr + 2, :]),
                                 reads=[Bcarry[pr]], writes=[Baext[a]])
                            if virt:
                                T.op("pool", lambda e: e.memset(aext[a][:, :, 2:TT + 2], 0.0), writes=[Baext[a]])
                            else:
                                T.op("act", lambda e: e.activation(out=aext[a][:, :, 2:TT + 2], in_=psa[pb][:, :, :], func=AF.Copy),
                                     reads=Bpsa[pb], writes=[Baext[a]])
                                T.op("pool", lambda e: e.tensor_copy(out=carry[:, 2 * pr:2 * pr + 2, :], in_=aext[a][:, :, TT:TT + 2]),
                                     reads=[Baext[a]], writes=[Bcarry[pr]])
                            for hf in range(2):
                                co = cwo + (pr * 2 + hf) * 4
                                T.op("act", lambda e: e.activation(out=ct[a][:, hf, 0:n], in_=aext[a][:, hf, 0:n], func=AF.Identity,
                                                                   scale=pall[:, co:co + 1], bias=pall[:, co + 3:co + 4]),
                                     reads=[Baext[a], Bpall], writes=[Bct[a]])
                        if prev is not None:
                            pa, ppr = prev
                            T.op("act", lambda e: e.activation(out=sg[pa][:, 0:n], in_=ct[pa][:, 0, 0:n], func=AF.Silu),
                                 reads=[Bct[pa]], writes=[Bsg[pa]])
                        if pr < NPAIR:
                            for hf in range(2):
                                co = cwo + (pr * 2 + hf) * 4
                                for t in (1, 2):
                                    T.op("dve", lambda e: e.scalar_tensor_tensor(out=ct[a][:, hf, 0:n], in0=aext[a][:, hf, t:t + n],
                                                                                 scalar=pall[:, co + t:co + t + 1], in1=ct[a][:, hf, 0:n],
                                                                                 op0=ALU.mult, op1=ALU.add),
                                         reads=[Baext[a], Bct[a], Bpall], writes=[Bct[a]])
                        if prev is not None:
                            pa, ppr = prev
                            T.op("dve", lambda e: e.tensor_tensor(out=yT[:, ppr, 0:n], in0=sg[pa][:, 0:n], in1=ct[pa][:, 1, 0:n], op=ALU.mult),
                                 reads=[Bsg[pa], Bct[pa]], writes=[ByT[ppr]])
                        prev = (a, pr) if pr < NPAIR else None

                def wout_stage(j):
                    virt = (j == J)
                    lo = 1 if j == 0 else 0
                    hi = 1 if virt else TT
                    n = hi - lo
                    tok0 = j * TT - 1 + lo
                    T.dma("sp", Bxres, xres[:, :, 0:n], srcB, xT(src, tok0, n), Bxres, slow=(n == 1))
                    for dc in range(KC):
                        o = dc % 2
                        if not wout_loaded[0]:
                            T.dma("sp", Bwout[dc], wout[:, dc, :, :].rearrange("p a b -> p (a b)"), Bwbf,
                                  wslice("f%d_wout" % li, dc * NPAIR * 128, NPAIR * 128), Bwout[dc])
                        for pr in range(NPAIR):
                            T.op("pe", lambda e: e.matmul(pso[o][:, 0:n], wout[:, dc, pr, :], yT[:, pr, lo:hi],
                                                          start=(pr == 0), stop=(pr == NPAIR - 1)),
                                 reads=[Bwout[dc], ByT[pr]], writes=[Bpso[o]])
                        T.op("dve", lambda e: e.tensor_tensor(out=xres[:, dc, 0:n], in0=pso[o][:, 0:n], in1=xres[:, dc, 0:n], op=ALU.add),
                             reads=[Bpso[o], Bxres], writes=[Bxres])
                    wout_loaded[0] = True
                    pend.append(lambda: T.dma("sp", dstB, xT(dst, tok0, n), Bxres, xres[:, :, 0:n], Bxres, slow=(n == 1)))

                N.emit(0, srcB, src, 0, TT, "nffn", li)
                for j in range(J + 1):
                    win_stage(j, j % 2)
                    wout_stage(j)
                while pend:
                    pend.pop()()
                T.barrier()

        def phase_gmlp(sq, li, gj, srcB, src, dstB, dst):
            S = sq["S"]
            J = S // TT
            with contextlib.ExitStack() as ph:
                N = NormCtx(ph)
                wu = ph.enter_context(sbt("g_wu", [128, 8, KC, 128], BF16))
                wv = ph.enter_context(sbt("g_wv", [128, KC, 1024], BF16))
                ws = ph.enter_context(sbt("g_ws", [128, 8, 128], BF16))
                wo = ph.enter_context(sbt("g_wo", [128, 8, KC, 128], BF16))
                Bw = Buf("g_w")
                Bw2 = [Buf("g_w2_%d" % i) for i in range(3)]
                uT = ph.enter_context(sbt("g_uT", [128, 8, TT], F32))
                BuT = [Buf("g_uT%d" % i) for i in range(8)]
                vg = [ph.enter_context(sbt("g_vg%d" % i, [128, 1024], F32)) for i in range(2)]
                Bvg = [Buf("g_vg%d" % i) for i in range(2)]
                vsq = ph.enter_context(sbt("g_vsq", [128, 1024], BF16))
                Bvsq = Buf("g_vsq")
                vn = ph.enter_context(sbt("g_vn", [128, 4, 1024], BF16))
                Bvn = [Buf("g_vn%d" % i) for i in range(4)]
                st = [ph.enter_context(sbt("g_st%d" % i, [128, 8], F32)) for i in range(2)]
                Bst = [Buf("g_st%d" % i) for i in range(2)]
                v2 = [ph.enter_context(sbt("g_v2_%d" % i, [128, TT], F32)) for i in range(2)]
                Bv2 = [Buf("g_v2_%d" % i) for i in range(2)]
                yT = ph.enter_context(sbt("g_yT", [128, 8, TT], BF16))
                ByT = [Buf("g_yT%d" % i) for i in range(8)]
                psu = [ph.enter_context(pst_("g_psu%d" % i, [128, TT], F32)) for i in range(2)]
                Bpsu = [Buf("g_psu%d" % i, excl=True) for i in range(2)]
                psv = [ph.enter_context(pst_("g_psv%d" % i, [128, TT], F32)) for i in range(2)]
                Bpsv = [Buf("g_psv%d" % i, excl=True) for i in range(2)]
                pss = [ph.enter_context(pst_("g_pss%d" % i, [128, TT], F32)) for i in range(2)]
                Bpss = [Buf("g_pss%d" % i, excl=True) for i in range(2)]
                T.dma("sp", Bw, wu[:].rearrange("p a b c -> p (a b c)"), Bwbf, wslice("g%d_wu" % gj), Bw)
                T.dma("sp", Bw2[0], wv[:].rearrange("p a b -> p (a b)"), Bwbf, wslice("g%d_wv" % gj), Bw2[0])
                T.dma("sp", Bw2[1], ws[:].rearrange("p a b -> p (a b)"), Bwbf, wslice("g%d_ws" % gj), Bw2[1])
                T.dma("sp", Bw2[2], wo[:].rearrange("p a b c -> p (a b c)"), Bwbf, wslice("g%d_wo" % gj), Bw2[2])
                bso = pl.off["bsrep"][0] + gj * 8 * 128
                N.emit(0, srcB, src, 0, TT, "nmix", li)
                for j in range(J):
                    slot = j % 2
                    hT, BhT, xin, Bxin = N.hT[slot], N.BhT[slot], N.xin[slot], N.Bxin[slot]
                    for sub in range(4):
                        a = sub % 2
                        for hf in range(2):
                            for kc in range(KC):
                                T.op("pe", lambda e: e.matmul(psv[hf][:], hT[:, kc, sub * 128:(sub + 1) * 128], wv[:, kc, hf * 512:(hf + 1) * 512],
                                                              start=(kc == 0), stop=(kc == KC - 1)),
                                     reads=[Bw2[0], BhT], writes=[Bpsv[hf]])
                            T.op("act", lambda e: e.activation(out=vg[a][:, hf * 512:(hf + 1) * 512], in_=psv[hf][:], func=AF.Gelu_apprx_tanh),
                                 reads=[Bpsv[hf]], writes=[Bvg[a]])
                        T.op("act", lambda e: e.activation(out=vsq[:], in_=vg[a][:], func=AF.Square, accum_out=st[a][:, 0:1]),
                             reads=[Bvg[a]], writes=[Bvsq, Bst[a]])
                        T.op("act", lambda e: e.activation(out=st[a][:, 1:2], in_=st[a][:, 0:1], func=AF.Ln, scale=1.0 / 1024, bias=NORM_EPS),
                             reads=[Bst[a]], writes=[Bst[a]])
                        T.op("act", lambda e: e.activation(out=st[a][:, 2:3], in_=st[a][:, 1:2], func=AF.Exp, scale=-0.5),
                             reads=[Bst[a]], writes=[Bst[a]])
                        T.op("dve", lambda e: e.tensor_scalar(out=vn[:, sub, :], in0=vg[a][:], scalar1=st[a][:, 2:3], scalar2=None, op0=ALU.mult),
                             reads=[Bvg[a], Bst[a]], writes=[Bvn[sub]])
                    for m in range(8):
                        o = m % 2
                        for kc in range(KC):
                            T.op("pe", lambda e: e.matmul(psu[o][:], wu[:, m, kc, :], hT[:, kc, :], start=(kc == 0), stop=(kc == KC - 1)),
                                 reads=[Bw, BhT], writes=[Bpsu[o]])
                        T.op("act", lambda e: e.activation(out=uT[:, m, :], in_=psu[o][:], func=AF.Gelu_apprx_tanh),
                             reads=[Bpsu[o]], writes=[BuT[m]])
                    if j + 1 < J:
                        N.emit((j + 1) % 2, srcB, src, (j + 1) * TT, TT, "nmix", li)
                    for g in range(8):
                        o = g % 2
                        for sub in range(4):
                            T.op("pe", lambda e: e.matmul(pss[o][:, sub * 128:(sub + 1) * 128], vn[:, sub, g * 128:(g + 1) * 128], ws[:, g, :],
                                                          start=True, stop=True, skip_group_check=True),
                                 reads=[Bw2[1], Bvn[sub]], writes=[Bpss[o]])
                        for sub in range(4):
                            T.op("dve", lambda e: e.scalar_tensor_tensor(out=v2[o][:, sub * 128:(sub + 1) * 128], in0=pss[o][:, sub * 128:(sub + 1) * 128],
                                                                         scalar=pcol("vgain", gj * 8 + g),
                                                                         in1=pall[:, bso + g * 128:bso + (g + 1) * 128], op0=ALU.mult, op1=ALU.add),
                                 reads=[Bpss[o], Bpall], writes=[Bv2[o]])
                        T.op("dve", lambda e: e.tensor_tensor(out=yT[:, g, :], in0=uT[:, g, :], in1=v2[o][:], op=ALU.mult),
                             reads=[BuT[g], Bv2[o]], writes=[ByT[g]])
                    for dc in range(KC):
                        o = dc % 2
                        for g in range(8):
                            T.op("pe", lambda e: e.matmul(psu[o][:], wo[:, dc, g, :], yT[:, g, :], start=(g == 0), stop=(g == 7)),
                                 reads=[Bw2[2], ByT[g]], writes=[Bpsu[o]])
                        T.op("dve", lambda e: e.tensor_tensor(out=xin[:, dc, :], in0=psu[o][:], in1=xin[:, dc, :], op=ALU.add),
                             reads=[Bpsu[o], Bxin], writes=[Bxin])
                    T.dma("pool", dstB, xT(dst, j * TT, TT), Bxin, xin[:], Bxin)
                T.barrier()

        def phase_qkv(sq, li, aj, srcB, src):
            S = sq["S"]
            J = S // TT
            with contextlib.ExitStack() as ph:
                N = NormCtx(ph)
                NWS = 3
                wqk = [ph.enter_context(sbt("a_wqk%d" % i, [128, 2, KC, 128], BF16)) for i in range(NWS)]
                Bwqk = [Buf("a_wqk%d" % i) for i in range(NWS)]
                wv = ph.enter_context(sbt("a_wv", [128, KC, 1024], BF16))
                Bwv = Buf("a_wv")
                cs = [ph.enter_context(sbt("a_cs%d" % i, [128, 2, TT], F32)) for i in range(2)]
                Bcs = [Buf("a_cs%d" % i) for i in range(2)]
                t1 = [ph.enter_context(sbt("a_t1_%d" % i, [128, 2, TT], F32)) for i in range(2)]
                Bt1 = [Buf("a_t1_%d" % i) for i in range(2)]
                qo = [ph.enter_context(sbt("a_qo%d" % i, [128, TT], BF16)) for i in range(3)]
                Bqo = [Buf("a_qo%d" % i) for i in range(3)]
                vt = [ph.enter_context(sbt("a_vt%d" % i, [128, 1024], BF16)) for i in range(2)]
                Bvt = [Buf("a_vt%d" % i) for i in range(2)]
                psz = [ph.enter_context(pst_("a_psz%d" % i, [128, 2, TT], F32)) for i in range(2)]
                Bpsz = [[Buf("a_psz%d_%d" % (i, v), excl=True) for v in range(2)] for i in range(2)]
                psv = [ph.enter_context(pst_("a_psv%d" % i, [128, TT], F32)) for i in range(2)]
                Bpsv = [Buf("a_psv%d" % i, excl=True) for i in range(2)]
                T.dma("sp", Bwv, wv[:].rearrange("p a b -> p (a b)"), Bwbf, wslice("a%d_wv" % aj), Bwv)
                c0 = 0
                c1 = 0
                N.emit(0, srcB, src, 0, TT, "nmix", li)
                for j in range(J):
                    slot = j % 2
                    t0 = j * TT
                    hT, BhT = N.hT[slot], N.BhT[slot]
                    T.dma("sp", Bcs[slot], cs[slot][:, 0, :], Bconst, cos_d[:, t0:t0 + TT], Bcs[slot])
                    T.dma("sp", Bcs[slot], cs[slot][:, 1, :], Bconst, sin_d[:, t0:t0 + TT], Bcs[slot])
                    for m in range(16):
                        ws_ = c0 % NWS
                        a = c0 % 2
                        q = c0 % 3
                        c0 += 1
                        T.dma("sp", Bwqk[ws_], wqk[ws_][:].rearrange("p a b c -> p (a b c)"), Bwbf,
                              wslice("a%d_wqk" % aj, m * 2048, 2048), Bwqk[ws_])
                        for var in range(2):
                            for kc in range(KC):
                                T.op("pe", lambda e: e.matmul(psz[a][:, var, :], wqk[ws_][:, var, kc, :], hT[:, kc, :],
                                                              start=(kc == 0), stop=(kc == KC - 1)),
                                     reads=[Bwqk[ws_], BhT], writes=[Bpsz[a][var]])
                        for var in range(2):
                            T.op("dve", lambda e: e.tensor_tensor(out=t1[a][:, var, :], in0=psz[a][:, var, :], in1=cs[slot][:, var, :], op=ALU.mult),
                                 reads=[Bpsz[a][var], Bcs[slot]], writes=[Bt1[a]])
                        T.op("pool", lambda e: e.tensor_tensor(out=qo[q][:], in0=t1[a][:, 0, :], in1=t1[a][:, 1, :], op=ALU.add),
                             reads=[Bt1[a]], writes=[Bqo[q]])
                        T.dma("pool", sq["Bqk"], sq["qk"][m, :, t0:t0 + TT], Bqo[q], qo[q][:], Bqo[q])
                    if j + 1 < J:
                        N.emit((j + 1) % 2, srcB, src, (j + 1) * TT, TT, "nmix", li)
                    for sub in range(4):
                        v_ = c1 % 2
                        c1 += 1
                        for hf in range(2):
                            for kc in range(KC):
                                T.op("pe", lambda e: e.matmul(psv[hf][:], hT[:, kc, sub * 128:(sub + 1) * 128], wv[:, kc, hf * 512:(hf + 1) * 512],
                                                              start=(kc == 0), stop=(kc == KC - 1)),
                                     reads=[Bwv, BhT], writes=[Bpsv[hf]])
                            T.op("act", lambda e: e.activation(out=vt[v_][:, hf * 512:(hf + 1) * 512], in_=psv[hf][:], func=AF.Copy),
                                 reads=[Bpsv[hf]], writes=[Bvt[v_]])
                        tk = t0 + sub * 128
                        T.dma("pool", sq["Bvs"], sq["vs"][:, tk:tk + 128, :].rearrange("h t e -> t h e"),
                              Bvt[v_], vt[v_][:].rearrange("p (h e) -> p h e", h=H), Bvt[v_])
                T.barrier(N.bufs() + Bwqk + [Bwv] + Bcs + Bqo + Bvt)

        def phase_attn(sq, li, aj):
            S = sq["S"]
            NQ = S // TT
            NK = S // 128
            scale = 64 ** -0.5
            with contextlib.ExitStack() as ph:
                kT = [ph.enter_context(sbt("b_kT%d" % i, [128, S], BF16)) for i in range(2)]
                qT = [ph.enter_context(sbt("b_qT%d" % i, [128, S], BF16)) for i in range(2)]
                vh = [ph.enter_context(sbt("b_vh%d" % i, [128, NK, 130], BF16)) for i in range(2)]
                BkT = [Buf("b_kT%d" % i) for i in range(2)]
                BqT = [Buf("b_qT%d" % i) for i in range(2)]
                Bvh = [Buf("b_vh%d" % i) for i in range(2)]
                NP_ = 3
                pT = [ph.enter_context(sbt("b_pT%d" % i, [128, 2, TT], BF16)) for i in range(NP_)]
                BpT = [Buf("b_pT%d" % i) for i in range(NP_)]
                accs = ph.enter_context(sbt("b_accs", [128, 8, 129], F32))
                Baccs = Buf("b_accs")
                sm = ph.enter_context(sbt("b_sm", [128, 32], F32))
                Bsm = Buf("b_sm")
                o1 = [ph.enter_context(sbt("b_o1_%d" % i, [128, 128], F32)) for i in range(2)]
                o2 = [ph.enter_context(sbt("b_o2_%d" % i, [128, 128], F32)) for i in range(2)]
                ob = [ph.enter_context(sbt("b_ob_%d" % i, [128, 128], BF16)) for i in range(4)]
                Bob = [Buf("b_ob%d" % i) for i in range(4)]
                Bo = [Buf("b_o%d" % i) for i in range(2)]
                o1a = ph.enter_context(sbt("b_o1a", [128, 4, 128], F32))
                o2a = ph.enter_context(sbt("b_o2a", [128, 4, 128], F32))
                Bo1a = Buf("b_o1a")
                Bo2a = Buf("b_o2a")
                sm2 = ph.enter_context(sbt("b_sm2", [128, 8], F32))
                Bsm2 = Buf("b_sm2")
                oT = [ph.enter_context(sbt("b_oT%d" % i, [128, TT], BF16)) for i in range(2)]
                BoT = [Buf("b_oT%d" % i) for i in range(2)]
                pst = [ph.enter_context(pst_("b_pst%d" % i, [128, 2, TT], F32)) for i in range(2)]
                Bpst = [Buf("b_pst%d" % i, excl=True) for i in range(2)]
                pacc = ph.enter_context(pst_("b_pacc", [128, 3, TT], F32))
                Bpacc = [Buf("b_pacc%d" % i, excl=True) for i in range(3)]
                ptr = ph.enter_context(pst_("b_ptr", [128, TT], BF16))
                Bptr = Buf("b_ptr", excl=True)
                for i in range(2):
                    T.op("pool", lambda e: e.memset(vh[i][:, :, 128:130], 1.0), writes=[Bvh[i]])
                cc = 0
                eo = 0
                pending = []
                for h in range(H):
                    s = h % 2
                    T.dma("sp", BkT[s], kT[s][:], sq["Bqk"], sq["qk"][8 + h, :, :], BkT[s])
                    T.dma("sp", BqT[s], qT[s][:], sq["Bqk"], sq["qk"][h, :, :], BqT[s])
                    T.dma("sp", Bvh[s], vh[s][:, :, 0:128], sq["Bvs"], sq["vs"][h, :, :].rearrange("(kt p) e -> p kt e", p=128), Bvh[s])
                    units = [(qt_, kt_) for qt_ in range(NQ) for kt_ in range(NK)]

                    def emit_S(i):
                        qt_, kt_ = units[i]
                        a_ = (cc + i) % 2
                        for c in range(2):
                            T.op("pe", lambda e: e.matmul(pst[a_][:, c, :], kT[s][c * 64:(c + 1) * 64, kt_ * 128:(kt_ + 1) * 128],
                                                          qT[s][c * 64:(c + 1) * 64, qt_ * TT:(qt_ + 1) * TT], start=True, stop=True),
                                 reads=[BkT[s], BqT[s]], writes=[Bpst[a_]])

                    emit_S(0)
                    for ui in range(len(units)):
                        qt, kt = units[ui]
                        a = (cc + ui) % 2
                        p = (cc + ui) % NP_
                        if ui + 1 < len(units):
                            emit_S(ui + 1)
                        T.op("act", lambda e: e.activation(out=pT[p][:].rearrange("p a b -> p (a b)"),
                                                           in_=pst[a][:].rearrange("p a b -> p (a b)"), func=AF.Exp, scale=scale),
                             reads=[Bpst[a]], writes=[BpT[p]])
                        for c in range(2):
                            for qb in range(4):
                                ai = c * 4 + qb
                                bk, col = ai // 3, (ai % 3) * 129
                                T.op("pe", lambda e: e.matmul(pacc[:, bk, col:col + 129], pT[p][:, c, qb * 128:(qb + 1) * 128], vh[s][:, kt, 0:129],
                                                              start=(kt == 0 and ai % 3 == 0), stop=(kt == NK - 1), skip_group_check=True),
                                     reads=[BpT[p], Bvh[s]], writes=[Bpacc[bk]])
                        if kt == min(8, NK - 2) and pending:
                            for f_ in pending:
                                f_()
                            pending = []
                        if kt != NK - 1:
                            continue
                        for bk in range(3):
                            na = 3 if bk < 2 else 2
                            T.op("dve", lambda e: e.tensor_copy(out=accs[:, bk * 3:bk * 3 + na, :].rearrange("p a b -> p (a b)"), in_=pacc[:, bk, 0:na * 129]),
                                 reads=[Bpacc[bk]], writes=[Baccs])
                        T.op("dve", lambda e: e.reciprocal(out=sm[:, 0:8], in_=accs[:, :, 128]), reads=[Baccs], writes=[Bsm])
                        T.op("dve", lambda e: e.tensor_scalar(out=sm[:, 8:12], in0=sm[:, 4:8], scalar1=lamt[:, aj:aj + 1], scalar2=None, op0=ALU.mult),
                             reads=[Bsm, Bmisc], writes=[Bsm])
                        ot_s = eo % 2
                        eo += 1
                        for qb in range(4):
                            T.op("dve", lambda e: e.tensor_scalar(out=o2a[:, qb, :], in0=accs[:, 4 + qb, 0:128], scalar1=sm[:, 8 + qb:9 + qb], scalar2=None, op0=ALU.mult),
                                 reads=[Baccs, Bsm], writes=[Bo2a])
                            T.op("dve", lambda e: e.scalar_tensor_tensor(out=o1a[:, qb, :], in0=accs[:, qb, 0:128], scalar=sm[:, qb:qb + 1], in1=o2a[:, qb, :],
                                                                         op0=ALU.mult, op1=ALU.add),
                                 reads=[Baccs, Bsm, Bo2a], writes=[Bo1a])
                        T.op("dve", lambda e: e.tensor_tensor(out=o2a[:], in0=o1a[:], in1=o1a[:], op=ALU.mult), reads=[Bo1a, Bo2a], writes=[Bo2a])
                        T.op("dve", lambda e: e.reduce_sum(out=sm[:, 12:16], in_=o2a[:], axis=mybir.AxisListType.X), reads=[Bo2a, Bsm], writes=[Bsm])
                        T.op("pool", lambda e: e.tensor_scalar(out=sm2[:, 0:4], in0=sm[:, 12:16], scalar1=1.0 / 128, scalar2=SUBLN_EPS, op0=ALU.mult, op1=ALU.add),
                             reads=[Bsm], writes=[Bsm2])
                        T.op("pool", lambda e: e.tensor_tensor(out=sm2[:, 4:8], in0=sm2[:, 0:4], in1=mhalf[:, 0:4], op=ALU.pow),
                             reads=[Bsm2, Bmisc], writes=[Bsm2])
                        for qb in range(4):
                            T.op("dve", lambda e: e.scalar_tensor_tensor(out=ob[qb][:], in0=o1a[:, qb, :], scalar=sm2[:, 4 + qb:5 + qb], in1=subgs[:, aj, :],
                                                                         op0=ALU.mult, op1=ALU.mult),
                                 reads=[Bo1a, Bsm2, Bmisc], writes=[Bob[qb]])

                        def part2(qt=qt, ot_s=ot_s, h=h):
                            for qb in range(4):
                                T.op("pe", lambda e: e.transpose(ptr[:, qb * 128:(qb + 1) * 128], ob[qb][:], ident[:]),
                                     reads=[Bob[qb], Bmisc], writes=[Bptr])
                            T.op("dve", lambda e: e.tensor_copy(out=oT[ot_s][:], in_=ptr[:]), reads=[Bptr], writes=[BoT[ot_s]])
                            T.dma("pool", sq["Bot"], sq["ot"][h, :, qt * TT:(qt + 1) * TT], BoT[ot_s], oT[ot_s][:], BoT[ot_s])

                        pending.append(part2)
                    for f_ in pending:
                        f_()
                    pending = []
                    cc += len(units)
                T.barrier(BkT + BqT + Bvh + BoT)

        def phase_aout(sq, li, aj, srcB, src, dstB, dst):
            S = sq["S"]
            J = S // TT
            with contextlib.ExitStack() as ph:
                wo = ph.enter_context(sbt("c_wo", [128, 8, KC, 128], BF16))
                Bwo = Buf("c_wo")
                oT = [ph.enter_context(sbt("c_oT%d" % i, [128, H, TT], BF16)) for i in range(2)]
                BoT = [Buf("c_oT%d" % i) for i in range(2)]
                xr = [ph.enter_context(sbt("c_xr%d" % i, [128, KC, TT], F32)) for i in range(2)]
                Bxr = [Buf("c_xr%d" % i) for i in range(2)]
                xn = [ph.enter_context(sbt("c_xn%d" % i, [128, KC, TT], F32)) for i in range(2)]
                Bxn = [Buf("c_xn%d" % i) for i in range(2)]
                ps = [ph.enter_context(pst_("c_ps%d" % i, [128, TT], F32)) for i in range(2)]
                Bps = [Buf("c_ps%d" % i, excl=True) for i in range(2)]
                T.dma("sp", Bwo, wo[:].rearrange("p a b c -> p (a b c)"), Bwbf, wslice("a%d_wo" % aj), Bwo)
                for j in range(J):
                    s = j % 2
                    t0 = j * TT
                    T.dma("sp", BoT[s], oT[s][:], sq["Bot"], sq["ot"][:, :, t0:t0 + TT].rearrange("h p t -> p h t"), BoT[s])
                    T.dma("sp", Bxr[s], xr[s][:], srcB, xT(src, t0, TT), Bxr[s])
                    for dc in range(KC):
                        o = dc % 2
                        for h in range(H):
                            T.op("pe", lambda e: e.matmul(ps[o][:], wo[:, dc, h, :], oT[s][:, h, :], start=(h == 0), stop=(h == H - 1)),
                                 reads=[Bwo, BoT[s]], writes=[Bps[o]])
                        T.op("dve", lambda e: e.tensor_tensor(out=xn[s][:, dc, :], in0=ps[o][:], in1=xr[s][:, dc, :], op=ALU.add),
                             reads=[Bps[o], Bxr[s]], writes=[Bxn[s]])
                    T.dma("pool", dstB, xT(dst, t0, TT), Bxn[s], xn[s][:], Bxn[s])
                T.barrier([Bwo] + BoT + Bxr + Bxn)

        def phase_final(sq, srcB, src):
            S = sq["S"]
            J = S // TT
            with contextlib.ExitStack() as ph:
                N = NormCtx(ph)
                yo = [ph.enter_context(sbt("z_yo%d" % i, [128, KC, TT], F32)) for i in range(2)]
                Byo = [Buf("z_yo%d" % i) for i in range(2)]
                for j in range(J):
                    s = j % 2
                    N.BhT[s] = Byo[s]
                    N.emit(s, srcB, src, j * TT, TT, "nfin", 0, out_f32=yo[s])
                    T.dma("pool", sq["Byout"], xT(sq["yout"], j * TT, TT), Byo[s], yo[s][:], Byo[s])
                T.barrier(N.bufs() + Byo)

        cur = {tag: ("Bxin", "xin") for tag, _ in seqs}

        def nxt(tag):
            return ("Bxb", "xb") if cur[tag][1] in ("xin", "xa") else ("Bxa", "xa")

        for li in range(DEPTH):
            for tag, _ in seqs:
                sq = SQ[tag]
                sB, s = cur[tag]
                dB, d = nxt(tag)
                if li % 2 == 0:
                    phase_gmlp(sq, li, li // 2, sq[sB], sq[s], sq[dB], sq[d])
                else:
                    phase_qkv(sq, li, li // 2, sq[sB], sq[s])
                    phase_attn(sq, li, li // 2)
                    phase_aout(sq, li, li // 2, sq[sB], sq[s], sq[dB], sq[d])
                cur[tag] = (dB, d)
            for tag, _ in seqs:
                sq = SQ[tag]
                sB, s = cur[tag]
                dB, d = nxt(tag)
                phase_ffn(sq, li, sq[sB], sq[s], sq[dB], sq[d])
                cur[tag] = (dB, d)
        for tag, _ in seqs:
            sq = SQ[tag]
            sB, s = cur[tag]
            phase_final(sq, sq[sB], sq[s])
        T.barrier([], release=False)
        build_program.stats = (dict(T.nins), T.nwait, len(T.semh))
    return nc


N_CORES = 8


def kernel(**inputs):
    xs = np.asarray(inputs["x_sample"], dtype=np.float32)
    xp = np.asarray(inputs["x_prompt"], dtype=np.float32)
    Ss, Sp = xs.shape[1], xp.shape[1]
    wall, pall, cosT, sinT = host_pack(inputs, max(Ss, Sp))
    nc = build_program([("s", Ss), ("p", Sp)])
    in_maps = []
    for c in range(N_CORES):
        in_maps.append({
            "wall": wall, "pall": pall, "cosT": cosT, "sinT": sinT,
            "x_s": np.ascontiguousarray(xs[c].T),
            "x_p": np.ascontiguousarray(xp[c // 2].T),
        })
    res = run_bass_kernel_spmd(nc, in_maps, core_ids=list(range(N_CORES)))
    y_s = np.stack([np.ascontiguousarray(res.results[c]["y_s"].T) for c in range(N_CORES)], axis=0)
    y_p = np.stack([np.ascontiguousarray(res.results[2 * i]["y_p"].T) for i in range(xp.shape[0])], axis=0)
    return (y_p.astype(np.float32), y_s.astype(np.float32))
```

```python
import contextlib
import math
import numpy as np
import concourse.bass as bass
import concourse.mybir as mybir
from concourse.bass_utils import run_bass_kernel_spmd

F32 = mybir.dt.float32
BF16 = mybir.dt.bfloat16
AF = mybir.ActivationFunctionType
ALU = mybir.AluOpType

D = 1024
KC = 8
DEPTH = 4
DFF = 2816
NPAIR = 22
H = 8
NORM_EPS = 1e-6
SUBLN_EPS = 1e-5
ROPE_THETA = 10000.0
TT = 512
GELU_C = 0.044715


class Buf:
    __slots__ = ("name", "w", "r", "excl", "merge", "dsem")

    def __init__(self, name, excl=False, merge=False):
        self.name = name
        self.w = {}
        self.r = {}
        self.excl = excl
        self.merge = merge
        self.dsem = {}


class Trk:
    EPOCH = 30000

    def __init__(self, nc, es):
        self.nc = nc
        self.es = es
        self.eng = {"pe": nc.tensor, "act": nc.scalar, "dve": nc.vector, "pool": nc.gpsimd, "sp": nc.sync}
        self.semh = {}
        self.cnt = {e: 0 for e in self.eng}
        self.epoch = {e: 0 for e in self.eng}
        self.cur = {}
        self.waited = {e: {} for e in self.eng}
        self.dval = {}
        self.active = []
        self.dfree = {e: [] for e in self.eng}
        self.nd = 0
        self.nwait = 0
        self.nins = {e: 0 for e in self.eng}
        for e in self.eng:
            self._newsem(e)

    def _mk(self, name):
        self.semh[name] = self.es.enter_context(self.nc.semaphore(name))
        return name

    def _newsem(self, e):
        self.cur[e] = self._mk("q_%s_%d" % (e, self.epoch[e]))
        self.cnt[e] = 0

    def _deps(self, reads, writes):
        deps = {}
        reads = [b for b in reads if not b.merge]
        writes = [b for b in writes if not b.merge]
        for b in reads:
            for s, v in b.w.items():
                if deps.get(s, 0) < v:
                    deps[s] = v
            if b.excl:
                for s, v in b.r.items():
                    if deps.get(s, 0) < v:
                        deps[s] = v
        for b in writes:
            for s, v in b.w.items():
                if deps.get(s, 0) < v:
                    deps[s] = v
            for s, v in b.r.items():
                if deps.get(s, 0) < v:
                    deps[s] = v
        return deps

    def _wait(self, e, deps):
        engine = self.eng[e]
        wd = self.waited[e]
        own = self.cur[e]
        for s, v in deps.items():
            if e == "pe" and s == own:
                continue
            if wd.get(s, 0) >= v:
                continue
            engine.wait_ge(self.semh[s], v)
            wd[s] = v
            self.nwait += 1

    def _record(self, ev, reads, writes):
        s, v = ev
        for b in writes:
            if b.merge:
                continue
            b.w = {s: v}
            b.r = {}
        for b in reads:
            if b.merge:
                continue
            if b.excl:
                b.w = {s: v}
                b.r = {}
            elif b.r.get(s, 0) < v:
                b.r[s] = v

    def op(self, e, fn, reads=(), writes=()):
        if self.cnt[e] >= self.EPOCH:
            self.epoch[e] += 1
            self._newsem(e)
        self._wait(e, self._deps(reads, writes))
        ins = fn(self.eng[e])
        self.cnt[e] += 1
        ev = (self.cur[e], self.cnt[e])
        ins.then_inc(self.semh[ev[0]], 1)
        self._record(ev, reads, writes)
        self.nins[e] += 1
        return ins

    def dma(self, q, dst, dst_ap, src, src_ap, side, slow=False):
        if q not in side.dsem:
            if self.dfree[q]:
                side.dsem[q] = self.dfree[q].pop()
            else:
                side.dsem[q] = self._mk("d%s%d" % (q, self.nd))
                self.nd += 1
                self.dval[side.dsem[q]] = 0
            self.active.append((side, q))
        name = side.dsem[q]
        self._wait(q, self._deps((src,), (dst,)))
        if slow:
            ins = self.eng[q].dma_start(out=dst_ap, in_=src_ap, allow_slow_non_contiguous=True)
        else:
            ins = self.eng[q].dma_start(out=dst_ap, in_=src_ap)
        self.dval[name] += 16
        ev = (name, self.dval[name])
        ins.then_inc(self.semh[name], 16)
        self._record(ev, (src,), (dst,))
        self.nins[q] += 1
        return ins

    def barrier(self, bufs=(), release=True):
        deps = {}
        for e in self.eng:
            if self.cnt[e] > 0:
                deps[self.cur[e]] = self.cnt[e]
        for b, q in self.active:
            deps[b.dsem[q]] = self.dval[b.dsem[q]]
        for e in self.eng:
            self._wait(e, dict(deps))
        if release:
            for b, q in self.active:
                self.dfree[q].append(b.dsem.pop(q))
            self.active = []


class Lay:
    def __init__(self):
        self.off = {}
        self.n = 0

    def add(self, name, ncols):
        self.off[name] = (self.n, ncols)
        self.n += ncols


def make_layouts():
    wl = Lay()
    for j in range(2):
        wl.add("g%d_wu" % j, 8 * 8 * 128)
        wl.add("g%d_wv" % j, 8 * 1024)
        wl.add("g%d_ws" % j, 8 * 128)
        wl.add("g%d_wo" % j, 8 * 8 * 128)
    for j in range(2):
        wl.add("a%d_wqk" % j, 16 * 2 * 8 * 128)
        wl.add("a%d_wv" % j, 8 * 1024)
        wl.add("a%d_wo" % j, 8 * 8 * 128)
    for i in range(DEPTH):
        wl.add("f%d_win" % i, 44 * 8 * 128)
        wl.add("f%d_wout" % i, 8 * NPAIR * 128)
    pl = Lay()
    pl.add("nmix", DEPTH * 8)
    pl.add("nffn", DEPTH * 8)
    pl.add("nfin", 8)
    pl.add("vgain", 2 * 8)
    pl.add("convw", DEPTH * 44 * 4)
    pl.add("bsrep", 2 * 8 * 128)
    pl.add("subg", 2 * 128)
    pl.add("lam", 2 * 4 * 64)
    pl.add("ident", 128)
    return wl, pl


def pack_lhsT(w):
    K, M = w.shape
    return w.reshape(K // 128, 128, M // 128, 128).transpose(1, 2, 0, 3).reshape(128, -1)


def pack_rhs(w):
    K, N = w.shape
    return w.reshape(K // 128, 128, N).transpose(1, 0, 2).reshape(128, -1)


def pack_wout(w):
    K, N = w.shape
    return w.reshape(K // 128, 128, N // 128, 128).transpose(1, 2, 0, 3).reshape(128, -1)


def host_pack(inp, smax):
    wl, pl = make_layouts()
    wall = np.empty((128, wl.n), np.float32)
    pall = np.empty((128, pl.n), np.float32)

    def putw(name, arr):
        o, n = wl.off[name]
        assert arr.shape == (128, n), (name, arr.shape, n)
        wall[:, o:o + n] = arr

    def putp(name, arr):
        o, n = pl.off[name]
        assert arr.shape == (128, n), (name, arr.shape, n)
        pall[:, o:o + n] = arr

    for j in range(2):
        w = np.asarray(inp["gmlp_w_in"][j])
        putw("g%d_wu" % j, pack_lhsT(w[:, :1024]))
        putw("g%d_wv" % j, pack_rhs(w[:, 1024:]))
        putw("g%d_ws" % j, np.asarray(inp["gmlp_w_s"][j]).transpose(2, 0, 1).reshape(128, -1))
        putw("g%d_wo" % j, pack_wout(np.asarray(inp["gmlp_w_out"][j])))
    swap = np.arange(128)
    swap = (swap // 64) * 64 + ((swap % 64) + 32) % 64
    for j in range(2):
        w = np.asarray(inp["diff_w_qkv"][j])
        qk = w[:, :2048].reshape(1024, 16, 128)
        both = np.stack([qk, qk[:, :, swap]], axis=2)
        arr = both.reshape(8, 128, 16, 2, 128).transpose(1, 2, 3, 0, 4)
        putw("a%d_wqk" % j, arr.reshape(128, -1))
        putw("a%d_wv" % j, pack_rhs(w[:, 2048:]))
        putw("a%d_wo" % j, pack_wout(np.asarray(inp["diff_w_out"][j])))
    for i in range(DEPTH):
        w = np.asarray(inp["ffn_w_in"][i])
        wp = np.stack([w[:, :DFF].reshape(1024, NPAIR, 128), w[:, DFF:].reshape(1024, NPAIR, 128)], axis=2)
        putw("f%d_win" % i, pack_lhsT(wp.reshape(1024, 2 * DFF)))
        putw("f%d_wout" % i, pack_wout(np.asarray(inp["ffn_w_out"][i])))

    def pervec(v):
        v = np.asarray(v).reshape(-1, 8, 128)
        return v.transpose(2, 0, 1).reshape(128, -1)

    putp("nmix", pervec(inp["norm_mix"]))
    putp("nffn", pervec(inp["norm_ffn"]))
    putp("nfin", pervec(inp["norm_final"]))
    putp("vgain", pervec(inp["gmlp_v_gain"]))
    cw = np.asarray(inp["ffn_conv_w"])
    cb = np.asarray(inp["ffn_conv_b"])
    c4 = np.concatenate([cw, cb[:, None, :]], axis=1)
    c4 = c4.reshape(DEPTH, 4, 2, NPAIR, 128)
    putp("convw", c4.transpose(4, 0, 3, 2, 1).reshape(128, -1))
    bs = np.asarray(inp["gmlp_b_s"])
    bsr = np.broadcast_to(bs[None, :, :, :], (128, 2, 8, 128))
    putp("bsrep", np.ascontiguousarray(bsr).reshape(128, -1))
    sg = np.asarray(inp["diff_subln_g"])
    putp("subg", np.ascontiguousarray(np.broadcast_to(sg[None], (128, 2, 128))).reshape(128, -1))
    lam = np.stack([np.asarray(inp["diff_lam_q1"]), np.asarray(inp["diff_lam_k1"]),
                    np.asarray(inp["diff_lam_q2"]), np.asarray(inp["diff_lam_k2"])], axis=1)
    putp("lam", np.ascontiguousarray(np.broadcast_to(lam[None], (128, 2, 4, 64))).reshape(128, -1))
    putp("ident", np.eye(128, dtype=np.float32))
    pos = np.arange(smax, dtype=np.float32)
    inv_freq = (ROPE_THETA ** (-np.arange(0, 64, 2, dtype=np.float32) / np.float32(64))).astype(np.float32)
    ang = (pos[:, None] * inv_freq[None, :]).astype(np.float32)
    dd = np.arange(128) % 64
    cosT = np.cos(ang).astype(np.float32).T[dd % 32, :]
    sinT = np.sin(ang).astype(np.float32).T[dd % 32, :]
    sgn = np.where(dd < 32, -1.0, 1.0).astype(np.float32)[:, None]
    return wall, pall, np.ascontiguousarray(cosT), np.ascontiguousarray(sinT * sgn)


def build_program(seqs):
    wl, pl = make_layouts()
    smax = max(s for _, s in seqs)
    nc = bass.Bass("TRN2", target_bir_lowering=False)
    wall = nc.dram_tensor("wall", [128, wl.n], F32, kind="ExternalInput").ap()
    pall_d = nc.dram_tensor("pall", [128, pl.n], F32, kind="ExternalInput").ap()
    cos_d = nc.dram_tensor("cosT", [128, smax], F32, kind="ExternalInput").ap()
    sin_d = nc.dram_tensor("sinT", [128, smax], F32, kind="ExternalInput").ap()
    wbf = nc.dram_tensor("wbf", [128, wl.n], BF16, kind="Internal").ap()
    Bwall = Buf("wall", merge=True)
    Bwbf = Buf("wbf", merge=True)
    Bconst = Buf("constd", merge=True)
    SQ = {}
    for tag, S in seqs:
        d = {}
        d["S"] = S
        d["xin"] = nc.dram_tensor("x_" + tag, [D, S], F32, kind="ExternalInput").ap()
        d["yout"] = nc.dram_tensor("y_" + tag, [D, S], F32, kind="ExternalOutput").ap()
        d["xa"] = nc.dram_tensor("xa_" + tag, [D, S], F32, kind="Internal").ap()
        d["xb"] = nc.dram_tensor("xb_" + tag, [D, S], F32, kind="Internal").ap()
        d["qk"] = nc.dram_tensor("qk_" + tag, [16, 128, S], BF16, kind="Internal").ap()
        d["vs"] = nc.dram_tensor("vs_" + tag, [H, S, 128], BF16, kind="Internal").ap()
        d["ot"] = nc.dram_tensor("ot_" + tag, [H, 128, S], BF16, kind="Internal").ap()
        for k in ("xin", "yout", "xa", "xb", "qk", "vs", "ot"):
            d["B" + k] = Buf(k + "_" + tag, merge=True)
        SQ[tag] = d

    uid = [0]

    def sbt(name, shape, dt):
        uid[0] += 1
        return nc.sbuf_tensor("%s_u%d" % (name, uid[0]), shape, dt)

    def pst_(name, shape, dt):
        uid[0] += 1
        return nc.psum_tensor("%s_u%d" % (name, uid[0]), shape, dt)

    es = contextlib.ExitStack()
    with es:
        T = Trk(nc, es)

        def wslice(name, a=0, n=None):
            o, tot = wl.off[name]
            if n is None:
                n = tot - a
            return wbf[:, o + a:o + a + n]

        pall = es.enter_context(sbt("pall_sb", [128, pl.n], F32))
        Bpall = Buf("pall")
        T.dma("sp", Bpall, pall[:], Bconst, pall_d[:, :], Bpall)

        def pcol(name, a, n=1):
            o, _ = pl.off[name]
            return pall[:, o + a:o + a + n]

        ident = es.enter_context(sbt("ident", [128, 128], BF16))
        ones32 = es.enter_context(sbt("ones32", [128, 128], F32))
        lamt = es.enter_context(sbt("lamt", [128, 16], F32))
        subgs = es.enter_context(sbt("subgs", [128, 2, 128], F32))
        Bmisc = Buf("misc")
        T.op("dve", lambda e: e.tensor_copy(out=ident[:], in_=pcol("ident", 0, 128)), reads=[Bpall], writes=[Bmisc])
        T.op("pool", lambda e: e.memset(ones32[:], 1.0), writes=[Bmisc])
        mhalf = es.enter_context(sbt("mhalf", [128, 4], F32))
        T.op("pool", lambda e: e.memset(mhalf[:], -0.5), writes=[Bmisc])
        lam_inits = []
        for j in range(2):
            li = 0.8 - 0.6 * math.exp(-0.3 * (2 * j + 1))
            lam_inits.append(li)
            lo = pl.off["lam"][0] + j * 256
            for w_ in range(2):
                T.op("dve", lambda e: e.tensor_tensor(out=subgs[:, 0, 0:64], in0=pall[:, lo + w_ * 128:lo + w_ * 128 + 64],
                                                      in1=pall[:, lo + w_ * 128 + 64:lo + w_ * 128 + 128], op=ALU.mult),
                     reads=[Bpall, Bmisc], writes=[Bmisc])
                T.op("dve", lambda e: e.reduce_sum(out=lamt[:, 8 + w_:9 + w_], in_=subgs[:, 0, 0:64], axis=mybir.AxisListType.X),
                     reads=[Bmisc], writes=[Bmisc])
            T.op("act", lambda e: e.activation(out=lamt[:, 10:12], in_=lamt[:, 8:10], func=AF.Exp), reads=[Bmisc], writes=[Bmisc])
            T.op("dve", lambda e: e.scalar_tensor_tensor(out=lamt[:, j:j + 1], in0=lamt[:, 11:12], scalar=-li, in1=lamt[:, 10:11],
                                                         op0=ALU.add, op1=ALU.subtract), reads=[Bmisc], writes=[Bmisc])
        for j in range(2):
            T.op("dve", lambda e: e.tensor_scalar(out=subgs[:, j, :], in0=pcol("subg", j * 128, 128), scalar1=1.0 - lam_inits[j],
                                                  scalar2=None, op0=ALU.mult), reads=[Bpall, Bmisc], writes=[Bmisc])
        T.barrier([Bpall])

        def phase_cast():
            CW = 4096
            with contextlib.ExitStack() as ph:
                NS = 3
                f = [ph.enter_context(sbt("cf%d" % i, [128, CW], F32)) for i in range(NS)]
                b = [ph.enter_context(sbt("cb%d" % i, [128, CW], BF16)) for i in range(NS)]
                Bf = [Buf("cf%d" % i) for i in range(NS)]
                Bb = [Buf("cb%d" % i) for i in range(NS)]
                nch = (wl.n + CW - 1) // CW
                for c in range(nch):
                    s = c % NS
                    a = c * CW
                    n = min(CW, wl.n - a)
                    T.dma("sp", Bf[s], f[s][:, 0:n], Bwall, wall[:, a:a + n], Bf[s])
                    if c % 2 == 0:
                        T.op("dve", lambda e: e.tensor_copy(out=b[s][:, 0:n], in_=f[s][:, 0:n]), reads=[Bf[s]], writes=[Bb[s]])
                    else:
                        T.op("act", lambda e: e.activation(out=b[s][:, 0:n], in_=f[s][:, 0:n], func=AF.Copy), reads=[Bf[s]], writes=[Bb[s]])
                    T.dma("pool", Bwbf, wbf[:, a:a + n], Bb[s], b[s][:, 0:n], Bb[s])
                T.barrier(Bf + Bb)

        phase_cast()

        def xT(ap, t0, n):
            return ap.rearrange("(kc p) t -> p kc t", p=128)[:, :, t0:t0 + n]

        class NormCtx:
            def __init__(self, ph, nslot=2):
                self.ns = nslot
                self.xin = [ph.enter_context(sbt("n_xin%d" % i, [128, KC, TT], F32)) for i in range(nslot)]
                self.hT = [ph.enter_context(sbt("n_hT%d" % i, [128, KC, TT], BF16)) for i in range(nslot)]
                self.sq = ph.enter_context(sbt("n_sq", [128, 2, TT], F32))
                self.rs = ph.enter_context(sbt("n_rs", [128, TT], F32))
                self.ps = ph.enter_context(pst_("n_ps", [128, TT], F32))
                self.Bxin = [Buf("n_xin%d" % i) for i in range(nslot)]
                self.BhT = [Buf("n_hT%d" % i) for i in range(nslot)]
                self.Bsq = [Buf("n_sq0"), Buf("n_sq1")]
                self.Brs = Buf("n_rs")
                self.Bps = Buf("n_ps", excl=True)

            def bufs(self):
                return self.Bxin + self.BhT + self.Bsq + [self.Brs]

            def emit(self, slot, srcB, src_ap, t0, n, gname, gidx, out_f32=None):
                xin, hT = self.xin[slot], self.hT[slot]
                Bxin, BhT = self.Bxin[slot], self.BhT[slot]
                T.dma("sp", Bxin, xin[:, :, 0:n], srcB, xT(src_ap, t0, n), Bxin)
                for kc in range(KC):
                    T.op("act", lambda e: e.activation(out=self.sq[:, kc % 2, 0:n], in_=xin[:, kc, 0:n], func=AF.Square),
                         reads=[Bxin], writes=[self.Bsq[kc % 2]])
                    T.op("pe", lambda e: e.matmul(self.ps[:, 0:n], ones32[:], self.sq[:, kc % 2, 0:n], start=(kc == 0), stop=(kc == KC - 1)),
                         reads=[self.Bsq[kc % 2], Bmisc], writes=[self.Bps])
                T.op("act", lambda e: e.activation(out=self.rs[:, 0:n], in_=self.ps[:, 0:n], func=AF.Ln, scale=1.0 / D, bias=NORM_EPS),
                     reads=[self.Bps], writes=[self.Brs])
                T.op("act", lambda e: e.activation(out=self.rs[:, 0:n], in_=self.rs[:, 0:n], func=AF.Exp, scale=-0.5),
                     reads=[self.Brs], writes=[self.Brs])
                for kc in range(KC):
                    dst = hT[:, kc, 0:n] if out_f32 is None else out_f32[:, kc, 0:n]
                    T.op("dve", lambda e: e.scalar_tensor_tensor(out=dst, in0=xin[:, kc, 0:n], scalar=pcol(gname, gidx * 8 + kc),
                                                                 in1=self.rs[:, 0:n], op0=ALU.mult, op1=ALU.mult),
                         reads=[Bxin, self.Brs, Bpall], writes=[BhT])

        def phase_ffn(sq, li, srcB, src, dstB, dst):
            S = sq["S"]
            J = S // TT
            with contextlib.ExitStack() as ph:
                N = NormCtx(ph)
                NW = 3
                win = [ph.enter_context(sbt("f_win%d" % i, [128, 2, KC, 128], BF16)) for i in range(NW)]
                Bwin = [Buf("f_win%d" % i) for i in range(NW)]
                wout = ph.enter_context(sbt("f_wout", [128, KC, NPAIR, 128], BF16))
                Bwout = [Buf("f_wout%d" % i) for i in range(KC)]
                NA = 3
                aext = [ph.enter_context(sbt("f_aext%d" % i, [128, 2, TT + 2], F32)) for i in range(NA)]
                Baext = [Buf("f_aext%d" % i) for i in range(NA)]
                ct = [ph.enter_context(sbt("f_ct%d" % i, [128, 2, TT], F32)) for i in range(NA)]
                Bct = [Buf("f_ct%d" % i) for i in range(NA)]
                sg = [ph.enter_context(sbt("f_sg%d" % i, [128, TT], F32)) for i in range(NA)]
                Bsg = [Buf("f_sg%d" % i) for i in range(NA)]
                carry = ph.enter_context(sbt("f_carry", [128, 2 * NPAIR, 2], F32))
                Bcarry = [Buf("f_carry%d" % i) for i in range(NPAIR)]
                yT = ph.enter_context(sbt("f_yT", [128, NPAIR, TT], BF16))
                ByT = [Buf("f_yT%d" % i) for i in range(NPAIR)]
                xres = ph.enter_context(sbt("f_xres", [128, KC, TT], F32))
                Bxres = Buf("f_xres")
                psa = [ph.enter_context(pst_("f_psa%d" % i, [128, 2, TT], F32)) for i in range(2)]
                Bpsa = [[Buf("f_psa%d_%d" % (i, h), excl=True) for h in range(2)] for i in range(2)]
                pso = [ph.enter_context(pst_("f_pso%d" % i, [128, TT], F32)) for i in range(2)]
                Bpso = [Buf("f_pso%d" % i, excl=True) for i in range(2)]
                cwo = pl.off["convw"][0] + li * 44 * 4

                T.op("pool", lambda e: e.memset(carry[:], 0.0), writes=Bcarry)
                ctr = [0, 0, 0]
                pend = []
                wout_loaded = [False]

                def win_stage(j, slot):
                    virt = (j == J)
                    hT, BhT = N.hT[slot], N.BhT[slot]
                    n = 1 if virt else TT
                    prev = None
                    for pr in range(NPAIR + 1):
                        if pr == 3:
                            while pend:
                                pend.pop()()
                        if pr == NPAIR // 2 and j + 1 < J:
                            N.emit((j + 1) % 2, srcB, src, (j + 1) * TT, TT, "nffn", li)
                        if pr < NPAIR:
                            ws = ctr[0] % NW
                            ctr[0] += 1
                            a = ctr[1] % NA
                            pb = ctr[1] % 2
                            ctr[1] += 1
                            if not virt:
                                T.dma("sp", Bwin[ws], win[ws][:].rearrange("p a b c -> p (a b c)"), Bwbf,
                                      wslice("f%d_win" % li, pr * 2048, 2048), Bwin[ws])
                                for hf in range(2):
                                    for kc in range(KC):
                                        T.op("pe", lambda e: e.matmul(psa[pb][:, hf, :], win[ws][:, hf, kc, :], hT[:, kc, :],
                                                                      start=(kc == 0), stop=(kc == KC - 1)),
                                             reads=[Bwin[ws], BhT], writes=[Bpsa[pb][hf]])
                            T.op("pool", lambda e: e.tensor_copy(out=aext[a][:, :, 0:2], in_=carry[:, 2 * pr:2 * pr + 2, :]),
                                 reads=[Bcarry[pr]], writes=[Baext[a]])
                            if virt:
                                T.op("pool", lambda e: e.memset(aext[a][:, :, 2:TT + 2], 0.0), writes=[Baext[a]])
                            else:
                                T.op("act", lambda e: e.activation(out=aext[a][:, :, 2:TT + 2], in_=psa[pb][:, :, :], func=AF.Copy),
                                     reads=Bpsa[pb], writes=[Baext[a]])
                                T.op("pool", lambda e: e.tensor_copy(out=carry[:, 2 * pr:2 * pr + 2, :], in_=aext[a][:, :, TT:TT + 2]),
                                     reads=[Baext[a]], writes=[Bcarry[pr]])
                            for hf in range(2):
                                co = cwo + (pr * 2 + hf) * 4
                                T.op("act", lambda e: e.activation(out=ct[a][:, hf, 0:n], in_=aext[a][:, hf, 0:n], func=AF.Identity,
                                                                   scale=pall[:, co:co + 1], bias=pall[:, co + 3:co + 4]),
                                     reads=[Baext[a], Bpall], writes=[Bct[a]])
                        if prev is not None:
                            pa, ppr = prev
                            T.op("act", lambda e: e.activation(out=sg[pa][:, 0:n], in_=ct[pa][:, 0, 0:n], func=AF.Silu),
                                 reads=[Bct[pa]], writes=[Bsg[pa]])
                        if pr < NPAIR:
                            for hf in range(2):
                                co = cwo + (pr * 2 + hf) * 4
                                for t in (1, 2):
                                    T.op("dve", lambda e: e.scalar_tensor_tensor(out=ct[a][:, hf, 0:n], in0=aext[a][:, hf, t:t + n],
                                                                                 scalar=pall[:, co + t:co + t + 1], in1=ct[a][:, hf, 0:n],
                                                                                 op0=ALU.mult, op1=ALU.add),
                                         reads=[Baext[a], Bct[a], Bpall], writes=[Bct[a]])
                        if prev is not None:
                            pa, ppr = prev
                            T.op("dve", lambda e: e.tensor_tensor(out=yT[:, ppr, 0:n], in0=sg[pa][:, 0:n], in1=ct[pa][:, 1, 0:n], op=ALU.mult),
                                 reads=[Bsg[pa], Bct[pa]], writes=[ByT[ppr]])
                        prev = (a, pr) if pr < NPAIR else None

                def wout_stage(j):
                    virt = (j == J)
                    lo = 1 if j == 0 else 0
                    hi = 1 if virt else TT
                    n = hi - lo
                    tok0 = j * TT - 1 + lo
                    T.dma("sp", Bxres, xres[:, :, 0:n], srcB, xT(src, tok0, n), Bxres, slow=(n == 1))
                    for dc in range(KC):
                        o = dc % 2
                        if not wout_loaded[0]:
                            T.dma("sp", Bwout[dc], wout[:, dc, :, :].rearrange("p a b -> p (a b)"), Bwbf,
                                  wslice("f%d_wout" % li, dc * NPAIR * 128, NPAIR * 128), Bwout[dc])
                        for pr in range(NPAIR):
                            T.op("pe", lambda e: e.matmul(pso[o][:, 0:n], wout[:, dc, pr, :], yT[:, pr, lo:hi],
                                                          start=(pr == 0), stop=(pr == NPAIR - 1)),
                                 reads=[Bwout[dc], ByT[pr]], writes=[Bpso[o]])
                        T.op("dve", lambda e: e.tensor_tensor(out=xres[:, dc, 0:n], in0=pso[o][:, 0:n], in1=xres[:, dc, 0:n], op=ALU.add),
                             reads=[Bpso[o], Bxres], writes=[Bxres])
                    wout_loaded[0] = True
                    pend.append(lambda: T.dma("sp", dstB, xT(dst, tok0, n), Bxres, xres[:, :, 0:n], Bxres, slow=(n == 1)))

                N.emit(0, srcB, src, 0, TT, "nffn", li)
                for j in range(J + 1):
                    win_stage(j, j % 2)
                    wout_stage(j)
                while pend:
                    pend.pop()()
                T.barrier()

        def phase_gmlp(sq, li, gj, srcB, src, dstB, dst):
            S = sq["S"]
            J = S // TT
            with contextlib.ExitStack() as ph:
                N = NormCtx(ph)
                wu = ph.enter_context(sbt("g_wu", [128, 8, KC, 128], BF16))
                wv = ph.enter_context(sbt("g_wv", [128, KC, 1024], BF16))
                ws = ph.enter_context(sbt("g_ws", [128, 8, 128], BF16))
                wo = ph.enter_context(sbt("g_wo", [128, 8, KC, 128], BF16))
                Bw = Buf("g_w")
                Bw2 = [Buf("g_w2_%d" % i) for i in range(3)]
                uT = ph.enter_context(sbt("g_uT", [128, 8, TT], F32))
                BuT = [Buf("g_uT%d" % i) for i in range(8)]
                vg = [ph.enter_context(sbt("g_vg%d" % i, [128, 1024], F32)) for i in range(2)]
                Bvg = [Buf("g_vg%d" % i) for i in range(2)]
                vsq = ph.enter_context(sbt("g_vsq", [128, 1024], BF16))
                Bvsq = Buf("g_vsq")
                vn = ph.enter_context(sbt("g_vn", [128, 4, 1024], BF16))
                Bvn = [Buf("g_vn%d" % i) for i in range(4)]
                st = [ph.enter_context(sbt("g_st%d" % i, [128, 8], F32)) for i in range(2)]
                Bst = [Buf("g_st%d" % i) for i in range(2)]
                v2 = [ph.enter_context(sbt("g_v2_%d" % i, [128, TT], F32)) for i in range(2)]
                Bv2 = [Buf("g_v2_%d" % i) for i in range(2)]
                yT = ph.enter_context(sbt("g_yT", [128, 8, TT], BF16))
                ByT = [Buf("g_yT%d" % i) for i in range(8)]
                psu = [ph.enter_context(pst_("g_psu%d" % i, [128, TT], F32)) for i in range(2)]
                Bpsu = [Buf("g_psu%d" % i, excl=True) for i in range(2)]
                psv = [ph.enter_context(pst_("g_psv%d" % i, [128, TT], F32)) for i in range(2)]
                Bpsv = [Buf("g_psv%d" % i, excl=True) for i in range(2)]
                pss = [ph.enter_context(pst_("g_pss%d" % i, [128, TT], F32)) for i in range(2)]
                Bpss = [Buf("g_pss%d" % i, excl=True) for i in range(2)]
                T.dma("sp", Bw, wu[:].rearrange("p a b c -> p (a b c)"), Bwbf, wslice("g%d_wu" % gj), Bw)
                T.dma("sp", Bw2[0], wv[:].rearrange("p a b -> p (a b)"), Bwbf, wslice("g%d_wv" % gj), Bw2[0])
                T.dma("sp", Bw2[1], ws[:].rearrange("p a b -> p (a b)"), Bwbf, wslice("g%d_ws" % gj), Bw2[1])
                T.dma("sp", Bw2[2], wo[:].rearrange("p a b c -> p (a b c)"), Bwbf, wslice("g%d_wo" % gj), Bw2[2])
                bso = pl.off["bsrep"][0] + gj * 8 * 128
                for j in range(J):
                    slot = j % 2
                    N.emit(slot, srcB, src, j * TT, TT, "nmix", li)
                    hT, BhT, xin, Bxin = N.hT[slot], N.BhT[slot], N.xin[slot], N.Bxin[slot]
                    for m in range(8):
                        o = m % 2
                        for kc in range(KC):
                            T.op("pe", lambda e: e.matmul(psu[o][:], wu[:, m, kc, :], hT[:, kc, :], start=(kc == 0), stop=(kc == KC - 1)),
                                 reads=[Bw, BhT], writes=[Bpsu[o]])
                        T.op("act", lambda e: e.activation(out=uT[:, m, :], in_=psu[o][:], func=AF.Gelu_apprx_tanh),
                             reads=[Bpsu[o]], writes=[BuT[m]])
                    for sub in range(4):
                        a = sub % 2
                        for hf in range(2):
                            for kc in range(KC):
                                T.op("pe", lambda e: e.matmul(psv[hf][:], hT[:, kc, sub * 128:(sub + 1) * 128], wv[:, kc, hf * 512:(hf + 1) * 512],
                                                              start=(kc == 0), stop=(kc == KC - 1)),
                                     reads=[Bw2[0], BhT], writes=[Bpsv[hf]])
                            T.op("act", lambda e: e.activation(out=vg[a][:, hf * 512:(hf + 1) * 512], in_=psv[hf][:], func=AF.Gelu_apprx_tanh),
                                 reads=[Bpsv[hf]], writes=[Bvg[a]])
                        T.op("act", lambda e: e.activation(out=vsq[:], in_=vg[a][:], func=AF.Square, accum_out=st[a][:, 0:1]),
                             reads=[Bvg[a]], writes=[Bvsq, Bst[a]])
                        T.op("act", lambda e: e.activation(out=st[a][:, 1:2], in_=st[a][:, 0:1], func=AF.Ln, scale=1.0 / 1024, bias=NORM_EPS),
                             reads=[Bst[a]], writes=[Bst[a]])
                        T.op("act", lambda e: e.activation(out=st[a][:, 2:3], in_=st[a][:, 1:2], func=AF.Exp, scale=-0.5),
                             reads=[Bst[a]], writes=[Bst[a]])
                        T.op("dve", lambda e: e.tensor_scalar(out=vn[:, sub, :], in0=vg[a][:], scalar1=st[a][:, 2:3], scalar2=None, op0=ALU.mult),
                             reads=[Bvg[a], Bst[a]], writes=[Bvn[sub]])
                    for g in range(8):
                        o = g % 2
                        for sub in range(4):
                            T.op("pe", lambda e: e.matmul(pss[o][:, sub * 128:(sub + 1) * 128], vn[:, sub, g * 128:(g + 1) * 128], ws[:, g, :],
                                                          start=True, stop=True, skip_group_check=True),
                                 reads=[Bw2[1], Bvn[sub]], writes=[Bpss[o]])
                        for sub in range(4):
                            T.op("dve", lambda e: e.scalar_tensor_tensor(out=v2[o][:, sub * 128:(sub + 1) * 128], in0=pss[o][:, sub * 128:(sub + 1) * 128],
                                                                         scalar=pcol("vgain", gj * 8 + g),
                                                                         in1=pall[:, bso + g * 128:bso + (g + 1) * 128], op0=ALU.mult, op1=ALU.add),
                                 reads=[Bpss[o], Bpall], writes=[Bv2[o]])
                        T.op("pool", lambda e: e.tensor_tensor(out=yT[:, g, :], in0=uT[:, g, :], in1=v2[o][:], op=ALU.mult),
                             reads=[BuT[g], Bv2[o]], writes=[ByT[g]])
                    for dc in range(KC):
                        o = dc % 2
                        for g in range(8):
                            T.op("pe", lambda e: e.matmul(psu[o][:], wo[:, dc, g, :], yT[:, g, :], start=(g == 0), stop=(g == 7)),
                                 reads=[Bw2[2], ByT[g]], writes=[Bpsu[o]])
                        T.op("dve", lambda e: e.tensor_tensor(out=xin[:, dc, :], in0=psu[o][:], in1=xin[:, dc, :], op=ALU.add),
                             reads=[Bpsu[o], Bxin], writes=[Bxin])
                    T.dma("pool", dstB, xT(dst, j * TT, TT), Bxin, xin[:], Bxin)
                T.barrier()

        def phase_qkv(sq, li, aj, srcB, src):
            S = sq["S"]
            J = S // TT
            with contextlib.ExitStack() as ph:
                N = NormCtx(ph)
                NWS = 3
                wqk = [ph.enter_context(sbt("a_wqk%d" % i, [128, 2, KC, 128], BF16)) for i in range(NWS)]
                Bwqk = [Buf("a_wqk%d" % i) for i in range(NWS)]
                wv = ph.enter_context(sbt("a_wv", [128, KC, 1024], BF16))
                Bwv = Buf("a_wv")
                cs = [ph.enter_context(sbt("a_cs%d" % i, [128, 2, TT], F32)) for i in range(2)]
                Bcs = [Buf("a_cs%d" % i) for i in range(2)]
                t1 = [ph.enter_context(sbt("a_t1_%d" % i, [128, 2, TT], F32)) for i in range(2)]
                Bt1 = [Buf("a_t1_%d" % i) for i in range(2)]
                qo = [ph.enter_context(sbt("a_qo%d" % i, [128, TT], BF16)) for i in range(3)]
                Bqo = [Buf("a_qo%d" % i) for i in range(3)]
                vt = [ph.enter_context(sbt("a_vt%d" % i, [128, 1024], BF16)) for i in range(2)]
                Bvt = [Buf("a_vt%d" % i) for i in range(2)]
                psz = [ph.enter_context(pst_("a_psz%d" % i, [128, 2, TT], F32)) for i in range(2)]
                Bpsz = [[Buf("a_psz%d_%d" % (i, v), excl=True) for v in range(2)] for i in range(2)]
                psv = [ph.enter_context(pst_("a_psv%d" % i, [128, TT], F32)) for i in range(2)]
                Bpsv = [Buf("a_psv%d" % i, excl=True) for i in range(2)]
                T.dma("sp", Bwv, wv[:].rearrange("p a b -> p (a b)"), Bwbf, wslice("a%d_wv" % aj), Bwv)
                c0 = 0
                c1 = 0
                N.emit(0, srcB, src, 0, TT, "nmix", li)
                for j in range(J):
                    slot = j % 2
                    t0 = j * TT
                    hT, BhT = N.hT[slot], N.BhT[slot]
                    T.dma("sp", Bcs[slot], cs[slot][:, 0, :], Bconst, cos_d[:, t0:t0 + TT], Bcs[slot])
                    T.dma("sp", Bcs[slot], cs[slot][:, 1, :], Bconst, sin_d[:, t0:t0 + TT], Bcs[slot])
                    for m in range(16):
                        ws_ = c0 % NWS
                        a = c0 % 2
                        q = c0 % 3
                        c0 += 1
                        T.dma("sp", Bwqk[ws_], wqk[ws_][:].rearrange("p a b c -> p (a b c)"), Bwbf,
                              wslice("a%d_wqk" % aj, m * 2048, 2048), Bwqk[ws_])
                        for var in range(2):
                            for kc in range(KC):
                                T.op("pe", lambda e: e.matmul(psz[a][:, var, :], wqk[ws_][:, var, kc, :], hT[:, kc, :],
                                                              start=(kc == 0), stop=(kc == KC - 1)),
                                     reads=[Bwqk[ws_], BhT], writes=[Bpsz[a][var]])
                        for var in range(2):
                            T.op("dve", lambda e: e.tensor_tensor(out=t1[a][:, var, :], in0=psz[a][:, var, :], in1=cs[slot][:, var, :], op=ALU.mult),
                                 reads=[Bpsz[a][var], Bcs[slot]], writes=[Bt1[a]])
                        T.op("pool", lambda e: e.tensor_tensor(out=qo[q][:], in0=t1[a][:, 0, :], in1=t1[a][:, 1, :], op=ALU.add),
                             reads=[Bt1[a]], writes=[Bqo[q]])
                        T.dma("pool", sq["Bqk"], sq["qk"][m, :, t0:t0 + TT], Bqo[q], qo[q][:], Bqo[q])
                    if j + 1 < J:
                        N.emit((j + 1) % 2, srcB, src, (j + 1) * TT, TT, "nmix", li)
                    for sub in range(4):
                        v_ = c1 % 2
                        c1 += 1
                        for hf in range(2):
                            for kc in range(KC):
                                T.op("pe", lambda e: e.matmul(psv[hf][:], hT[:, kc, sub * 128:(sub + 1) * 128], wv[:, kc, hf * 512:(hf + 1) * 512],
                                                              start=(kc == 0), stop=(kc == KC - 1)),
                                     reads=[Bwv, BhT], writes=[Bpsv[hf]])
                            T.op("act", lambda e: e.activation(out=vt[v_][:, hf * 512:(hf + 1) * 512], in_=psv[hf][:], func=AF.Copy),
                                 reads=[Bpsv[hf]], writes=[Bvt[v_]])
                        tk = t0 + sub * 128
                        T.dma("pool", sq["Bvs"], sq["vs"][:, tk:tk + 128, :].rearrange("h t e -> t h e"),
                              Bvt[v_], vt[v_][:].rearrange("p (h e) -> p h e", h=H), Bvt[v_])
                T.barrier(N.bufs() + Bwqk + [Bwv] + Bcs + Bqo + Bvt)

        def phase_attn(sq, li, aj):
            S = sq["S"]
            NQ = S // TT
            NK = S // 128
            scale = 64 ** -0.5
            with contextlib.ExitStack() as ph:
                kT = [ph.enter_context(sbt("b_kT%d" % i, [128, S], BF16)) for i in range(2)]
                qT = [ph.enter_context(sbt("b_qT%d" % i, [128, S], BF16)) for i in range(2)]
                vh = [ph.enter_context(sbt("b_vh%d" % i, [128, NK, 130], BF16)) for i in range(2)]
                BkT = [Buf("b_kT%d" % i) for i in range(2)]
                BqT = [Buf("b_qT%d" % i) for i in range(2)]
                Bvh = [Buf("b_vh%d" % i) for i in range(2)]
                NP_ = 3
                pT = [ph.enter_context(sbt("b_pT%d" % i, [128, 2, TT], BF16)) for i in range(NP_)]
                BpT = [Buf("b_pT%d" % i) for i in range(NP_)]
                accs = ph.enter_context(sbt("b_accs", [128, 8, 129], F32))
                Baccs = Buf("b_accs")
                sm = ph.enter_context(sbt("b_sm", [128, 32], F32))
                Bsm = Buf("b_sm")
                o1 = [ph.enter_context(sbt("b_o1_%d" % i, [128, 128], F32)) for i in range(2)]
                o2 = [ph.enter_context(sbt("b_o2_%d" % i, [128, 128], F32)) for i in range(2)]
                ob = [ph.enter_context(sbt("b_ob_%d" % i, [128, 128], BF16)) for i in range(4)]
                Bob = [Buf("b_ob%d" % i) for i in range(4)]
                Bo = [Buf("b_o%d" % i) for i in range(2)]
                o1a = ph.enter_context(sbt("b_o1a", [128, 4, 128], F32))
                o2a = ph.enter_context(sbt("b_o2a", [128, 4, 128], F32))
                Bo1a = Buf("b_o1a")
                Bo2a = Buf("b_o2a")
                sm2 = ph.enter_context(sbt("b_sm2", [128, 8], F32))
                Bsm2 = Buf("b_sm2")
                oT = [ph.enter_context(sbt("b_oT%d" % i, [128, TT], BF16)) for i in range(2)]
                BoT = [Buf("b_oT%d" % i) for i in range(2)]
                pst = [ph.enter_context(pst_("b_pst%d" % i, [128, 2, TT], F32)) for i in range(2)]
                Bpst = [Buf("b_pst%d" % i, excl=True) for i in range(2)]
                pacc = ph.enter_context(pst_("b_pacc", [128, 3, TT], F32))
                Bpacc = [Buf("b_pacc%d" % i, excl=True) for i in range(3)]
                ptr = ph.enter_context(pst_("b_ptr", [128, TT], BF16))
                Bptr = Buf("b_ptr", excl=True)
                for i in range(2):
                    T.op("pool", lambda e: e.memset(vh[i][:, :, 128:130], 1.0), writes=[Bvh[i]])
                cc = 0
                eo = 0
                pending = []
                for h in range(H):
                    s = h % 2
                    T.dma("sp", BkT[s], kT[s][:], sq["Bqk"], sq["qk"][8 + h, :, :], BkT[s])
                    T.dma("sp", BqT[s], qT[s][:], sq["Bqk"], sq["qk"][h, :, :], BqT[s])
                    T.dma("sp", Bvh[s], vh[s][:, :, 0:128], sq["Bvs"], sq["vs"][h, :, :].rearrange("(kt p) e -> p kt e", p=128), Bvh[s])
                    units = [(qt_, kt_) for qt_ in range(NQ) for kt_ in range(NK)]

                    def emit_S(i):
                        qt_, kt_ = units[i]
                        a_ = (cc + i) % 2
                        for c in range(2):
                            T.op("pe", lambda e: e.matmul(pst[a_][:, c, :], kT[s][c * 64:(c + 1) * 64, kt_ * 128:(kt_ + 1) * 128],
                                                          qT[s][c * 64:(c + 1) * 64, qt_ * TT:(qt_ + 1) * TT], start=True, stop=True),
                                 reads=[BkT[s], BqT[s]], writes=[Bpst[a_]])

                    emit_S(0)
                    for ui in range(len(units)):
                        qt, kt = units[ui]
                        a = (cc + ui) % 2
                        p = (cc + ui) % NP_
                        if ui + 1 < len(units):
                            emit_S(ui + 1)
                        T.op("act", lambda e: e.activation(out=pT[p][:].rearrange("p a b -> p (a b)"),
                                                           in_=pst[a][:].rearrange("p a b -> p (a b)"), func=AF.Exp, scale=scale),
                             reads=[Bpst[a]], writes=[BpT[p]])
                        for c in range(2):
                            for qb in range(4):
                                ai = c * 4 + qb
                                bk, col = ai // 3, (ai % 3) * 129
                                T.op("pe", lambda e: e.matmul(pacc[:, bk, col:col + 129], pT[p][:, c, qb * 128:(qb + 1) * 128], vh[s][:, kt, 0:129],
                                                              start=(kt == 0 and ai % 3 == 0), stop=(kt == NK - 1), skip_group_check=True),
                                     reads=[BpT[p], Bvh[s]], writes=[Bpacc[bk]])
                        if kt == min(8, NK - 2) and pending:
                            for f_ in pending:
                                f_()
                            pending = []
                        if kt != NK - 1:
                            continue
                        for bk in range(3):
                            na = 3 if bk < 2 else 2
                            T.op("dve", lambda e: e.tensor_copy(out=accs[:, bk * 3:bk * 3 + na, :].rearrange("p a b -> p (a b)"), in_=pacc[:, bk, 0:na * 129]),
                                 reads=[Bpacc[bk]], writes=[Baccs])
                        T.op("dve", lambda e: e.reciprocal(out=sm[:, 0:8], in_=accs[:, :, 128]), reads=[Baccs], writes=[Bsm])
                        T.op("dve", lambda e: e.tensor_scalar(out=sm[:, 8:12], in0=sm[:, 4:8], scalar1=lamt[:, aj:aj + 1], scalar2=None, op0=ALU.mult),
                             reads=[Bsm, Bmisc], writes=[Bsm])
                        ot_s = eo % 2
                        eo += 1
                        for qb in range(4):
                            T.op("dve", lambda e: e.tensor_scalar(out=o2a[:, qb, :], in0=accs[:, 4 + qb, 0:128], scalar1=sm[:, 8 + qb:9 + qb], scalar2=None, op0=ALU.mult),
                                 reads=[Baccs, Bsm], writes=[Bo2a])
                            T.op("dve", lambda e: e.scalar_tensor_tensor(out=o1a[:, qb, :], in0=accs[:, qb, 0:128], scalar=sm[:, qb:qb + 1], in1=o2a[:, qb, :],
                                                                         op0=ALU.mult, op1=ALU.add),
                                 reads=[Baccs, Bsm, Bo2a], writes=[Bo1a])
                        T.op("dve", lambda e: e.tensor_tensor(out=o2a[:], in0=o1a[:], in1=o1a[:], op=ALU.mult), reads=[Bo1a, Bo2a], writes=[Bo2a])
                        T.op("dve", lambda e: e.reduce_sum(out=sm[:, 12:16], in_=o2a[:], axis=mybir.AxisListType.X), reads=[Bo2a, Bsm], writes=[Bsm])
                        T.op("pool", lambda e: e.tensor_scalar(out=sm2[:, 0:4], in0=sm[:, 12:16], scalar1=1.0 / 128, scalar2=SUBLN_EPS, op0=ALU.mult, op1=ALU.add),
                             reads=[Bsm], writes=[Bsm2])
                        T.op("pool", lambda e: e.tensor_tensor(out=sm2[:, 4:8], in0=sm2[:, 0:4], in1=mhalf[:, 0:4], op=ALU.pow),
                             reads=[Bsm2, Bmisc], writes=[Bsm2])
                        for qb in range(4):
                            T.op("dve", lambda e: e.scalar_tensor_tensor(out=ob[qb][:], in0=o1a[:, qb, :], scalar=sm2[:, 4 + qb:5 + qb], in1=subgs[:, aj, :],
                                                                         op0=ALU.mult, op1=ALU.mult),
                                 reads=[Bo1a, Bsm2, Bmisc], writes=[Bob[qb]])

                        def part2(qt=qt, ot_s=ot_s, h=h):
                            for qb in range(4):
                                T.op("pe", lambda e: e.transpose(ptr[:, qb * 128:(qb + 1) * 128], ob[qb][:], ident[:]),
                                     reads=[Bob[qb], Bmisc], writes=[Bptr])
                            T.op("dve", lambda e: e.tensor_copy(out=oT[ot_s][:], in_=ptr[:]), reads=[Bptr], writes=[BoT[ot_s]])
                            T.dma("pool", sq["Bot"], sq["ot"][h, :, qt * TT:(qt + 1) * TT], BoT[ot_s], oT[ot_s][:], BoT[ot_s])

                        pending.append(part2)
                    for f_ in pending:
                        f_()
                    pending = []
                    cc += len(units)
                T.barrier(BkT + BqT + Bvh + BoT)

        def phase_aout(sq, li, aj, srcB, src, dstB, dst):
            S = sq["S"]
            J = S // TT
            with contextlib.ExitStack() as ph:
                wo = ph.enter_context(sbt("c_wo", [128, 8, KC, 128], BF16))
                Bwo = Buf("c_wo")
                oT = [ph.enter_context(sbt("c_oT%d" % i, [128, H, TT], BF16)) for i in range(2)]
                BoT = [Buf("c_oT%d" % i) for i in range(2)]
                xr = [ph.enter_context(sbt("c_xr%d" % i, [128, KC, TT], F32)) for i in range(2)]
                Bxr = [Buf("c_xr%d" % i) for i in range(2)]
                xn = [ph.enter_context(sbt("c_xn%d" % i, [128, KC, TT], F32)) for i in range(2)]
                Bxn = [Buf("c_xn%d" % i) for i in range(2)]
                ps = [ph.enter_context(pst_("c_ps%d" % i, [128, TT], F32)) for i in range(2)]
                Bps = [Buf("c_ps%d" % i, excl=True) for i in range(2)]
                T.dma("sp", Bwo, wo[:].rearrange("p a b c -> p (a b c)"), Bwbf, wslice("a%d_wo" % aj), Bwo)
                for j in range(J):
                    s = j % 2
                    t0 = j * TT
                    T.dma("sp", BoT[s], oT[s][:], sq["Bot"], sq["ot"][:, :, t0:t0 + TT].rearrange("h p t -> p h t"), BoT[s])
                    T.dma("sp", Bxr[s], xr[s][:], srcB, xT(src, t0, TT), Bxr[s])
                    for dc in range(KC):
                        o = dc % 2
                        for h in range(H):
                            T.op("pe", lambda e: e.matmul(ps[o][:], wo[:, dc, h, :], oT[s][:, h, :], start=(h == 0), stop=(h == H - 1)),
                                 reads=[Bwo, BoT[s]], writes=[Bps[o]])
                        T.op("dve", lambda e: e.tensor_tensor(out=xn[s][:, dc, :], in0=ps[o][:], in1=xr[s][:, dc, :], op=ALU.add),
                             reads=[Bps[o], Bxr[s]], writes=[Bxn[s]])
                    T.dma("pool", dstB, xT(dst, t0, TT), Bxn[s], xn[s][:], Bxn[s])
                T.barrier([Bwo] + BoT + Bxr + Bxn)

        def phase_final(sq, srcB, src):
            S = sq["S"]
            J = S // TT
            with contextlib.ExitStack() as ph:
                N = NormCtx(ph)
                yo = [ph.enter_context(sbt("z_yo%d" % i, [128, KC, TT], F32)) for i in range(2)]
                Byo = [Buf("z_yo%d" % i) for i in range(2)]
                for j in range(J):
                    s = j % 2
                    N.BhT[s] = Byo[s]
                    N.emit(s, srcB, src, j * TT, TT, "nfin", 0, out_f32=yo[s])
                    T.dma("pool", sq["Byout"], xT(sq["yout"], j * TT, TT), Byo[s], yo[s][:], Byo[s])
                T.barrier(N.bufs() + Byo)

        cur = {tag: ("Bxin", "xin") for tag, _ in seqs}

        def nxt(tag):
            return ("Bxb", "xb") if cur[tag][1] in ("xin", "xa") else ("Bxa", "xa")

        for li in range(DEPTH):
            for tag, _ in seqs:
                sq = SQ[tag]
                sB, s = cur[tag]
                dB, d = nxt(tag)
                if li % 2 == 0:
                    phase_gmlp(sq, li, li // 2, sq[sB], sq[s], sq[dB], sq[d])
                else:
                    phase_qkv(sq, li, li // 2, sq[sB], sq[s])
                    phase_attn(sq, li, li // 2)
                    phase_aout(sq, li, li // 2, sq[sB], sq[s], sq[dB], sq[d])
                cur[tag] = (dB, d)
            for tag, _ in seqs:
                sq = SQ[tag]
                sB, s = cur[tag]
                dB, d = nxt(tag)
                phase_ffn(sq, li, sq[sB], sq[s], sq[dB], sq[d])
                cur[tag] = (dB, d)
        for tag, _ in seqs:
            sq = SQ[tag]
            sB, s = cur[tag]
            phase_final(sq, sq[sB], sq[s])
        T.barrier([], release=False)
        build_program.stats = (dict(T.nins), T.nwait, len(T.semh))
    return nc


N_CORES = 8


def kernel(**inputs):
    xs = np.asarray(inputs["x_sample"], dtype=np.float32)
    xp = np.asarray(inputs["x_prompt"], dtype=np.float32)
    Ss, Sp = xs.shape[1], xp.shape[1]
    wall, pall, cosT, sinT = host_pack(inputs, max(Ss, Sp))
    nc = build_program([("s", Ss), ("p", Sp)])
    in_maps = []
    for c in range(N_CORES):
        in_maps.append({
            "wall": wall, "pall": pall, "cosT": cosT, "sinT": sinT,
            "x_s": np.ascontiguousarray(xs[c].T),
            "x_p": np.ascontiguousarray(xp[c // 2].T),
        })
    res = run_bass_kernel_spmd(nc, in_maps, core_ids=list(range(N_CORES)))
    y_s = np.stack([np.ascontiguousarray(res.results[c]["y_s"].T) for c in range(N_CORES)], axis=0)
    y_p = np.stack([np.ascontiguousarray(res.results[2 * i]["y_p"].T) for i in range(xp.shape[0])], axis=0)
    return (y_p.astype(np.float32), y_s.astype(np.float32))
```

```python
import contextlib
import math
import numpy as np
import concourse.bass as bass
import concourse.mybir as mybir
from concourse.bass_utils import run_bass_kernel_spmd

F32 = mybir.dt.float32
BF16 = mybir.dt.bfloat16
AF = mybir.ActivationFunctionType
ALU = mybir.AluOpType

D = 1024
KC = 8
DEPTH = 4
DFF = 2816
NPAIR = 22
H = 8
NORM_EPS = 1e-6
SUBLN_EPS = 1e-5
ROPE_THETA = 10000.0
TT = 512
GELU_C = 0.044715


class Buf:
    __slots__ = ("name", "w", "r", "excl", "merge", "dsem")

    def __init__(self, name, excl=False, merge=False):
        self.name = name
        self.w = {}
        self.r = {}
        self.excl = excl
        self.merge = merge
        self.dsem = {}


class Trk:
    EPOCH = 30000

    def __init__(self, nc, es):
        self.nc = nc
        self.es = es
        self.eng = {"pe": nc.tensor, "act": nc.scalar, "dve": nc.vector, "pool": nc.gpsimd, "sp": nc.sync}
        self.semh = {}
        self.cnt = {e: 0 for e in self.eng}
        self.epoch = {e: 0 for e in self.eng}
        self.cur = {}
        self.waited = {e: {} for e in self.eng}
        self.dval = {}
        self.active = []
        self.dfree = {e: [] for e in self.eng}
        self.nd = 0
        self.nwait = 0
        self.nins = {e: 0 for e in self.eng}
        for e in self.eng:
            self._newsem(e)

    def _mk(self, name):
        self.semh[name] = self.es.enter_context(self.nc.semaphore(name))
        return name

    def _newsem(self, e):
        self.cur[e] = self._mk("q_%s_%d" % (e, self.epoch[e]))
        self.cnt[e] = 0

    def _deps(self, reads, writes):
        deps = {}
        reads = [b for b in reads if not b.merge]
        writes = [b for b in writes if not b.merge]
        for b in reads:
            for s, v in b.w.items():
                if deps.get(s, 0) < v:
                    deps[s] = v
            if b.excl:
                for s, v in b.r.items():
                    if deps.get(s, 0) < v:
                        deps[s] = v
        for b in writes:
            for s, v in b.w.items():
                if deps.get(s, 0) < v:
                    deps[s] = v
            for s, v in b.r.items():
                if deps.get(s, 0) < v:
                    deps[s] = v
        return deps

    def _wait(self, e, deps):
        engine = self.eng[e]
        wd = self.waited[e]
        own = self.cur[e]
        for s, v in deps.items():
            if e == "pe" and s == own:
                continue
            if wd.get(s, 0) >= v:
                continue
            engine.wait_ge(self.semh[s], v)
            wd[s] = v
            self.nwait += 1

    def _record(self, ev, reads, writes):
        s, v = ev
        for b in writes:
            if b.merge:
                continue
            b.w = {s: v}
            b.r = {}
        for b in reads:
            if b.merge:
                continue
            if b.excl:
                b.w = {s: v}
                b.r = {}
            elif b.r.get(s, 0) < v:
                b.r[s] = v

    def op(self, e, fn, reads=(), writes=()):
        if self.cnt[e] >= self.EPOCH:
            self.epoch[e] += 1
            self._newsem(e)
        self._wait(e, self._deps(reads, writes))
        ins = fn(self.eng[e])
        self.cnt[e] += 1
        ev = (self.cur[e], self.cnt[e])
        ins.then_inc(self.semh[ev[0]], 1)
        self._record(ev, reads, writes)
        self.nins[e] += 1
        return ins

    def dma(self, q, dst, dst_ap, src, src_ap, side, slow=False):
        if q not in side.dsem:
            if self.dfree[q]:
                side.dsem[q] = self.dfree[q].pop()
            else:
                side.dsem[q] = self._mk("d%s%d" % (q, self.nd))
                self.nd += 1
                self.dval[side.dsem[q]] = 0
            self.active.append((side, q))
        name = side.dsem[q]
        self._wait(q, self._deps((src,), (dst,)))
        if slow:
            ins = self.eng[q].dma_start(out=dst_ap, in_=src_ap, allow_slow_non_contiguous=True)
        else:
            ins = self.eng[q].dma_start(out=dst_ap, in_=src_ap)
        self.dval[name] += 16
        ev = (name, self.dval[name])
        ins.then_inc(self.semh[name], 16)
        self._record(ev, (src,), (dst,))
        self.nins[q] += 1
        return ins

    def barrier(self, bufs=(), release=True):
        deps = {}
        for e in self.eng:
            if self.cnt[e] > 0:
                deps[self.cur[e]] = self.cnt[e]
        for b, q in self.active:
            deps[b.dsem[q]] = self.dval[b.dsem[q]]
        for e in self.eng:
            self._wait(e, dict(deps))
        if release:
            for b, q in self.active:
                self.dfree[q].append(b.dsem.pop(q))
            self.active = []


class Lay:
    def __init__(self):
        self.off = {}
        self.n = 0

    def add(self, name, ncols):
        self.off[name] = (self.n, ncols)
        self.n += ncols


def make_layouts():
    wl = Lay()
    for j in range(2):
        wl.add("g%d_wu" % j, 8 * 8 * 128)
        wl.add("g%d_wv" % j, 8 * 1024)
        wl.add("g%d_ws" % j, 8 * 128)
        wl.add("g%d_wo" % j, 8 * 8 * 128)
    for j in range(2):
        wl.add("a%d_wqk" % j, 16 * 2 * 8 * 128)
        wl.add("a%d_wv" % j, 8 * 1024)
        wl.add("a%d_wo" % j, 8 * 8 * 128)
    for i in range(DEPTH):
        wl.add("f%d_win" % i, 44 * 8 * 128)
        wl.add("f%d_wout" % i, 8 * NPAIR * 128)
    pl = Lay()
    pl.add("nmix", DEPTH * 8)
    pl.add("nffn", DEPTH * 8)
    pl.add("nfin", 8)
    pl.add("vgain", 2 * 8)
    pl.add("convw", DEPTH * 44 * 4)
    pl.add("bsrep", 2 * 8 * 128)
    pl.add("subg", 2 * 128)
    pl.add("lam", 2 * 4 * 64)
    pl.add("ident", 128)
    return wl, pl


def pack_lhsT(w):
    K, M = w.shape
    return w.reshape(K // 128, 128, M // 128, 128).transpose(1, 2, 0, 3).reshape(128, -1)


def pack_rhs(w):
    K, N = w.shape
    return w.reshape(K // 128, 128, N).transpose(1, 0, 2).reshape(128, -1)


def pack_wout(w):
    K, N = w.shape
    return w.reshape(K // 128, 128, N // 128, 128).transpose(1, 2, 0, 3).reshape(128, -1)


def host_pack(inp, smax):
    wl, pl = make_layouts()
    wall = np.empty((128, wl.n), np.float32)
    pall = np.empty((128, pl.n), np.float32)

    def putw(name, arr):
        o, n = wl.off[name]
        assert arr.shape == (128, n), (name, arr.shape, n)
        wall[:, o:o + n] = arr

    def putp(name, arr):
        o, n = pl.off[name]
        assert arr.shape == (128, n), (name, arr.shape, n)
        pall[:, o:o + n] = arr

    for j in range(2):
        w = np.asarray(inp["gmlp_w_in"][j])
        putw("g%d_wu" % j, pack_lhsT(w[:, :1024]))
        putw("g%d_wv" % j, pack_rhs(w[:, 1024:]))
        putw("g%d_ws" % j, np.asarray(inp["gmlp_w_s"][j]).transpose(2, 0, 1).reshape(128, -1))
        putw("g%d_wo" % j, pack_wout(np.asarray(inp["gmlp_w_out"][j])))
    swap = np.arange(128)
    swap = (swap // 64) * 64 + ((swap % 64) + 32) % 64
    for j in range(2):
        w = np.asarray(inp["diff_w_qkv"][j])
        qk = w[:, :2048].reshape(1024, 16, 128)
        both = np.stack([qk, qk[:, :, swap]], axis=2)
        arr = both.reshape(8, 128, 16, 2, 128).transpose(1, 2, 3, 0, 4)
        putw("a%d_wqk" % j, arr.reshape(128, -1))
        putw("a%d_wv" % j, pack_rhs(w[:, 2048:]))
        putw("a%d_wo" % j, pack_wout(np.asarray(inp["diff_w_out"][j])))
    for i in range(DEPTH):
        w = np.asarray(inp["ffn_w_in"][i])
        wp = np.stack([w[:, :DFF].reshape(1024, NPAIR, 128), w[:, DFF:].reshape(1024, NPAIR, 128)], axis=2)
        putw("f%d_win" % i, pack_lhsT(wp.reshape(1024, 2 * DFF)))
        putw("f%d_wout" % i, pack_wout(np.asarray(inp["ffn_w_out"][i])))

    def pervec(v):
        v = np.asarray(v).reshape(-1, 8, 128)
        return v.transpose(2, 0, 1).reshape(128, -1)

    putp("nmix", pervec(inp["norm_mix"]))
    putp("nffn", pervec(inp["norm_ffn"]))
    putp("nfin", pervec(inp["norm_final"]))
    putp("vgain", pervec(inp["gmlp_v_gain"]))
    cw = np.asarray(inp["ffn_conv_w"])
    cb = np.asarray(inp["ffn_conv_b"])
    c4 = np.concatenate([cw, cb[:, None, :]], axis=1)
    c4 = c4.reshape(DEPTH, 4, 2, NPAIR, 128)
    putp("convw", c4.transpose(4, 0, 3, 2, 1).reshape(128, -1))
    bs = np.asarray(inp["gmlp_b_s"])
    bsr = np.broadcast_to(bs[None, :, :, :], (128, 2, 8, 128))
    putp("bsrep", np.ascontiguousarray(bsr).reshape(128, -1))
    sg = np.asarray(inp["diff_subln_g"])
    putp("subg", np.ascontiguousarray(np.broadcast_to(sg[None], (128, 2, 128))).reshape(128, -1))
    lam = np.stack([np.asarray(inp["diff_lam_q1"]), np.asarray(inp["diff_lam_k1"]),
                    np.asarray(inp["diff_lam_q2"]), np.asarray(inp["diff_lam_k2"])], axis=1)
    putp("lam", np.ascontiguousarray(np.broadcast_to(lam[None], (128, 2, 4, 64))).reshape(128, -1))
    putp("ident", np.eye(128, dtype=np.float32))
    pos = np.arange(smax, dtype=np.float32)
    inv_freq = (ROPE_THETA ** (-np.arange(0, 64, 2, dtype=np.float32) / np.float32(64))).astype(np.float32)
    ang = (pos[:, None] * inv_freq[None, :]).astype(np.float32)
    dd = np.arange(128) % 64
    cosT = np.cos(ang).astype(np.float32).T[dd % 32, :]
    sinT = np.sin(ang).astype(np.float32).T[dd % 32, :]
    sgn = np.where(dd < 32, -1.0, 1.0).astype(np.float32)[:, None]
    return wall, pall, np.ascontiguousarray(cosT), np.ascontiguousarray(sinT * sgn)


def build_program(seqs):
    wl, pl = make_layouts()
    smax = max(s for _, s in seqs)
    nc = bass.Bass("TRN2", target_bir_lowering=False)
    wall = nc.dram_tensor("wall", [128, wl.n], F32, kind="ExternalInput").ap()
    pall_d = nc.dram_tensor("pall", [128, pl.n], F32, kind="ExternalInput").ap()
    cos_d = nc.dram_tensor("cosT", [128, smax], F32, kind="ExternalInput").ap()
    sin_d = nc.dram_tensor("sinT", [128, smax], F32, kind="ExternalInput").ap()
    wbf = nc.dram_tensor("wbf", [128, wl.n], BF16, kind="Internal").ap()
    Bwall = Buf("wall", merge=True)
    Bwbf = Buf("wbf", merge=True)
    Bconst = Buf("constd", merge=True)
    SQ = {}
    for tag, S in seqs:
        d = {}
        d["S"] = S
        d["xin"] = nc.dram_tensor("x_" + tag, [D, S], F32, kind="ExternalInput").ap()
        d["yout"] = nc.dram_tensor("y_" + tag, [D, S], F32, kind="ExternalOutput").ap()
        d["xa"] = nc.dram_tensor("xa_" + tag, [D, S], F32, kind="Internal").ap()
        d["xb"] = nc.dram_tensor("xb_" + tag, [D, S], F32, kind="Internal").ap()
        d["qk"] = nc.dram_tensor("qk_" + tag, [16, 128, S], BF16, kind="Internal").ap()
        d["vs"] = nc.dram_tensor("vs_" + tag, [H, S, 128], BF16, kind="Internal").ap()
        d["ot"] = nc.dram_tensor("ot_" + tag, [H, 128, S], BF16, kind="Internal").ap()
        for k in ("xin", "yout", "xa", "xb", "qk", "vs", "ot"):
            d["B" + k] = Buf(k + "_" + tag, merge=True)
        SQ[tag] = d

    uid = [0]

    def sbt(name, shape, dt):
        uid[0] += 1
        return nc.sbuf_tensor("%s_u%d" % (name, uid[0]), shape, dt)

    def pst_(name, shape, dt):
        uid[0] += 1
        return nc.psum_tensor("%s_u%d" % (name, uid[0]), shape, dt)

    es = contextlib.ExitStack()
    with es:
        T = Trk(nc, es)

        def wslice(name, a=0, n=None):
            o, tot = wl.off[name]
            if n is None:
                n = tot - a
            return wbf[:, o + a:o + a + n]

        pall = es.enter_context(sbt("pall_sb", [128, pl.n], F32))
        Bpall = Buf("pall")
        T.dma("sp", Bpall, pall[:], Bconst, pall_d[:, :], Bpall)

        def pcol(name, a, n=1):
            o, _ = pl.off[name]
            return pall[:, o + a:o + a + n]

        ident = es.enter_context(sbt("ident", [128, 128], BF16))
        ones32 = es.enter_context(sbt("ones32", [128, 128], F32))
        lamt = es.enter_context(sbt("lamt", [128, 16], F32))
        subgs = es.enter_context(sbt("subgs", [128, 2, 128], F32))
        Bmisc = Buf("misc")
        T.op("dve", lambda e: e.tensor_copy(out=ident[:], in_=pcol("ident", 0, 128)), reads=[Bpall], writes=[Bmisc])
        T.op("pool", lambda e: e.memset(ones32[:], 1.0), writes=[Bmisc])
        mhalf = es.enter_context(sbt("mhalf", [128, 4], F32))
        T.op("pool", lambda e: e.memset(mhalf[:], -0.5), writes=[Bmisc])
        lam_inits = []
        for j in range(2):
            li = 0.8 - 0.6 * math.exp(-0.3 * (2 * j + 1))
            lam_inits.append(li)
            lo = pl.off["lam"][0] + j * 256
            for w_ in range(2):
                T.op("dve", lambda e: e.tensor_tensor(out=subgs[:, 0, 0:64], in0=pall[:, lo + w_ * 128:lo + w_ * 128 + 64],
                                                      in1=pall[:, lo + w_ * 128 + 64:lo + w_ * 128 + 128], op=ALU.mult),
                     reads=[Bpall, Bmisc], writes=[Bmisc])
                T.op("dve", lambda e: e.reduce_sum(out=lamt[:, 8 + w_:9 + w_], in_=subgs[:, 0, 0:64], axis=mybir.AxisListType.X),
                     reads=[Bmisc], writes=[Bmisc])
            T.op("act", lambda e: e.activation(out=lamt[:, 10:12], in_=lamt[:, 8:10], func=AF.Exp), reads=[Bmisc], writes=[Bmisc])
            T.op("dve", lambda e: e.scalar_tensor_tensor(out=lamt[:, j:j + 1], in0=lamt[:, 11:12], scalar=-li, in1=lamt[:, 10:11],
                                                         op0=ALU.add, op1=ALU.subtract), reads=[Bmisc], writes=[Bmisc])
        for j in range(2):
            T.op("dve", lambda e: e.tensor_scalar(out=subgs[:, j, :], in0=pcol("subg", j * 128, 128), scalar1=1.0 - lam_inits[j],
                                                  scalar2=None, op0=ALU.mult), reads=[Bpall, Bmisc], writes=[Bmisc])
        T.barrier([Bpall])

        def phase_cast():
            CW = 4096
            with contextlib.ExitStack() as ph:
                NS = 3
                f = [ph.enter_context(sbt("cf%d" % i, [128, CW], F32)) for i in range(NS)]
                b = [ph.enter_context(sbt("cb%d" % i, [128, CW], BF16)) for i in range(NS)]
                Bf = [Buf("cf%d" % i) for i in range(NS)]
                Bb = [Buf("cb%d" % i) for i in range(NS)]
                nch = (wl.n + CW - 1) // CW
                for c in range(nch):
                    s = c % NS
                    a = c * CW
                    n = min(CW, wl.n - a)
                    T.dma("sp", Bf[s], f[s][:, 0:n], Bwall, wall[:, a:a + n], Bf[s])
                    if c % 2 == 0:
                        T.op("dve", lambda e: e.tensor_copy(out=b[s][:, 0:n], in_=f[s][:, 0:n]), reads=[Bf[s]], writes=[Bb[s]])
                    else:
                        T.op("act", lambda e: e.activation(out=b[s][:, 0:n], in_=f[s][:, 0:n], func=AF.Copy), reads=[Bf[s]], writes=[Bb[s]])
                    T.dma("pool", Bwbf, wbf[:, a:a + n], Bb[s], b[s][:, 0:n], Bb[s])
                T.barrier(Bf + Bb)

        phase_cast()

        def xT(ap, t0, n):
            return ap.rearrange("(kc p) t -> p kc t", p=128)[:, :, t0:t0 + n]

        class NormCtx:
            def __init__(self, ph, nslot=2):
                self.ns = nslot
                self.xin = [ph.enter_context(sbt("n_xin%d" % i, [128, KC, TT], F32)) for i in range(nslot)]
                self.hT = [ph.enter_context(sbt("n_hT%d" % i, [128, KC, TT], BF16)) for i in range(nslot)]
                self.sq = ph.enter_context(sbt("n_sq", [128, 2, TT], F32))
                self.rs = ph.enter_context(sbt("n_rs", [128, TT], F32))
                self.ps = ph.enter_context(pst_("n_ps", [128, TT], F32))
                self.Bxin = [Buf("n_xin%d" % i) for i in range(nslot)]
                self.BhT = [Buf("n_hT%d" % i) for i in range(nslot)]
                self.Bsq = [Buf("n_sq0"), Buf("n_sq1")]
                self.Brs = Buf("n_rs")
                self.Bps = Buf("n_ps", excl=True)

            def bufs(self):
                return self.Bxin + self.BhT + self.Bsq + [self.Brs]

            def emit(self, slot, srcB, src_ap, t0, n, gname, gidx, out_f32=None):
                xin, hT = self.xin[slot], self.hT[slot]
                Bxin, BhT = self.Bxin[slot], self.BhT[slot]
                T.dma("sp", Bxin, xin[:, :, 0:n], srcB, xT(src_ap, t0, n), Bxin)
                for kc in range(KC):
                    T.op("act", lambda e: e.activation(out=self.sq[:, kc % 2, 0:n], in_=xin[:, kc, 0:n], func=AF.Square),
                         reads=[Bxin], writes=[self.Bsq[kc % 2]])
                    T.op("pe", lambda e: e.matmul(self.ps[:, 0:n], ones32[:], self.sq[:, kc % 2, 0:n], start=(kc == 0), stop=(kc == KC - 1)),
                         reads=[self.Bsq[kc % 2], Bmisc], writes=[self.Bps])
                T.op("act", lambda e: e.activation(out=self.rs[:, 0:n], in_=self.ps[:, 0:n], func=AF.Ln, scale=1.0 / D, bias=NORM_EPS),
                     reads=[self.Bps], writes=[self.Brs])
                T.op("act", lambda e: e.activation(out=self.rs[:, 0:n], in_=self.rs[:, 0:n], func=AF.Exp, scale=-0.5),
                     reads=[self.Brs], writes=[self.Brs])
                for kc in range(KC):
                    dst = hT[:, kc, 0:n] if out_f32 is None else out_f32[:, kc, 0:n]
                    T.op("dve", lambda e: e.scalar_tensor_tensor(out=dst, in0=xin[:, kc, 0:n], scalar=pcol(gname, gidx * 8 + kc),
                                                                 in1=self.rs[:, 0:n], op0=ALU.mult, op1=ALU.mult),
                         reads=[Bxin, self.Brs, Bpall], writes=[BhT])

        def phase_ffn(sq, li, srcB, src, dstB, dst):
            S = sq["S"]
            J = S // TT
            with contextlib.ExitStack() as ph:
                N = NormCtx(ph)
                NW = 3
                win = [ph.enter_context(sbt("f_win%d" % i, [128, 2, KC, 128], BF16)) for i in range(NW)]
                Bwin = [Buf("f_win%d" % i) for i in range(NW)]
                wout = ph.enter_context(sbt("f_wout", [128, KC, NPAIR, 128], BF16))
                Bwout = [Buf("f_wout%d" % i) for i in range(KC)]
                NA = 3
                aext = [ph.enter_context(sbt("f_aext%d" % i, [128, 2, TT + 2], F32)) for i in range(NA)]
                Baext = [Buf("f_aext%d" % i) for i in range(NA)]
                ct = [ph.enter_context(sbt("f_ct%d" % i, [128, 2, TT], F32)) for i in range(NA)]
                Bct = [Buf("f_ct%d" % i) for i in range(NA)]
                sg = [ph.enter_context(sbt("f_sg%d" % i, [128, TT], F32)) for i in range(NA)]
                Bsg = [Buf("f_sg%d" % i) for i in range(NA)]
                carry = ph.enter_context(sbt("f_carry", [128, 2 * NPAIR, 2], F32))
                Bcarry = [Buf("f_carry%d" % i) for i in range(NPAIR)]
                yT = ph.enter_context(sbt("f_yT", [128, NPAIR, TT], BF16))
                ByT = [Buf("f_yT%d" % i) for i in range(NPAIR)]
                xres = ph.enter_context(sbt("f_xres", [128, KC, TT], F32))
                Bxres = Buf("f_xres")
                psa = [ph.enter_context(pst_("f_psa%d" % i, [128, 2, TT], F32)) for i in range(2)]
                Bpsa = [[Buf("f_psa%d_%d" % (i, h), excl=True) for h in range(2)] for i in range(2)]
                pso = [ph.enter_context(pst_("f_pso%d" % i, [128, TT], F32)) for i in range(3)]
                Bpso = [Buf("f_pso%d" % i, excl=True) for i in range(3)]
                cwo = pl.off["convw"][0] + li * 44 * 4

                T.op("pool", lambda e: e.memset(carry[:], 0.0), writes=Bcarry)
                ctr = [0, 0, 0]
                pend = []
                wout_loaded = [False]

                def win_stage(j, slot):
                    virt = (j == J)
                    hT, BhT = N.hT[slot], N.BhT[slot]
                    n = 1 if virt else TT
                    prev = None
                    for pr in range(NPAIR + 1):
                        if pr == 3:
                            while pend:
                                pend.pop()()
                        if pr == NPAIR // 2 and j + 1 < J:
                            N.emit((j + 1) % 2, srcB, src, (j + 1) * TT, TT, "nffn", li)
                        if pr < NPAIR:
                            ws = ctr[0] % NW
                            ctr[0] += 1
                            a = ctr[1] % NA
                            pb = ctr[1] % 2
                            ctr[1] += 1
                            if not virt:
                                T.dma("sp", Bwin[ws], win[ws][:].rearrange("p a b c -> p (a b c)"), Bwbf,
                                      wslice("f%d_win" % li, pr * 2048, 2048), Bwin[ws])
                                for hf in range(2):
                                    for kc in range(KC):
                                        T.op("pe", lambda e: e.matmul(psa[pb][:, hf, :], win[ws][:, hf, kc, :], hT[:, kc, :],
                                                                      start=(kc == 0), stop=(kc == KC - 1)),
                                             reads=[Bwin[ws], BhT], writes=[Bpsa[pb][hf]])
                            T.op("pool", lambda e: e.tensor_copy(out=aext[a][:, :, 0:2], in_=carry[:, 2 * pr:2 * pr + 2, :]),
                                 reads=[Bcarry[pr]], writes=[Baext[a]])
                            if virt:
                                T.op("pool", lambda e: e.memset(aext[a][:, :, 2:TT + 2], 0.0), writes=[Baext[a]])
                            else:
                                T.op("act", lambda e: e.activation(out=aext[a][:, :, 2:TT + 2], in_=psa[pb][:, :, :], func=AF.Copy),
                                     reads=Bpsa[pb], writes=[Baext[a]])
                                T.op("pool", lambda e: e.tensor_copy(out=carry[:, 2 * pr:2 * pr + 2, :], in_=aext[a][:, :, TT:TT + 2]),
                                     reads=[Baext[a]], writes=[Bcarry[pr]])
                            for hf in range(2):
                                co = cwo + (pr * 2 + hf) * 4
                                T.op("act", lambda e: e.activation(out=ct[a][:, hf, 0:n], in_=aext[a][:, hf, 0:n], func=AF.Identity,
                                                                   scale=pall[:, co:co + 1], bias=pall[:, co + 3:co + 4]),
                                     reads=[Baext[a], Bpall], writes=[Bct[a]])
                        if prev is not None:
                            pa, ppr = prev
                            T.op("act", lambda e: e.activation(out=sg[pa][:, 0:n], in_=ct[pa][:, 0, 0:n], func=AF.Silu),
                                 reads=[Bct[pa]], writes=[Bsg[pa]])
                        if pr < NPAIR:
                            for hf in range(2):
                                co = cwo + (pr * 2 + hf) * 4
                                for t in (1, 2):
                                    T.op("dve", lambda e: e.scalar_tensor_tensor(out=ct[a][:, hf, 0:n], in0=aext[a][:, hf, t:t + n],
                                                                                 scalar=pall[:, co + t:co + t + 1], in1=ct[a][:, hf, 0:n],
                                                                                 op0=ALU.mult, op1=ALU.add),
                                         reads=[Baext[a], Bct[a], Bpall], writes=[Bct[a]])
                        if prev is not None:
                            pa, ppr = prev
                            T.op("dve", lambda e: e.tensor_tensor(out=yT[:, ppr, 0:n], in0=sg[pa][:, 0:n], in1=ct[pa][:, 1, 0:n], op=ALU.mult),
                                 reads=[Bsg[pa], Bct[pa]], writes=[ByT[ppr]])
                        prev = (a, pr) if pr < NPAIR else None

                def wout_stage(j):
                    virt = (j == J)
                    lo = 1 if j == 0 else 0
                    hi = 1 if virt else TT
                    n = hi - lo
                    tok0 = j * TT - 1 + lo
                    T.dma("sp", Bxres, xres[:, :, 0:n], srcB, xT(src, tok0, n), Bxres, slow=(n == 1))
                    for dc in range(KC):
                        o = dc % 3
                        if not wout_loaded[0]:
                            T.dma("sp", Bwout[dc], wout[:, dc, :, :].rearrange("p a b -> p (a b)"), Bwbf,
                                  wslice("f%d_wout" % li, dc * NPAIR * 128, NPAIR * 128), Bwout[dc])
                        for pr in range(NPAIR):
                            T.op("pe", lambda e: e.matmul(pso[o][:, 0:n], wout[:, dc, pr, :], yT[:, pr, lo:hi],
                                                          start=(pr == 0), stop=(pr == NPAIR - 1)),
                                 reads=[Bwout[dc], ByT[pr]], writes=[Bpso[o]])
                        T.op("dve", lambda e: e.tensor_tensor(out=xres[:, dc, 0:n], in0=pso[o][:, 0:n], in1=xres[:, dc, 0:n], op=ALU.add),
                             reads=[Bpso[o], Bxres], writes=[Bxres])
                    wout_loaded[0] = True
                    pend.append(lambda: T.dma("sp", dstB, xT(dst, tok0, n), Bxres, xres[:, :, 0:n], Bxres, slow=(n == 1)))

                N.emit(0, srcB, src, 0, TT, "nffn", li)
                for j in range(J + 1):
                    win_stage(j, j % 2)
                    wout_stage(j)
                while pend:
                    pend.pop()()
                T.barrier()

        def phase_gmlp(sq, li, gj, srcB, src, dstB, dst):
            S = sq["S"]
            J = S // TT
            with contextlib.ExitStack() as ph:
                N = NormCtx(ph)
                wu = ph.enter_context(sbt("g_wu", [128, 8, KC, 128], BF16))
                wv = ph.enter_context(sbt("g_wv", [128, KC, 1024], BF16))
                ws = ph.enter_context(sbt("g_ws", [128, 8, 128], BF16))
                wo = ph.enter_context(sbt("g_wo", [128, 8, KC, 128], BF16))
                Bw = Buf("g_w")
                Bw2 = [Buf("g_w2_%d" % i) for i in range(3)]
                uT = ph.enter_context(sbt("g_uT", [128, 8, TT], F32))
                BuT = [Buf("g_uT%d" % i) for i in range(8)]
                vg = [ph.enter_context(sbt("g_vg%d" % i, [128, 1024], F32)) for i in range(2)]
                Bvg = [Buf("g_vg%d" % i) for i in range(2)]
                vsq = ph.enter_context(sbt("g_vsq", [128, 1024], BF16))
                Bvsq = Buf("g_vsq")
                vn = ph.enter_context(sbt("g_vn", [128, 4, 1024], BF16))
                Bvn = [Buf("g_vn%d" % i) for i in range(4)]
                st = [ph.enter_context(sbt("g_st%d" % i, [128, 8], F32)) for i in range(2)]
                Bst = [Buf("g_st%d" % i) for i in range(2)]
                v2 = [ph.enter_context(sbt("g_v2_%d" % i, [128, TT], F32)) for i in range(2)]
                Bv2 = [Buf("g_v2_%d" % i) for i in range(2)]
                yT = ph.enter_context(sbt("g_yT", [128, 8, TT], BF16))
                ByT = [Buf("g_yT%d" % i) for i in range(8)]
                psu = [ph.enter_context(pst_("g_psu%d" % i, [128, TT], F32)) for i in range(2)]
                Bpsu = [Buf("g_psu%d" % i, excl=True) for i in range(2)]
                psv = [ph.enter_context(pst_("g_psv%d" % i, [128, TT], F32)) for i in range(2)]
                Bpsv = [Buf("g_psv%d" % i, excl=True) for i in range(2)]
                pss = [ph.enter_context(pst_("g_pss%d" % i, [128, TT], F32)) for i in range(2)]
                Bpss = [Buf("g_pss%d" % i, excl=True) for i in range(2)]
                T.dma("sp", Bw, wu[:].rearrange("p a b c -> p (a b c)"), Bwbf, wslice("g%d_wu" % gj), Bw)
                T.dma("sp", Bw2[0], wv[:].rearrange("p a b -> p (a b)"), Bwbf, wslice("g%d_wv" % gj), Bw2[0])
                T.dma("sp", Bw2[1], ws[:].rearrange("p a b -> p (a b)"), Bwbf, wslice("g%d_ws" % gj), Bw2[1])
                T.dma("sp", Bw2[2], wo[:].rearrange("p a b c -> p (a b c)"), Bwbf, wslice("g%d_wo" % gj), Bw2[2])
                bso = pl.off["bsrep"][0] + gj * 8 * 128
                for j in range(J):
                    slot = j % 2
                    N.emit(slot, srcB, src, j * TT, TT, "nmix", li)
                    hT, BhT, xin, Bxin = N.hT[slot], N.BhT[slot], N.xin[slot], N.Bxin[slot]
                    for m in range(8):
                        o = m % 2
                        for kc in range(KC):
                            T.op("pe", lambda e: e.matmul(psu[o][:], wu[:, m, kc, :], hT[:, kc, :], start=(kc == 0), stop=(kc == KC - 1)),
                                 reads=[Bw, BhT], writes=[Bpsu[o]])
                        T.op("act", lambda e: e.activation(out=uT[:, m, :], in_=psu[o][:], func=AF.Gelu_apprx_tanh),
                             reads=[Bpsu[o]], writes=[BuT[m]])
                    for sub in range(4):
                        a = sub % 2
                        for hf in range(2):
                            for kc in range(KC):
                                T.op("pe", lambda e: e.matmul(psv[hf][:], hT[:, kc, sub * 128:(sub + 1) * 128], wv[:, kc, hf * 512:(hf + 1) * 512],
                                                              start=(kc == 0), stop=(kc == KC - 1)),
                                     reads=[Bw2[0], BhT], writes=[Bpsv[hf]])
                            T.op("act", lambda e: e.activation(out=vg[a][:, hf * 512:(hf + 1) * 512], in_=psv[hf][:], func=AF.Gelu_apprx_tanh),
                                 reads=[Bpsv[hf]], writes=[Bvg[a]])
                        T.op("act", lambda e: e.activation(out=vsq[:], in_=vg[a][:], func=AF.Square, accum_out=st[a][:, 0:1]),
                             reads=[Bvg[a]], writes=[Bvsq, Bst[a]])
                        T.op("act", lambda e: e.activation(out=st[a][:, 1:2], in_=st[a][:, 0:1], func=AF.Ln, scale=1.0 / 1024, bias=NORM_EPS),
                             reads=[Bst[a]], writes=[Bst[a]])
                        T.op("act", lambda e: e.activation(out=st[a][:, 2:3], in_=st[a][:, 1:2], func=AF.Exp, scale=-0.5),
                             reads=[Bst[a]], writes=[Bst[a]])
                        T.op("dve", lambda e: e.tensor_scalar(out=vn[:, sub, :], in0=vg[a][:], scalar1=st[a][:, 2:3], scalar2=None, op0=ALU.mult),
                             reads=[Bvg[a], Bst[a]], writes=[Bvn[sub]])
                    for g in range(8):
                        o = g % 2
                        for sub in range(4):
                            T.op("pe", lambda e: e.matmul(pss[o][:, sub * 128:(sub + 1) * 128], vn[:, sub, g * 128:(g + 1) * 128], ws[:, g, :],
                                                          start=True, stop=True, skip_group_check=True),
                                 reads=[Bw2[1], Bvn[sub]], writes=[Bpss[o]])
                        for sub in range(4):
                            T.op("dve", lambda e: e.scalar_tensor_tensor(out=v2[o][:, sub * 128:(sub + 1) * 128], in0=pss[o][:, sub * 128:(sub + 1) * 128],
                                                                         scalar=pcol("vgain", gj * 8 + g),
                                                                         in1=pall[:, bso + g * 128:bso + (g + 1) * 128], op0=ALU.mult, op1=ALU.add),
                                 reads=[Bpss[o], Bpall], writes=[Bv2[o]])
                        T.op("pool", lambda e: e.tensor_tensor(out=yT[:, g, :], in0=uT[:, g, :], in1=v2[o][:], op=ALU.mult),
                             reads=[BuT[g], Bv2[o]], writes=[ByT[g]])
                    for dc in range(KC):
                        o = dc % 2
                        for g in range(8):
                            T.op("pe", lambda e: e.matmul(psu[o][:], wo[:, dc, g, :], yT[:, g, :], start=(g == 0), stop=(g == 7)),
                                 reads=[Bw2[2], ByT[g]], writes=[Bpsu[o]])
                        T.op("dve", lambda e: e.tensor_tensor(out=xin[:, dc, :], in0=psu[o][:], in1=xin[:, dc, :], op=ALU.add),
                             reads=[Bpsu[o], Bxin], writes=[Bxin])
                    T.dma("pool", dstB, xT(dst, j * TT, TT), Bxin, xin[:], Bxin)
                T.barrier()

        def phase_qkv(sq, li, aj, srcB, src):
            S = sq["S"]
            J = S // TT
            with contextlib.ExitStack() as ph:
                N = NormCtx(ph)
                NWS = 3
                wqk = [ph.enter_context(sbt("a_wqk%d" % i, [128, 2, KC, 128], BF16)) for i in range(NWS)]
                Bwqk = [Buf("a_wqk%d" % i) for i in range(NWS)]
                wv = ph.enter_context(sbt("a_wv", [128, KC, 1024], BF16))
                Bwv = Buf("a_wv")
                cs = [ph.enter_context(sbt("a_cs%d" % i, [128, 2, TT], F32)) for i in range(2)]
                Bcs = [Buf("a_cs%d" % i) for i in range(2)]
                t1 = [ph.enter_context(sbt("a_t1_%d" % i, [128, 2, TT], F32)) for i in range(2)]
                Bt1 = [Buf("a_t1_%d" % i) for i in range(2)]
                qo = [ph.enter_context(sbt("a_qo%d" % i, [128, TT], BF16)) for i in range(3)]
                Bqo = [Buf("a_qo%d" % i) for i in range(3)]
                vt = [ph.enter_context(sbt("a_vt%d" % i, [128, 1024], BF16)) for i in range(2)]
                Bvt = [Buf("a_vt%d" % i) for i in range(2)]
                psz = [ph.enter_context(pst_("a_psz%d" % i, [128, 2, TT], F32)) for i in range(2)]
                Bpsz = [[Buf("a_psz%d_%d" % (i, v), excl=True) for v in range(2)] for i in range(2)]
                psv = [ph.enter_context(pst_("a_psv%d" % i, [128, TT], F32)) for i in range(2)]
                Bpsv = [Buf("a_psv%d" % i, excl=True) for i in range(2)]
                T.dma("sp", Bwv, wv[:].rearrange("p a b -> p (a b)"), Bwbf, wslice("a%d_wv" % aj), Bwv)
                c0 = 0
                c1 = 0
                N.emit(0, srcB, src, 0, TT, "nmix", li)
                for j in range(J):
                    slot = j % 2
                    t0 = j * TT
                    hT, BhT = N.hT[slot], N.BhT[slot]
                    T.dma("sp", Bcs[slot], cs[slot][:, 0, :], Bconst, cos_d[:, t0:t0 + TT], Bcs[slot])
                    T.dma("sp", Bcs[slot], cs[slot][:, 1, :], Bconst, sin_d[:, t0:t0 + TT], Bcs[slot])
                    for m in range(16):
                        ws_ = c0 % NWS
                        a = c0 % 2
                        q = c0 % 3
                        c0 += 1
                        T.dma("sp", Bwqk[ws_], wqk[ws_][:].rearrange("p a b c -> p (a b c)"), Bwbf,
                              wslice("a%d_wqk" % aj, m * 2048, 2048), Bwqk[ws_])
                        for var in range(2):
                            for kc in range(KC):
                                T.op("pe", lambda e: e.matmul(psz[a][:, var, :], wqk[ws_][:, var, kc, :], hT[:, kc, :],
                                                              start=(kc == 0), stop=(kc == KC - 1)),
                                     reads=[Bwqk[ws_], BhT], writes=[Bpsz[a][var]])
                        for var in range(2):
                            T.op("dve", lambda e: e.tensor_tensor(out=t1[a][:, var, :], in0=psz[a][:, var, :], in1=cs[slot][:, var, :], op=ALU.mult),
                                 reads=[Bpsz[a][var], Bcs[slot]], writes=[Bt1[a]])
                        T.op("pool", lambda e: e.tensor_tensor(out=qo[q][:], in0=t1[a][:, 0, :], in1=t1[a][:, 1, :], op=ALU.add),
                             reads=[Bt1[a]], writes=[Bqo[q]])
                        T.dma("pool", sq["Bqk"], sq["qk"][m, :, t0:t0 + TT], Bqo[q], qo[q][:], Bqo[q])
                    if j + 1 < J:
                        N.emit((j + 1) % 2, srcB, src, (j + 1) * TT, TT, "nmix", li)
                    for sub in range(4):
                        v_ = c1 % 2
                        c1 += 1
                        for hf in range(2):
                            for kc in range(KC):
                                T.op("pe", lambda e: e.matmul(psv[hf][:], hT[:, kc, sub * 128:(sub + 1) * 128], wv[:, kc, hf * 512:(hf + 1) * 512],
                                                              start=(kc == 0), stop=(kc == KC - 1)),
                                     reads=[Bwv, BhT], writes=[Bpsv[hf]])
                            T.op("act", lambda e: e.activation(out=vt[v_][:, hf * 512:(hf + 1) * 512], in_=psv[hf][:], func=AF.Copy),
                                 reads=[Bpsv[hf]], writes=[Bvt[v_]])
                        tk = t0 + sub * 128
                        T.dma("pool", sq["Bvs"], sq["vs"][:, tk:tk + 128, :].rearrange("h t e -> t h e"),
                              Bvt[v_], vt[v_][:].rearrange("p (h e) -> p h e", h=H), Bvt[v_])
                T.barrier(N.bufs() + Bwqk + [Bwv] + Bcs + Bqo + Bvt)

        def phase_attn(sq, li, aj):
            S = sq["S"]
            NQ = S // TT
            NK = S // 128
            scale = 64 ** -0.5
            with contextlib.ExitStack() as ph:
                kT = [ph.enter_context(sbt("b_kT%d" % i, [128, S], BF16)) for i in range(2)]
                qT = [ph.enter_context(sbt("b_qT%d" % i, [128, S], BF16)) for i in range(2)]
                vh = [ph.enter_context(sbt("b_vh%d" % i, [128, NK, 130], BF16)) for i in range(2)]
                BkT = [Buf("b_kT%d" % i) for i in range(2)]
                BqT = [Buf("b_qT%d" % i) for i in range(2)]
                Bvh = [Buf("b_vh%d" % i) for i in range(2)]
                NP_ = 3
                pT = [ph.enter_context(sbt("b_pT%d" % i, [128, 2, TT], BF16)) for i in range(NP_)]
                BpT = [Buf("b_pT%d" % i) for i in range(NP_)]
                accs = ph.enter_context(sbt("b_accs", [128, 8, 129], F32))
                Baccs = Buf("b_accs")
                sm = ph.enter_context(sbt("b_sm", [128, 32], F32))
                Bsm = Buf("b_sm")
                o1 = [ph.enter_context(sbt("b_o1_%d" % i, [128, 128], F32)) for i in range(2)]
                o2 = [ph.enter_context(sbt("b_o2_%d" % i, [128, 128], F32)) for i in range(2)]
                ob = [ph.enter_context(sbt("b_ob_%d" % i, [128, 128], BF16)) for i in range(4)]
                Bob = [Buf("b_ob%d" % i) for i in range(4)]
                Bo = [Buf("b_o%d" % i) for i in range(2)]
                o1a = ph.enter_context(sbt("b_o1a", [128, 4, 128], F32))
                o2a = ph.enter_context(sbt("b_o2a", [128, 4, 128], F32))
                Bo1a = Buf("b_o1a")
                Bo2a = Buf("b_o2a")
                sm2 = ph.enter_context(sbt("b_sm2", [128, 8], F32))
                Bsm2 = Buf("b_sm2")
                oT = [ph.enter_context(sbt("b_oT%d" % i, [128, TT], BF16)) for i in range(2)]
                BoT = [Buf("b_oT%d" % i) for i in range(2)]
                pst = [ph.enter_context(pst_("b_pst%d" % i, [128, 2, TT], F32)) for i in range(2)]
                Bpst = [Buf("b_pst%d" % i, excl=True) for i in range(2)]
                pacc = ph.enter_context(pst_("b_pacc", [128, 3, TT], F32))
                Bpacc = [Buf("b_pacc%d" % i, excl=True) for i in range(3)]
                ptr = ph.enter_context(pst_("b_ptr", [128, TT], BF16))
                Bptr = Buf("b_ptr", excl=True)
                for i in range(2):
                    T.op("pool", lambda e: e.memset(vh[i][:, :, 128:130], 1.0), writes=[Bvh[i]])
                cc = 0
                eo = 0
                pending = []
                for h in range(H):
                    s = h % 2
                    T.dma("sp", BkT[s], kT[s][:], sq["Bqk"], sq["qk"][8 + h, :, :], BkT[s])
                    T.dma("sp", BqT[s], qT[s][:], sq["Bqk"], sq["qk"][h, :, :], BqT[s])
                    T.dma("sp", Bvh[s], vh[s][:, :, 0:128], sq["Bvs"], sq["vs"][h, :, :].rearrange("(kt p) e -> p kt e", p=128), Bvh[s])
                    units = [(qt_, kt_) for qt_ in range(NQ) for kt_ in range(NK)]

                    def emit_S(i):
                        qt_, kt_ = units[i]
                        a_ = (cc + i) % 2
                        for c in range(2):
                            T.op("pe", lambda e: e.matmul(pst[a_][:, c, :], kT[s][c * 64:(c + 1) * 64, kt_ * 128:(kt_ + 1) * 128],
                                                          qT[s][c * 64:(c + 1) * 64, qt_ * TT:(qt_ + 1) * TT], start=True, stop=True),
                                 reads=[BkT[s], BqT[s]], writes=[Bpst[a_]])

                    emit_S(0)
                    for ui in range(len(units)):
                        qt, kt = units[ui]
                        a = (cc + ui) % 2
                        p = (cc + ui) % NP_
                        if ui + 1 < len(units):
                            emit_S(ui + 1)
                        T.op("act", lambda e: e.activation(out=pT[p][:].rearrange("p a b -> p (a b)"),
                                                           in_=pst[a][:].rearrange("p a b -> p (a b)"), func=AF.Exp, scale=scale),
                             reads=[Bpst[a]], writes=[BpT[p]])
                        for c in range(2):
                            for qb in range(4):
                                ai = c * 4 + qb
                                bk, col = ai // 3, (ai % 3) * 129
                                T.op("pe", lambda e: e.matmul(pacc[:, bk, col:col + 129], pT[p][:, c, qb * 128:(qb + 1) * 128], vh[s][:, kt, 0:129],
                                                              start=(kt == 0 and ai % 3 == 0), stop=(kt == NK - 1), skip_group_check=True),
                                     reads=[BpT[p], Bvh[s]], writes=[Bpacc[bk]])
                        if kt == min(8, NK - 2) and pending:
                            for f_ in pending:
                                f_()
                            pending = []
                        if kt != NK - 1:
                            continue
                        for bk in range(3):
                            na = 3 if bk < 2 else 2
                            T.op("dve", lambda e: e.tensor_copy(out=accs[:, bk * 3:bk * 3 + na, :].rearrange("p a b -> p (a b)"), in_=pacc[:, bk, 0:na * 129]),
                                 reads=[Bpacc[bk]], writes=[Baccs])
                        T.op("dve", lambda e: e.reciprocal(out=sm[:, 0:8], in_=accs[:, :, 128]), reads=[Baccs], writes=[Bsm])
                        T.op("dve", lambda e: e.tensor_scalar(out=sm[:, 8:12], in0=sm[:, 4:8], scalar1=lamt[:, aj:aj + 1], scalar2=None, op0=ALU.mult),
                             reads=[Bsm, Bmisc], writes=[Bsm])
                        ot_s = eo % 2
                        eo += 1
                        for qb in range(4):
                            T.op("dve", lambda e: e.tensor_scalar(out=o2a[:, qb, :], in0=accs[:, 4 + qb, 0:128], scalar1=sm[:, 8 + qb:9 + qb], scalar2=None, op0=ALU.mult),
                                 reads=[Baccs, Bsm], writes=[Bo2a])
                            T.op("dve", lambda e: e.scalar_tensor_tensor(out=o1a[:, qb, :], in0=accs[:, qb, 0:128], scalar=sm[:, qb:qb + 1], in1=o2a[:, qb, :],
                                                                         op0=ALU.mult, op1=ALU.add),
                                 reads=[Baccs, Bsm, Bo2a], writes=[Bo1a])
                        T.op("dve", lambda e: e.tensor_tensor(out=o2a[:], in0=o1a[:], in1=o1a[:], op=ALU.mult), reads=[Bo1a, Bo2a], writes=[Bo2a])
                        T.op("dve", lambda e: e.reduce_sum(out=sm[:, 12:16], in_=o2a[:], axis=mybir.AxisListType.X), reads=[Bo2a, Bsm], writes=[Bsm])
                        T.op("pool", lambda e: e.tensor_scalar(out=sm2[:, 0:4], in0=sm[:, 12:16], scalar1=1.0 / 128, scalar2=SUBLN_EPS, op0=ALU.mult, op1=ALU.add),
                             reads=[Bsm], writes=[Bsm2])
                        T.op("pool", lambda e: e.tensor_tensor(out=sm2[:, 4:8], in0=sm2[:, 0:4], in1=mhalf[:, 0:4], op=ALU.pow),
                             reads=[Bsm2, Bmisc], writes=[Bsm2])
                        for qb in range(4):
                            T.op("dve", lambda e: e.scalar_tensor_tensor(out=ob[qb][:], in0=o1a[:, qb, :], scalar=sm2[:, 4 + qb:5 + qb], in1=subgs[:, aj, :],
                                                                         op0=ALU.mult, op1=ALU.mult),
                                 reads=[Bo1a, Bsm2, Bmisc], writes=[Bob[qb]])

                        def part2(qt=qt, ot_s=ot_s, h=h):
                            for qb in range(4):
                                T.op("pe", lambda e: e.transpose(ptr[:, qb * 128:(qb + 1) * 128], ob[qb][:], ident[:]),
                                     reads=[Bob[qb], Bmisc], writes=[Bptr])
                            T.op("dve", lambda e: e.tensor_copy(out=oT[ot_s][:], in_=ptr[:]), reads=[Bptr], writes=[BoT[ot_s]])
                            T.dma("pool", sq["Bot"], sq["ot"][h, :, qt * TT:(qt + 1) * TT], BoT[ot_s], oT[ot_s][:], BoT[ot_s])

                        pending.append(part2)
                    for f_ in pending:
                        f_()
                    pending = []
                    cc += len(units)
                T.barrier(BkT + BqT + Bvh + BoT)

        def phase_aout(sq, li, aj, srcB, src, dstB, dst):
            S = sq["S"]
            J = S // TT
            with contextlib.ExitStack() as ph:
                wo = ph.enter_context(sbt("c_wo", [128, 8, KC, 128], BF16))
                Bwo = Buf("c_wo")
                oT = [ph.enter_context(sbt("c_oT%d" % i, [128, H, TT], BF16)) for i in range(2)]
                BoT = [Buf("c_oT%d" % i) for i in range(2)]
                xr = [ph.enter_context(sbt("c_xr%d" % i, [128, KC, TT], F32)) for i in range(2)]
                Bxr = [Buf("c_xr%d" % i) for i in range(2)]
                xn = [ph.enter_context(sbt("c_xn%d" % i, [128, KC, TT], F32)) for i in range(2)]
                Bxn = [Buf("c_xn%d" % i) for i in range(2)]
                ps = [ph.enter_context(pst_("c_ps%d" % i, [128, TT], F32)) for i in range(2)]
                Bps = [Buf("c_ps%d" % i, excl=True) for i in range(2)]
                T.dma("sp", Bwo, wo[:].rearrange("p a b c -> p (a b c)"), Bwbf, wslice("a%d_wo" % aj), Bwo)
                for j in range(J):
                    s = j % 2
                    t0 = j * TT
                    T.dma("sp", BoT[s], oT[s][:], sq["Bot"], sq["ot"][:, :, t0:t0 + TT].rearrange("h p t -> p h t"), BoT[s])
                    T.dma("sp", Bxr[s], xr[s][:], srcB, xT(src, t0, TT), Bxr[s])
                    for dc in range(KC):
                        o = dc % 2
                        for h in range(H):
                            T.op("pe", lambda e: e.matmul(ps[o][:], wo[:, dc, h, :], oT[s][:, h, :], start=(h == 0), stop=(h == H - 1)),
                                 reads=[Bwo, BoT[s]], writes=[Bps[o]])
                        T.op("dve", lambda e: e.tensor_tensor(out=xn[s][:, dc, :], in0=ps[o][:], in1=xr[s][:, dc, :], op=ALU.add),
                             reads=[Bps[o], Bxr[s]], writes=[Bxn[s]])
                    T.dma("pool", dstB, xT(dst, t0, TT), Bxn[s], xn[s][:], Bxn[s])
                T.barrier([Bwo] + BoT + Bxr + Bxn)

        def phase_final(sq, srcB, src):
            S = sq["S"]
            J = S // TT
            with contextlib.ExitStack() as ph:
                N = NormCtx(ph)
                yo = [ph.enter_context(sbt("z_yo%d" % i, [128, KC, TT], F32)) for i in range(2)]
                Byo = [Buf("z_yo%d" % i) for i in range(2)]
                for j in range(J):
                    s = j % 2
                    N.BhT[s] = Byo[s]
                    N.emit(s, srcB, src, j * TT, TT, "nfin", 0, out_f32=yo[s])
                    T.dma("pool", sq["Byout"], xT(sq["yout"], j * TT, TT), Byo[s], yo[s][:], Byo[s])
                T.barrier(N.bufs() + Byo)

        cur = {tag: ("Bxin", "xin") for tag, _ in seqs}

        def nxt(tag):
            return ("Bxb", "xb") if cur[tag][1] in ("xin", "xa") else ("Bxa", "xa")

        for li in range(DEPTH):
            for tag, _ in seqs:
                sq = SQ[tag]
                sB, s = cur[tag]
                dB, d = nxt(tag)
                if li % 2 == 0:
                    phase_gmlp(sq, li, li // 2, sq[sB], sq[s], sq[dB], sq[d])
                else:
                    phase_qkv(sq, li, li // 2, sq[sB], sq[s])
                    phase_attn(sq, li, li // 2)
                    phase_aout(sq, li, li // 2, sq[sB], sq[s], sq[dB], sq[d])
                cur[tag] = (dB, d)
            for tag, _ in seqs:
                sq = SQ[tag]
                sB, s = cur[tag]
                dB, d = nxt(tag)
                phase_ffn(sq, li, sq[sB], sq[s], sq[dB], sq[d])
                cur[tag] = (dB, d)
        for tag, _ in seqs:
            sq = SQ[tag]
            sB, s = cur[tag]
            phase_final(sq, sq[sB], sq[s])
        T.barrier([], release=False)
        build_program.stats = (dict(T.nins), T.nwait, len(T.semh))
    return nc


N_CORES = 8


def kernel(**inputs):
    xs = np.asarray(inputs["x_sample"], dtype=np.float32)
    xp = np.asarray(inputs["x_prompt"], dtype=np.float32)
    Ss, Sp = xs.shape[1], xp.shape[1]
    wall, pall, cosT, sinT = host_pack(inputs, max(Ss, Sp))
    nc = build_program([("s", Ss), ("p", Sp)])
    in_maps = []
    for c in range(N_CORES):
        in_maps.append({
            "wall": wall, "pall": pall, "cosT": cosT, "sinT": sinT,
            "x_s": np.ascontiguousarray(xs[c].T),
            "x_p": np.ascontiguousarray(xp[c // 2].T),
        })
    res = run_bass_kernel_spmd(nc, in_maps, core_ids=list(range(N_CORES)))
    y_s = np.stack([np.ascontiguousarray(res.results[c]["y_s"].T) for c in range(N_CORES)], axis=0)
    y_p = np.stack([np.ascontiguousarray(res.results[2 * i]["y_p"].T) for i in range(xp.shape[0])], axis=0)
    return (y_p.astype(np.float32), y_s.astype(np.float32))
```
